# Optimizing a Trainium2 kernel written in Bass

```python
import jax, jax.numpy as jnp
from jax import lax
import numpy as np


D_MODEL = 2048
BATCH = 2
SEQ = 16384
DEPTH = 1
DEC_BATCH = 2
DEC_SEQ = 8192
PAST_LEN = 128

ATTN_HEADS = 8
ATTN_HEAD_DIM = D_MODEL // 16
ATTN_WIDTH = ATTN_HEADS * ATTN_HEAD_DIM
DN_HEADS = 8
DN_HEAD_DIM = D_MODEL // 16
DN_WIDTH = DN_HEADS * DN_HEAD_DIM
MIX_WIDTH = ATTN_WIDTH + DN_WIDTH
DILATION_PATTERNS = ((128, 1), (512, 4), (2048, 16))
N_BUCKETS = 32
MAX_DISTANCE = 1024
CONV_K = 5
DN_CHUNK = 64
D_FF = 256 * ((8 * D_MODEL // 3 + 255) // 256)
FFN_RESIDUAL = 0.5
N_MOD = 9
IN_COLS = 3 * ATTN_WIDTH + 4 * DN_WIDTH + 4 * DN_HEADS
EPS = 1e-6
NEG_INF = -1e30

kernel_name = 'hybrid_dilated_attn_gated_deltanet_encoder'


def rms_norm(x, w):
    xf = x.astype(jnp.float32)
    y = xf * lax.rsqrt(jnp.mean(xf * xf, axis=-1, keepdims=True) + EPS)
    return (y * w.astype(jnp.float32)).astype(x.dtype)


def modulate(h, shift, scale):
    return h * (1 + scale) + shift


def swiglu(h, w_in, w_out):
    g, u = jnp.split(h @ w_in, 2, axis=-1)
    return (jax.nn.silu(g) * u) @ w_out


def l2_normalize(t):
    return t * lax.rsqrt(jnp.sum(t * t, axis=-1, keepdims=True) + EPS)


def t5_bucket(rel):
    half = N_BUCKETS // 2
    max_exact = half // 2
    n = np.abs(rel)
    large = max_exact + (np.log(np.maximum(n, 1) / max_exact) / np.log(MAX_DISTANCE / max_exact) * (half - max_exact)).astype(np.int32)
    large = np.minimum(large, half - 1)
    return (np.where(rel > 0, half, 0) + np.where(n < max_exact, n, large)).astype(np.int32)


def dilated_window_attention(q, k, v, rel_bias, window, dilation):
    b, s, h, dh = q.shape
    side = window // (2 * dilation)
    sub_len = s // dilation
    n_blk = -(-sub_len // side)
    pad_len = n_blk * side - sub_len
    bd = b * dilation

    def to_residue(t):
        return t.reshape(b, sub_len, dilation, h, dh).transpose(0, 2, 1, 3, 4).reshape(bd, sub_len, h, dh)

    def band(t):
        t = jnp.pad(t, ((0, 0), (side, pad_len + side), (0, 0), (0, 0))).reshape(bd, n_blk + 2, side, h, dh)
        return jnp.concatenate([t[:, :-2], t[:, 1:-1], t[:, 2:]], axis=2)

    qb = jnp.pad(to_residue(q), ((0, 0), (0, pad_len), (0, 0), (0, 0))).reshape(bd, n_blk, side, h, dh)
    kb = band(to_residue(k))
    vb = band(to_residue(v))

    rel = np.arange(3 * side)[None, :] - side - np.arange(side)[:, None]
    key_pos = np.arange(n_blk)[:, None] * side + np.arange(3 * side)[None, :] - side
    allowed = (np.abs(rel) <= side)[None] & ((key_pos >= 0) & (key_pos < sub_len))[:, None, :]
    bias = jnp.transpose(rel_bias[t5_bucket(rel * dilation)], (2, 0, 1)).astype(jnp.float32)

    logits = jnp.einsum('bnqhd,bnkhd->bnhqk', qb, kb, preferred_element_type=jnp.float32) * (dh ** -0.5) + bias
    logits = jnp.where(jnp.asarray(allowed)[None, :, None], logits, NEG_INF)
    m = jnp.max(logits, axis=-1, keepdims=True)
    p = jnp.exp(logits - m)
    den = jnp.sum(p, axis=-1, keepdims=True)
    o = jnp.einsum('bnhqk,bnkhd->bnqhd', p / den, vb.astype(jnp.float32))
    lse = jnp.swapaxes((m + jnp.log(den))[..., 0], 2, 3)
    o = o.reshape(bd, n_blk * side, h, dh)[:, :sub_len].reshape(b, dilation, sub_len, h, dh).transpose(0, 2, 1, 3, 4).reshape(b, s, h, dh)
    lse = lse.reshape(bd, n_blk * side, h)[:, :sub_len].reshape(b, dilation, sub_len, h).transpose(0, 2, 1, 3).reshape(b, s, h)
    return o, lse


def gated_delta_rule(q, k, v, beta, g):
    b, s, h, dk = q.shape
    dv = v.shape[-1]
    c = DN_CHUNK
    n = s // c

    def chunks(t):
        return jnp.moveaxis(t.reshape(b, n, c, h, *t.shape[3:]), 3, 1)

    q, k, v, beta, g = map(chunks, (q, k, v, beta, g))
    q = q * (dk ** -0.5)
    gc = jnp.cumsum(g, axis=-1)
    tri_incl = jnp.tril(jnp.ones((c, c), dtype=bool))
    tri_strict = jnp.tril(jnp.ones((c, c), dtype=bool), -1)
    decay = jnp.exp(jnp.where(tri_incl, gc[..., :, None] - gc[..., None, :], -jnp.inf))
    kk = jnp.einsum('bhnik,bhnjk->bhnij', k, k)
    lower = jnp.where(tri_strict, beta[..., :, None] * kk * decay, 0.0)
    a = lower + jnp.eye(c, dtype=lower.dtype)
    rhs = jnp.concatenate([v * beta[..., None], k * (beta * jnp.exp(gc))[..., None]], axis=-1)
    sol = lax.linalg.triangular_solve(a, rhs, left_side=True, lower=True)
    u, w = sol[..., :dv], sol[..., dv:]
    aqk = jnp.where(tri_incl, jnp.einsum('bhnik,bhnjk->bhnij', q, k) * decay, 0.0)
    q_dec = q * jnp.exp(gc)[..., None]
    k_dec = k * jnp.exp(gc[..., -1:] - gc)[..., None]
    g_last = jnp.exp(gc[..., -1])

    def step(state, inp):
        u_c, w_c, aqk_c, qd_c, kd_c, gl_c = inp
        v_new = u_c - jnp.einsum('bhck,bhkv->bhcv', w_c, state)
        o_c = jnp.einsum('bhck,bhkv->bhcv', qd_c, state) + jnp.einsum('bhij,bhjv->bhiv', aqk_c, v_new)
        state = state * gl_c[..., None, None] + jnp.einsum('bhck,bhcv->bhkv', kd_c, v_new)
        return state, o_c

    xs = tuple(jnp.moveaxis(t, 2, 0) for t in (u, w, aqk, q_dec, k_dec, g_last))
    _, o = lax.scan(step, jnp.zeros((b, h, dk, dv), jnp.float32), xs)
    return jnp.transpose(o, (1, 0, 3, 2, 4)).reshape(b, s, h, dv)


def centred_depthwise_conv(x, w):
    kw, ch = w.shape
    return lax.conv_general_dilated(x, w.astype(x.dtype).reshape(kw, 1, ch), window_strides=(1,), padding=[(kw // 2, kw // 2)], dimension_numbers=('NWC', 'WIO', 'NWC'), feature_group_count=ch)


def deltanet_mixer(dq, dk, dv, z, beta_raw, alpha_raw, conv_w, a_log, dt_bias, norm_dn_out):
    b, s, _ = dq.shape
    qkv = jnp.concatenate([dq, dk, dv], axis=-1).astype(jnp.float32)
    qkv = jax.nn.silu(centred_depthwise_conv(qkv, conv_w))
    q, k, v = [t.reshape(b, s, DN_HEADS, DN_HEAD_DIM) for t in jnp.split(qkv, 3, axis=-1)]
    q, k = l2_normalize(q), l2_normalize(k)
    beta = jax.nn.sigmoid(beta_raw.astype(jnp.float32)).reshape(b, s, 2, DN_HEADS)
    g = -jnp.exp(a_log.astype(jnp.float32)) * jax.nn.softplus(alpha_raw.astype(jnp.float32).reshape(b, s, 2, DN_HEADS) + dt_bias.astype(jnp.float32))
    o_fwd = gated_delta_rule(q, k, v, beta[:, :, 0], g[:, :, 0])
    flip = lambda t: jnp.flip(t, axis=1)
    o_bwd = flip(gated_delta_rule(flip(q), flip(k), flip(v), flip(beta[:, :, 1]), flip(g[:, :, 1])))
    o = o_fwd + o_bwd
    o = o * lax.rsqrt(jnp.mean(o * o, axis=-1, keepdims=True) + EPS) * norm_dn_out.astype(jnp.float32)
    o = o * jax.nn.silu(z.astype(jnp.float32).reshape(b, s, DN_HEADS, DN_HEAD_DIM))
    return o.reshape(b, s, DN_WIDTH)


def hybrid_mixer(h, w_in, conv_w, a_log, dt_bias, norm_attn_out, norm_dn_out, w_out, rel_bias):
    b, s, _ = h.shape
    proj = h @ w_in
    cuts = [int(x) for x in np.cumsum([ATTN_WIDTH] * 3 + [DN_WIDTH] * 4 + [2 * DN_HEADS])]
    aq, ak, av, dq, dk, dv, z, beta_raw, alpha_raw = jnp.split(proj, cuts, axis=-1)
    to_heads = lambda t: t.reshape(b, s, ATTN_HEADS, ATTN_HEAD_DIM)
    q, k, v = to_heads(aq), to_heads(ak), to_heads(av)
    outs, lses = [], []
    for window, dilation in DILATION_PATTERNS:
        o_p, lse_p = dilated_window_attention(q, k, v, rel_bias, window, dilation)
        outs.append(o_p)
        lses.append(lse_p)
    mix_w = jax.nn.softmax(jnp.stack(lses, axis=0), axis=0)
    o_attn = jnp.einsum('pbsh,pbshd->bshd', mix_w, jnp.stack(outs, axis=0)).reshape(b, s, ATTN_WIDTH)
    o_attn = rms_norm(o_attn, norm_attn_out)
    o_dn = deltanet_mixer(dq, dk, dv, z, beta_raw, alpha_raw, conv_w, a_log, dt_bias, norm_dn_out)
    y = jnp.concatenate([o_attn.astype(h.dtype), o_dn.astype(h.dtype)], axis=-1)
    return y @ w_out


def encoder_layer(x, c, w_mod, b_mod, norm_ffn1, w_ffn1_in, w_ffn1_out, norm_mix, w_in, conv_w, a_log, dt_bias, norm_attn_out, norm_dn_out, w_out, norm_ffn2, w_ffn2_in, w_ffn2_out, rel_bias):
    mod = jax.nn.silu(c) @ w_mod + b_mod
    sh1, sc1, g1, sh2, sc2, g2, sh3, sc3, g3 = [m[:, None, :] for m in jnp.split(mod, N_MOD, axis=-1)]
    h = modulate(rms_norm(x, norm_ffn1), sh1, sc1)
    x = x + FFN_RESIDUAL * g1 * swiglu(h, w_ffn1_in, w_ffn1_out)
    h = modulate(rms_norm(x, norm_mix), sh2, sc2)
    x = x + g2 * hybrid_mixer(h, w_in, conv_w, a_log, dt_bias, norm_attn_out, norm_dn_out, w_out, rel_bias)
    h = modulate(rms_norm(x, norm_ffn2), sh3, sc3)
    x = x + FFN_RESIDUAL * g3 * swiglu(h, w_ffn2_in, w_ffn2_out)
    return x


def encoder_trunk(x, c, w_mod, b_mod, norm_ffn1, w_ffn1_in, w_ffn1_out, norm_mix, w_in, conv_w, a_log, dt_bias, norm_attn_out, norm_dn_out, w_out, norm_ffn2, w_ffn2_in, w_ffn2_out, rel_bias, norm_final):
    for l in range(DEPTH):
        x = encoder_layer(x, c, w_mod[l], b_mod[l], norm_ffn1[l], w_ffn1_in[l], w_ffn1_out[l], norm_mix[l], w_in[l], conv_w[l], a_log[l], dt_bias[l], norm_attn_out[l], norm_dn_out[l], w_out[l], norm_ffn2[l], w_ffn2_in[l], w_ffn2_out[l], rel_bias)
    return rms_norm(x, norm_final)


def setup_inputs(seed: int = 0) -> dict:
    key = jax.random.key(seed)
    ks = jax.random.split(key, 24)
    nrm = lambda k, shape, scale: jax.random.normal(k, shape, jnp.float32) * scale
    gain = lambda k, shape: 1.0 + 0.02 * jax.random.normal(k, shape, jnp.float32)
    dt = jnp.exp(jax.random.uniform(ks[20], (DEPTH, 2, DN_HEADS), jnp.float32, np.log(1e-3), np.log(1e-1)))
    return {
        'x_prompt': nrm(ks[0], (BATCH, SEQ, D_MODEL), 1.0),
        'x_sample': nrm(ks[1], (DEC_BATCH, DEC_SEQ, D_MODEL), 1.0),
        'c_prompt': nrm(ks[2], (BATCH, D_MODEL), 1.0),
        'c_sample': nrm(ks[3], (DEC_BATCH, D_MODEL), 1.0),
        'w_mod': nrm(ks[4], (DEPTH, D_MODEL, N_MOD * D_MODEL), D_MODEL ** -0.5),
        'b_mod': nrm(ks[5], (DEPTH, N_MOD * D_MODEL), 0.01),
        'norm_ffn1': gain(ks[6], (DEPTH, D_MODEL)),
        'w_ffn1_in': nrm(ks[7], (DEPTH, D_MODEL, 2 * D_FF), D_MODEL ** -0.5),
        'w_ffn1_out': nrm(ks[8], (DEPTH, D_FF, D_MODEL), D_FF ** -0.5),
        'norm_mix': gain(ks[9], (DEPTH, D_MODEL)),
        'w_in': nrm(ks[10], (DEPTH, D_MODEL, IN_COLS), D_MODEL ** -0.5),
        'conv_w': nrm(ks[11], (DEPTH, CONV_K, 3 * DN_WIDTH), CONV_K ** -0.5),
        'a_log': jnp.log(jax.random.uniform(ks[12], (DEPTH, 2, DN_HEADS), jnp.float32, 1.0, 16.0)),
        'dt_bias': dt + jnp.log(-jnp.expm1(-dt)),
        'norm_attn_out': gain(ks[13], (DEPTH, ATTN_WIDTH)),
        'norm_dn_out': gain(ks[14], (DEPTH, DN_HEAD_DIM)),
        'w_out': nrm(ks[15], (DEPTH, MIX_WIDTH, D_MODEL), MIX_WIDTH ** -0.5),
        'norm_ffn2': gain(ks[16], (DEPTH, D_MODEL)),
        'w_ffn2_in': nrm(ks[17], (DEPTH, D_MODEL, 2 * D_FF), D_MODEL ** -0.5),
        'w_ffn2_out': nrm(ks[18], (DEPTH, D_FF, D_MODEL), D_FF ** -0.5),
        'rel_bias': nrm(ks[19], (N_BUCKETS, ATTN_HEADS), 0.5),
        'norm_final': gain(ks[21], (D_MODEL,)),
    }


def reference(x_prompt, x_sample, c_prompt, c_sample, w_mod, b_mod, norm_ffn1, w_ffn1_in, w_ffn1_out, norm_mix, w_in, conv_w, a_log, dt_bias, norm_attn_out, norm_dn_out, w_out, norm_ffn2, w_ffn2_in, w_ffn2_out, rel_bias, norm_final):
    y_prompt = encoder_trunk(x_prompt, c_prompt, w_mod, b_mod, norm_ffn1, w_ffn1_in, w_ffn1_out, norm_mix, w_in, conv_w, a_log, dt_bias, norm_attn_out, norm_dn_out, w_out, norm_ffn2, w_ffn2_in, w_ffn2_out, rel_bias, norm_final)
    y_sample = encoder_trunk(x_sample, c_sample, w_mod, b_mod, norm_ffn1, w_ffn1_in, w_ffn1_out, norm_mix, w_in, conv_w, a_log, dt_bias, norm_attn_out, norm_dn_out, w_out, norm_ffn2, w_ffn2_in, w_ffn2_out, rel_bias, norm_final)
    return (y_prompt, y_sample)
```

```python
from concourse.bass_utils import run_bass_kernel_spmd
import contextlib
import numpy as np
import concourse.bass as bass
import concourse.mybir as mybir

F32 = mybir.dt.float32
BF16 = mybir.dt.bfloat16
AF = mybir.ActivationFunctionType
ALU = mybir.AluOpType
AX = mybir.AxisListType

ENGS = ["pe", "act", "dve", "pool", "sp"]


class Buf:
    def __init__(self, name, t=None):
        self.name = name
        self.t = t
        self.writers = []
        self.readers = []
        self.dsem = None
        self.dcount = 0
        self.war = []
        self.is_psum = False

    def __getitem__(self, idx):
        return self.t[idx]


class Op:
    __slots__ = ("eng", "fn", "waits", "dma", "dsem", "dval", "needed", "mval", "seq")
    _n = [0]

    def __init__(self, eng, fn):
        Op._n[0] += 1
        self.seq = Op._n[0]
        self.eng = eng
        self.fn = fn
        self.waits = []
        self.dma = False
        self.dsem = None
        self.dval = 0
        self.needed = False
        self.mval = 0


class Prog:
    def __init__(self, nc):
        self.nc = nc
        self.stack = contextlib.ExitStack()
        self.ops = {e: [] for e in ENGS}
        self.sems = {}
        self.nbuf = 0
        self.pstack = self.stack
        self.sem_pool = []
        self.phase_bufs = []

    def sem(self, name):
        self.nbuf += 1
        name = "%s_%d" % (name, self.nbuf)
        return self.pstack.enter_context(self.nc.semaphore(name))

    def dsem_get(self, buf):
        if self.sem_pool:
            h, cnt = self.sem_pool.pop()
        else:
            h, cnt = self.sem("dq"), 0
        buf.dsem = h
        buf.dcount = cnt
        self.phase_bufs.append(buf)

    def release_phase_sems(self):
        for b in self.phase_bufs:
            self.sem_pool.append((b.dsem, b.dcount))
            b.dsem = None
        self.phase_bufs = []

    def sbuf(self, name, shape, dt=F32):
        self.nbuf += 1
        name = "%s_%d" % (name, self.nbuf)
        t = self.stack.enter_context(self.nc.sbuf_tensor(name, list(shape), dt))
        return Buf(name, t)

    def psum(self, name, shape, dt=F32):
        self.nbuf += 1
        name = "%s_%d" % (name, self.nbuf)
        t = self.stack.enter_context(self.nc.psum_tensor(name, list(shape), dt))
        b = Buf(name, t)
        b.is_psum = True
        return b

    def dram(self, name, shape, dt=F32, kind=None):
        if kind is None:
            t = self.nc.dram_tensor(name, list(shape), dt)
        else:
            t = self.nc.dram_tensor(name, list(shape), dt, kind=kind)
        return Buf(name, t)

    def _deps(self, op, reads, writes):
        for b in reads:
            for w in b.writers:
                op.waits.append(w)
            if b.is_psum:
                for r in b.readers:
                    if r.eng != op.eng:
                        op.waits.append(r)
        for b in writes:
            if b.readers:
                b.war = _compress(list(b.readers) + list(b.writers))
                op.waits.extend(b.war)
                b.readers = []
                b.writers = [op]
            else:
                op.waits.extend(b.war)
                if op.dma or any(w.dma for w in b.writers):
                    op.waits.extend(b.writers)
                b.writers.append(op)
                if len(b.writers) > 64:
                    b.writers = _compress(b.writers)
        for b in reads:
            b.readers.append(op)
            if len(b.readers) > 64:
                b.readers = _compress(b.readers)

    def op(self, eng, fn, reads=(), writes=()):
        o = Op(eng, fn)
        self._deps(o, reads, writes)
        self.ops[eng].append(o)
        return o

    def dma(self, fn, sb, reads=(), writes=(), eng="sp"):
        o = Op(eng, fn)
        o.dma = True
        if sb.dsem is None:
            self.dsem_get(sb)
        sb.dcount += 16
        o.dsem = sb.dsem
        o.dval = sb.dcount
        self._deps(o, reads, writes)
        self.ops[eng].append(o)
        return o

    def emit(self, final_waits=()):
        nc = self.nc
        esem = {e: self.sem("e_" + e) for e in ENGS}
        for e in ENGS:
            for o in self.ops[e]:
                for w in o.waits:
                    if not w.dma:
                        w.needed = True
        for e in ENGS:
            c = 0
            for o in self.ops[e]:
                if o.needed and not o.dma:
                    c += 1
                    o.mval = c
        engmap = {"pe": "tensor", "act": "scalar", "dve": "vector", "pool": "gpsimd", "sp": "sync"}
        nops = {e: len(self.ops[e]) for e in ENGS}
        nwaits = [0]
        with nc.Block() as block:
            def make(e):
                def body(eng):
                    waited = {}
                    for o in self.ops[e]:
                        req = {}
                        for w in o.waits:
                            if w.dma:
                                key, val = w.dsem, w.dval
                            else:
                                if w.eng == e and e == "pe":
                                    continue
                                if w is o:
                                    continue
                                key, val = esem[w.eng], w.mval
                            if req.get(id(key), (None, 0))[1] < val:
                                req[id(key)] = (key, val)
                        for k, (key, val) in req.items():
                            if waited.get(k, 0) < val:
                                eng.wait_ge(key, val)
                                waited[k] = val
                                nwaits[0] += 1
                        inst = o.fn(eng)
                        if o.dma:
                            inst.then_inc(o.dsem, 16)
                        elif o.needed:
                            inst.then_inc(esem[e], 1)
                    if e == "sp":
                        req = {}
                        for w in final_waits:
                            if w.dma:
                                key, val = w.dsem, w.dval
                            else:
                                key, val = esem[w.eng], w.mval
                            if req.get(id(key), (None, 0))[1] < val:
                                req[id(key)] = (key, val)
                        for k, (key, val) in req.items():
                            eng.wait_ge(key, val)
                return body
            for e in ENGS:
                if not self.ops[e] and e != "sp":
                    continue
                getattr(block, engmap[e])(make(e))
        self.stats = dict(nops=nops, nwaits=nwaits[0])
        return self.stats

    def close(self):
        self.stack.close()


def _compress(toks):
    best = {}
    for o in toks:
        key = ("d", id(o.dsem)) if o.dma else ("e", o.eng)
        cur = best.get(key)
        if cur is None:
            best[key] = o
        elif o.dma:
            if o.dval > cur.dval:
                best[key] = o
        elif o.seq > cur.seq:
            best[key] = o
    return list(best.values())


import contextlib
import numpy as np
import concourse.bass as bass
import concourse.mybir as mybir

D = 2048
KC = 16
DFF = 5632
FC = 44
NMOD = 9
INC = 7200
EPS = 1e-6
T = 512


class MK:
    def __init__(self, L, stages, dump=(), lite=False, as_input=()):
        self.lite = lite
        self.as_input = set(as_input)
        self.L = L
        self.NB = L // T
        self.stages = stages
        self.dump = set(dump)
        nc = bass.Bass("TRN2", target_bir_lowering=False)
        self.nc = nc
        self.P = Prog(nc)
        P = self.P
        self.ext_inputs = []
        def ein(n, s, dt=F32):
            self.ext_inputs.append(n)
            if lite and n.startswith("w_"):
                s = [128, 128]
            return P.dram(n, s, dt, kind="ExternalInput")
        self.x = ein("x", [L, D])
        self.cT = ein("cT", [128, KC])
        self.mask = ein("mask", [1, L])
        self.w_mod = ein("w_mod", [D, NMOD * D])
        self.b_modT = ein("b_modT", [128, NMOD * KC])
        self.normsT = ein("normsT", [128, 4 * KC])
        self.w_ffn1_in = ein("w_ffn1_in", [D, 2 * DFF])
        self.w_ffn1_out = ein("w_ffn1_out", [DFF, D])
        self.w_in = ein("w_in", [D, INC])
        self.w_out = ein("w_out", [D, D])
        self.w_ffn2_in = ein("w_ffn2_in", [D, 2 * DFF])
        self.w_ffn2_out = ein("w_ffn2_out", [DFF, D])
        self.ident_in = ein("ident", [128, 128])
        self.W1I = P.dram("W1I", [2 * FC, 128, KC * 128], BF16)
        self.W1O = P.dram("W1O", [KC, 128, FC * 128], BF16)
        self.W2I = P.dram("W2I", [2 * FC, 128, KC * 128], BF16)
        self.W2O = P.dram("W2O", [KC, 128, FC * 128], BF16)
        self.WOr = P.dram("WOr", [KC, 128, KC * 128], BF16)
        self.X1T = P.dram("X1T", [D, L], F32, kind="ExternalOutput" if "X1T" in self.dump else None)
        self.H2T = P.dram("H2T", [D, L], BF16, kind="ExternalOutput" if "H2T" in self.dump else None)
        self.outs = []
        self.ident = P.sbuf("identS", [128, 128], F32)
        self.ones = P.sbuf("onesS", [128, 128], F32)
        self.modT = P.sbuf("modT", [128, NMOD * KC], F32)
        self.nrm = P.sbuf("nrm", [128, 4 * KC], F32)
        self.AB = P.sbuf("AB", [128, 9 * KC], F32)
        self.epsc = P.sbuf("epsc", [128, 1], F32)
        self.last_tokens = []

    def barrier(self):
        P = self.P
        toks = []
        for e in ENGS:
            if P.ops[e]:
                for o in reversed(P.ops[e]):
                    if not o.dma:
                        toks.append(o)
                        break
        dl = {}
        for e in ENGS:
            for o in P.ops[e]:
                if o.dma:
                    dl[id(o.dsem)] = o
        toks += list(dl.values())
        for e in ENGS:
            o = P.op(e, lambda eng: eng.nop())
            o.waits = list(toks)
        P.release_phase_sems()
        return toks

    def setup(self):
        P = self.P
        P.dma(lambda e: e.dma_start(out=self.ident[:, :], in_=self.ident_in[:, :]), self.ident, writes=[self.ident])
        P.dma(lambda e: e.dma_start(out=self.nrm[:, :], in_=self.normsT[:, :]), self.nrm, writes=[self.nrm])
        P.op("pool", lambda e: e.memset(self.ones[:, :], 1.0), writes=[self.ones])
        P.op("pool", lambda e: e.memset(self.epsc[:, :], EPS), writes=[self.epsc])

    def convert_stationary(self, st, src, K, c0, ncols, dst, f0, tag):
        P = self.P
        kcn = K // 128
        kg = 16 if kcn == 16 else 11
        nkg = kcn // kg
        cb_n = (ncols + 511) // 512
        i = 0
        for cb in range(cb_n):
            cw = min(512, ncols - cb * 512)
            nf = cw // 128
            for g in range(nkg):
                t32 = st["t32"][i % 2]
                t16 = st["t16"][i % 2]
                srcap = src.t[g * kg * 128:(g + 1) * kg * 128, c0 + cb * 512:c0 + cb * 512 + cw].rearrange("(k p) n -> p k n", p=128)
                P.dma(lambda e, t32=t32, srcap=srcap, cw=cw, kg=kg: e.dma_start(out=t32.t[:, 0:kg, 0:cw], in_=srcap), t32, reads=[src], writes=[t32])
                eng = "dve" if i % 2 == 0 else "act"
                def cast(e, t32=t32, t16=t16, cw=cw, nf=nf, kg=kg, eng=eng):
                    o = t16.t[:, 0:nf * kg * 128].rearrange("p (f k n) -> p k f n", f=nf, k=kg)
                    i_ = t32.t[:, 0:kg, 0:cw].rearrange("p k (f n) -> p k f n", f=nf)
                    if eng == "dve":
                        return e.tensor_copy(out=o, in_=i_)
                    return e.activation(out=o, in_=i_, func=AF.Copy)
                P.op(eng, cast, reads=[t32], writes=[t16])
                dstap = dst.t[f0 + cb * 4:f0 + cb * 4 + nf, :, g * kg * 128:(g + 1) * kg * 128].rearrange("f p x -> p f x")
                P.dma(lambda e, t16=t16, dstap=dstap, nf=nf, kg=kg: e.dma_start(out=dstap, in_=t16.t[:, 0:nf * kg * 128].rearrange("p (f x) -> p f x", f=nf)), t16, reads=[t16], writes=[dst], eng="pool")
                i += 1

    def p0(self):
        P = self.P
        with contextlib.ExitStack() as es:
            old = P.stack
            P.stack = es
            st = dict(t32=[P.sbuf("cv32_%d" % i, [128, 16, 512], F32) for i in range(2)],
                      t16=[P.sbuf("cv16_%d" % i, [128, 16 * 512], BF16) for i in range(2)])
            self.convert_stationary(st, self.w_ffn1_in, D, 0, 2 * DFF, self.W1I, 0, "w1i")
            self.convert_stationary(st, self.w_ffn1_out, DFF, 0, D, self.W1O, 0, "w1o")
            if "p7" in self.stages:
                self.convert_stationary(st, self.w_ffn2_in, D, 0, 2 * DFF, self.W2I, 0, "w2i")
                self.convert_stationary(st, self.w_ffn2_out, DFF, 0, D, self.W2O, 0, "w2o")
                self.convert_stationary(st, self.w_out, D, 0, D, self.WOr, 0, "wo")
            self.barrier()
            P.stack = old

    def p1(self):
        P = self.P
        with contextlib.ExitStack() as es:
            old = P.stack
            P.stack = es
            cs = P.sbuf("cS", [128, KC], F32)
            sc = P.sbuf("scS", [128, KC], F32)
            bm = P.sbuf("bmS", [128, NMOD * KC], F32)
            wm = [P.sbuf("wm%d" % i, [128, KC, 512], F32) for i in range(2)]
            ps = P.psum("ps_mod", [128, 512], F32)
            P.dma(lambda e: e.dma_start(out=cs[:, :], in_=self.cT[:, :]), cs, writes=[cs])
            P.dma(lambda e: e.dma_start(out=bm[:, :], in_=self.b_modT[:, :]), bm, writes=[bm])
            P.op("act", lambda e: e.activation(out=sc[:, :], in_=cs[:, :], func=AF.Silu), reads=[cs], writes=[sc])
            ng = NMOD * D // 512
            for g in range(ng):
                w = wm[g % 2]
                srcap = self.w_mod.t[:, g * 512:(g + 1) * 512].rearrange("(k p) n -> p k n", p=128)
                P.dma(lambda e, w=w, srcap=srcap: e.dma_start(out=w.t[:, :, :], in_=srcap), w, writes=[w])
                for jj in range(4):
                    j = 4 * g + jj
                    for kc in range(KC):
                        P.op("pe", lambda e, w=w, jj=jj, kc=kc, j=j: e.matmul(ps.t[:, j:j + 1], lhsT=w.t[:, kc, jj * 128:(jj + 1) * 128], rhs=sc.t[:, kc:kc + 1], start=(kc == 0), stop=(kc == KC - 1)),
                             reads=[w, sc], writes=[ps])
            P.op("dve", lambda e: e.tensor_tensor(out=self.modT[:, :], in0=ps.t[:, 0:NMOD * KC], in1=bm[:, :], op=ALU.add), reads=[ps, bm], writes=[self.modT])
            for i in range(3):
                sh = self.modT.t[:, (3 * i) * KC:(3 * i + 1) * KC]
                scl = self.modT.t[:, (3 * i + 1) * KC:(3 * i + 2) * KC]
                gt = self.modT.t[:, (3 * i + 2) * KC:(3 * i + 3) * KC]
                nr = self.nrm.t[:, i * KC:(i + 1) * KC]
                A = self.AB.t[:, (3 * i) * KC:(3 * i + 1) * KC]
                Bv = self.AB.t[:, (3 * i + 1) * KC:(3 * i + 2) * KC]
                G = self.AB.t[:, (3 * i + 2) * KC:(3 * i + 3) * KC]
                P.op("dve", lambda e, A=A, scl=scl, nr=nr: e.scalar_tensor_tensor(out=A, in0=scl, scalar=1.0, in1=nr, op0=ALU.add, op1=ALU.mult), reads=[self.modT, self.nrm], writes=[self.AB])
                P.op("dve", lambda e, Bv=Bv, sh=sh: e.tensor_copy(out=Bv, in_=sh), reads=[self.modT], writes=[self.AB])
                gs = 1.0 if i == 1 else 0.5
                P.op("dve", lambda e, G=G, gt=gt, gs=gs: e.tensor_scalar(out=G, in0=gt, scalar1=gs, scalar2=None, op0=ALU.mult), reads=[self.modT], writes=[self.AB])
            self.barrier()
            P.stack = old

    def alloc_ffn(self):
        P = self.P
        s = {}
        s["xtok"] = [P.sbuf("xtok%d" % i, [128, D], F32) for i in range(2)]
        s["xT"] = P.sbuf("xT", [128, KC, T], F32)
        s["hT"] = P.sbuf("hT", [128, KC, T], BF16)
        s["aT"] = P.sbuf("aT", [128, FC, T], BF16)
        s["sq"] = [P.sbuf("sq%d" % i, [128, T], F32) for i in range(2)]
        s["tmp"] = [P.sbuf("tmp%d" % i, [128, T], F32) for i in range(2)]
        s["rstd"] = P.sbuf("rstd", [128, T], F32)
        s["sg"] = [P.sbuf("sg%d" % i, [128, T], F32) for i in range(2)]
        s["wi"] = [P.sbuf("wi%d" % i, [128, KC * 128], BF16) for i in range(6)]
        s["wo"] = [P.sbuf("wo%d" % i, [128, FC * 128], BF16) for i in range(2)]
        s["mk"] = P.sbuf("mk", [128, T], F32)
        s["pt"] = [P.psum("pt%d" % i, [128, T], F32) for i in range(2)]
        s["pg"] = [P.psum("pg%d" % i, [128, T], F32) for i in range(2)]
        s["pu"] = [P.psum("pu%d" % i, [128, T], F32) for i in range(2)]
        s["pss"] = P.psum("pss", [128, T], F32)
        s["wi_n"] = 0
        s["wo_n"] = 0
        s["pn"] = 0
        return s

    def load_xT_from_tokens(self, s, xd, b):
        P = self.P
        xT = s["xT"]
        for j in range(T // 128):
            xt = s["xtok"][j % 2]
            r0 = b * T + j * 128
            P.dma(lambda e, xt=xt, r0=r0: e.dma_start(out=xt.t[:, :], in_=xd.t[r0:r0 + 128, :]), xt, reads=[xd], writes=[xt])
            for cg in range(KC // 4):
                pt = s["pt"][s["pn"] % 2]
                s["pn"] += 1
                for ci in range(4):
                    c = cg * 4 + ci
                    P.op("pe", lambda e, pt=pt, xt=xt, c=c, ci=ci: e.transpose(out=pt.t[:, ci * 128:(ci + 1) * 128], in_=xt.t[:, c * 128:(c + 1) * 128], identity=self.ident.t[:, :]),
                         reads=[xt, self.ident], writes=[pt])
                eng = "dve" if (cg % 2 == 0) else "act"
                def ev(e, pt=pt, cg=cg, j=j, eng=eng):
                    o = xT.t[:, cg * 4:(cg + 1) * 4, j * 128:(j + 1) * 128]
                    i_ = pt.t[:, :].rearrange("p (c n) -> p c n", c=4)
                    if eng == "dve":
                        return e.tensor_copy(out=o, in_=i_)
                    return e.activation(out=o, in_=i_, func=AF.Copy)
                P.op(eng, ev, reads=[pt], writes=[xT])

    def rms_stats(self, s):
        P = self.P
        xT = s["xT"]
        pss = s["pss"]
        for c in range(KC):
            sq = s["sq"][c % 2]
            P.op("act", lambda e, sq=sq, c=c: e.activation(out=sq.t[:, :], in_=xT.t[:, c, :], func=AF.Square), reads=[xT], writes=[sq])
            P.op("pe", lambda e, sq=sq, c=c: e.matmul(pss.t[:, :], lhsT=self.ones.t[:, :], rhs=sq.t[:, :], start=(c == 0), stop=(c == KC - 1)), reads=[sq, self.ones], writes=[pss])
        rstd = s["rstd"]
        P.op("act", lambda e: e.activation(out=rstd.t[:, :], in_=pss.t[:, :], func=AF.Sqrt, bias=self.epsc.t[:, 0:1], scale=1.0 / D), reads=[pss, self.epsc], writes=[rstd])
        P.op("dve", lambda e: e.reciprocal(out=rstd.t[:, :], in_=rstd.t[:, :]), reads=[rstd], writes=[rstd])

    def norm_affine(self, s, i, mask_b=None):
        P = self.P
        xT, hT, rstd = s["xT"], s["hT"], s["rstd"]
        for c in range(KC):
            tmp = s["tmp"][c % 2]
            Ac = self.AB.t[:, 3 * i * KC + c:3 * i * KC + c + 1]
            Bc = self.AB.t[:, (3 * i + 1) * KC + c:(3 * i + 1) * KC + c + 1]
            P.op("dve", lambda e, tmp=tmp, c=c, Ac=Ac: e.scalar_tensor_tensor(out=tmp.t[:, :], in0=xT.t[:, c, :], scalar=Ac, in1=rstd.t[:, :], op0=ALU.mult, op1=ALU.mult),
                 reads=[xT, rstd, self.AB], writes=[tmp])
            if mask_b is None:
                P.op("act", lambda e, tmp=tmp, c=c, Bc=Bc: e.activation(out=hT.t[:, c, :], in_=tmp.t[:, :], func=AF.Identity, bias=Bc, scale=1.0), reads=[tmp, self.AB], writes=[hT])
            else:
                P.op("dve", lambda e, tmp=tmp, c=c, Bc=Bc: e.scalar_tensor_tensor(out=hT.t[:, c, :], in0=tmp.t[:, :], scalar=Bc, in1=mask_b.t[:, :], op0=ALU.add, op1=ALU.mult),
                     reads=[tmp, self.AB, mask_b], writes=[hT])

    def ffn(self, s, WI, WO, gi):
        P = self.P
        xT, hT, aT = s["xT"], s["hT"], s["aT"]
        for f in range(FC):
            wg = s["wi"][s["wi_n"] % 6]
            wu = s["wi"][(s["wi_n"] + 1) % 6]
            s["wi_n"] += 2
            P.dma(lambda e, wg=wg, f=f: e.dma_start(out=wg.t[:, :], in_=WI.t[f, :, :]), wg, reads=[WI], writes=[wg])
            P.dma(lambda e, wu=wu, f=f: e.dma_start(out=wu.t[:, :], in_=WI.t[FC + f, :, :]), wu, reads=[WI], writes=[wu])
            pg = s["pg"][f % 2]
            pu = s["pu"][f % 2]
            for kc in range(KC):
                P.op("pe", lambda e, pg=pg, wg=wg, kc=kc: e.matmul(pg.t[:, :], lhsT=wg.t[:, kc * 128:(kc + 1) * 128], rhs=hT.t[:, kc, :], start=(kc == 0), stop=(kc == KC - 1)), reads=[wg, hT], writes=[pg])
            for kc in range(KC):
                P.op("pe", lambda e, pu=pu, wu=wu, kc=kc: e.matmul(pu.t[:, :], lhsT=wu.t[:, kc * 128:(kc + 1) * 128], rhs=hT.t[:, kc, :], start=(kc == 0), stop=(kc == KC - 1)), reads=[wu, hT], writes=[pu])
            sg = s["sg"][f % 2]
            P.op("act", lambda e, sg=sg, pg=pg: e.activation(out=sg.t[:, :], in_=pg.t[:, :], func=AF.Silu), reads=[pg], writes=[sg])
            P.op("dve", lambda e, sg=sg, pu=pu, f=f: e.tensor_tensor(out=aT.t[:, f, :], in0=sg.t[:, :], in1=pu.t[:, :], op=ALU.mult), reads=[sg, pu], writes=[aT])
        for dc in range(KC):
            wo = s["wo"][s["wo_n"] % 2]
            s["wo_n"] += 1
            P.dma(lambda e, wo=wo, dc=dc: e.dma_start(out=wo.t[:, :], in_=WO.t[dc, :, :]), wo, reads=[WO], writes=[wo])
            py = s["pt"][s["pn"] % 2]
            s["pn"] += 1
            for f in range(FC):
                P.op("pe", lambda e, py=py, wo=wo, f=f: e.matmul(py.t[:, :], lhsT=wo.t[:, f * 128:(f + 1) * 128], rhs=aT.t[:, f, :], start=(f == 0), stop=(f == FC - 1)), reads=[wo, aT], writes=[py])
            Gc = self.AB.t[:, (3 * gi + 2) * KC + dc:(3 * gi + 2) * KC + dc + 1]
            P.op("dve", lambda e, py=py, dc=dc, Gc=Gc: e.scalar_tensor_tensor(out=xT.t[:, dc, :], in0=py.t[:, :], scalar=Gc, in1=xT.t[:, dc, :], op0=ALU.mult, op1=ALU.add),
                 reads=[py, xT, self.AB], writes=[xT])

    def p2(self):
        P = self.P
        with contextlib.ExitStack() as es:
            old = P.stack
            P.stack = es
            s = self.alloc_ffn()
            for b in range(self.NB):
                self.load_xT_from_tokens(s, self.x, b)
                self.rms_stats(s)
                self.norm_affine(s, 0)
                self.ffn(s, self.W1I, self.W1O, 0)
                xT = s["xT"]
                dst = self.X1T.t[:, b * T:(b + 1) * T].rearrange("(c p) t -> p c t", p=128)
                P.dma(lambda e, dst=dst: e.dma_start(out=dst, in_=xT.t[:, :, :]), xT, reads=[xT], writes=[self.X1T], eng="pool")
                mk = s["mk"]
                P.dma(lambda e, b=b: e.dma_start(out=mk.t[:, :], in_=bcast_rows(self.mask.t, b * T, T)), mk, reads=[self.mask], writes=[mk])
                self.rms_stats(s)
                self.norm_affine(s, 1, mask_b=mk)
                hT = s["hT"]
                dsth = self.H2T.t[:, b * T:(b + 1) * T].rearrange("(c p) t -> p c t", p=128)
                P.dma(lambda e, dsth=dsth: e.dma_start(out=dsth, in_=hT.t[:, :, :]), hT, reads=[hT], writes=[self.H2T], eng="pool")
            self.last_tokens = self.barrier()
            P.stack = old

    def build(self):
        self.setup()
        if "p0" in self.stages:
            self.p0()
        if "p1" in self.stages:
            self.p1()
        if "p2" in self.stages:
            self.p2()
        toks = self.barrier()
        st = self.P.emit(final_waits=toks)
        print("ops", st)
        return self.nc


def bcast_rows(t, c0, n):
    ap = t[0:1, c0:c0 + n]
    return bass.AP(ap.tensor, ap.offset, [[0, 128], [1, n]])


AW = 1024
NH = 8
DH = 128


def t5_bucket_np(rel):
    half = 16
    max_exact = 8
    n = np.abs(rel)
    large = max_exact + (np.log(np.maximum(n, 1) / max_exact) / np.log(1024 / max_exact) * (half - max_exact)).astype(np.int32)
    large = np.minimum(large, half - 1)
    return (np.where(rel > 0, half, 0) + np.where(n < max_exact, n, large)).astype(np.int32)


DILS = (1, 4, 16)


def make_onehot():
    oh = np.zeros((3, 33, 384), np.float32)
    for p, dil in enumerate(DILS):
        for m in range(383):
            rel = m - 191
            if abs(rel) <= 64:
                oh[p, int(t5_bucket_np(np.array(rel * dil))), m] = 1.0
            else:
                oh[p, 32, m] = -1e30
        oh[p, 32, 383] = -1e30
    return oh


class MK2(MK):
    def __init__(self, L, stages, dump=(), lite=False, as_input=()):
        super().__init__(L, stages, dump, lite, as_input)
        P = self.P
        def ein(n, s, dt=F32):
            self.ext_inputs.append(n)
            return P.dram(n, s, dt, kind="ExternalInput")
        dmp = lambda n: ("ExternalOutput" if n in self.dump else None)
        self.rel_bias = ein("rel_bias", [32, NH])
        self.onehot = ein("onehot", [3, 33, 384])
        self.nattn_in = ein("norm_attn_out", [1, AW])
        self.WA = P.dram("WA", [8, 128, KC * 512], BF16)
        self.WD = P.dram("WD", [24, 128, KC * 128], BF16)
        self.QKV = P.dram("QKV", [L, 3 * AW], BF16, kind=dmp("QKV"))
        self.Z = P.dram("Z", [L, AW], F32, kind=dmp("Z"))
        self.DQKVT = P.dram("DQKVT", [3 * AW, L], F32, kind=dmp("DQKVT"))
        self.BAT = P.dram("BAT", [32, L], F32, kind=dmp("BAT"))
        self.AO = P.dram("AO", [3, L, AW], F32, kind=dmp("AO"))
        self.AM = P.dram("AM", [3, L, 16], F32, kind=dmp("AM"))
        self.YT = P.dram("YT", [D, L], BF16, kind=dmp("YT"))
        self.NEGM = P.dram("NEGM", [1, L], F32)
        self.BIASR = P.dram("BIASR", [3, NH, 384], F32)
        self.BIAS2 = P.dram("BIAS2", [3, NH, 128 * 385], F32)
        self.identb = P.sbuf("identb", [128, 128], BF16)
        self.WB = P.sbuf("WB", [128, KC * 32], BF16)

    def setup(self):
        super().setup()
        P = self.P
        P.op("dve", lambda e: e.tensor_copy(out=self.identb[:, :], in_=self.ident[:, :]), reads=[self.ident], writes=[self.identb])

    def convert_moving(self, st, src, c0, ngroups, dst, g0):
        P = self.P
        for g in range(ngroups):
            t32 = st["t32"][g % 2]
            t16 = st["t16"][g % 2]
            srcap = src.t[:, c0 + g * 512:c0 + (g + 1) * 512].rearrange("(k p) n -> p k n", p=128)
            P.dma(lambda e, t32=t32, srcap=srcap: e.dma_start(out=t32.t[:, :, :], in_=srcap), t32, reads=[src], writes=[t32])
            eng = "dve" if g % 2 == 0 else "act"
            def cast(e, t32=t32, t16=t16, eng=eng):
                o = t16.t[:, :]
                i_ = t32.t[:, :, :].rearrange("p k n -> p (k n)")
                if eng == "dve":
                    return e.tensor_copy(out=o, in_=i_)
                return e.activation(out=o, in_=i_, func=AF.Copy)
            P.op(eng, cast, reads=[t32], writes=[t16])
            P.dma(lambda e, t16=t16, g=g: e.dma_start(out=dst.t[g0 + g, :, :], in_=t16.t[:, :]), t16, reads=[t16], writes=[dst], eng="pool")

    def p0(self):
        P = self.P
        super().p0()
        if "p3" not in self.stages:
            return
        with contextlib.ExitStack() as es:
            old = P.stack
            P.stack = es
            st = dict(t32=[P.sbuf("cw32_%d" % i, [128, 16, 512], F32) for i in range(2)],
                      t16=[P.sbuf("cw16_%d" % i, [128, 16 * 512], BF16) for i in range(2)])
            self.convert_moving(st, self.w_in, 0, 6, self.WA, 0)
            self.convert_moving(st, self.w_in, 6 * AW, 2, self.WA, 6)
            self.convert_stationary(st, self.w_in, D, 3 * AW, 3 * AW, self.WD, 0, "wd")
            t32 = st["t32"][0]
            srcap = self.w_in.t[:, 7 * AW:7 * AW + 32].rearrange("(k p) n -> p k n", p=128)
            P.dma(lambda e: e.dma_start(out=t32.t[:, :, 0:32], in_=srcap), t32, reads=[self.w_in, t32], writes=[t32])
            P.op("dve", lambda e: e.tensor_copy(out=self.WB.t[:, :].rearrange("p (k n) -> p k n", k=KC), in_=t32.t[:, :, 0:32]), reads=[t32], writes=[self.WB])
            self.barrier()
            P.stack = old

    def p3(self):
        P = self.P
        L = self.L
        with contextlib.ExitStack() as es:
            old = P.stack
            P.stack = es
            hT = [P.sbuf("h2b%d" % i, [128, KC, T], BF16) for i in range(2)]
            wa = [P.sbuf("wa%d" % i, [128, KC * 512], BF16) for i in range(2)]
            wd = [P.sbuf("wd%d" % i, [128, KC * 128], BF16) for i in range(3)]
            tok16 = [P.sbuf("tok16_%d" % i, [128, 512], BF16) for i in range(3)]
            tok32 = [P.sbuf("tok32_%d" % i, [128, 512], F32) for i in range(3)]
            ps = [P.psum("p3ps%d" % i, [128, 512], F32) for i in range(4)]
            n_ps = 0
            n16 = 0
            n32 = 0
            nwa = 0
            nwd = 0
            for b in range(self.NB):
                h = hT[b % 2]
                src = self.H2T.t[:, b * T:(b + 1) * T].rearrange("(c p) t -> p c t", p=128)
                P.dma(lambda e, h=h, src=src: e.dma_start(out=h.t[:, :, :], in_=src), h, reads=[self.H2T], writes=[h])
                for g in range(8):
                    w = wa[nwa % 2]
                    nwa += 1
                    P.dma(lambda e, w=w, g=g: e.dma_start(out=w.t[:, :], in_=self.WA.t[g, :, :]), w, reads=[self.WA], writes=[w])
                    for j in range(4):
                        pp = ps[n_ps % 4]
                        n_ps += 1
                        for kc in range(KC):
                            P.op("pe", lambda e, pp=pp, h=h, w=w, kc=kc, j=j: e.matmul(pp.t[:, :], lhsT=h.t[:, kc, j * 128:(j + 1) * 128], rhs=w.t[:, kc * 512:(kc + 1) * 512], start=(kc == 0), stop=(kc == KC - 1)),
                                 reads=[h, w], writes=[pp])
                        r0 = b * T + j * 128
                        if g < 6:
                            o = tok16[n16 % 3]
                            n16 += 1
                            eng = "act" if n16 % 2 else "dve"
                            if eng == "act":
                                P.op("act", lambda e, o=o, pp=pp: e.activation(out=o.t[:, :], in_=pp.t[:, :], func=AF.Copy), reads=[pp], writes=[o])
                            else:
                                P.op("dve", lambda e, o=o, pp=pp: e.tensor_copy(out=o.t[:, :], in_=pp.t[:, :]), reads=[pp], writes=[o])
                            P.dma(lambda e, o=o, r0=r0, g=g: e.dma_start(out=self.QKV.t[r0:r0 + 128, g * 512:(g + 1) * 512], in_=o.t[:, :]), o, reads=[o], writes=[self.QKV], eng="pool")
                        else:
                            o = tok32[n32 % 3]
                            n32 += 1
                            P.op("act", lambda e, o=o, pp=pp: e.activation(out=o.t[:, :], in_=pp.t[:, :], func=AF.Copy), reads=[pp], writes=[o])
                            P.dma(lambda e, o=o, r0=r0, g=g: e.dma_start(out=self.Z.t[r0:r0 + 128, (g - 6) * 512:(g - 5) * 512], in_=o.t[:, :]), o, reads=[o], writes=[self.Z], eng="pool")
                for f in range(24):
                    w = wd[nwd % 3]
                    nwd += 1
                    P.dma(lambda e, w=w, f=f: e.dma_start(out=w.t[:, :], in_=self.WD.t[f, :, :]), w, reads=[self.WD], writes=[w])
                    pp = ps[n_ps % 4]
                    n_ps += 1
                    for kc in range(KC):
                        P.op("pe", lambda e, pp=pp, h=h, w=w, kc=kc: e.matmul(pp.t[:, :], lhsT=w.t[:, kc * 128:(kc + 1) * 128], rhs=h.t[:, kc, :], start=(kc == 0), stop=(kc == KC - 1)),
                             reads=[h, w], writes=[pp])
                    o = tok32[n32 % 3]
                    n32 += 1
                    P.op("dve", lambda e, o=o, pp=pp: e.tensor_copy(out=o.t[:, :], in_=pp.t[:, :]), reads=[pp], writes=[o])
                    P.dma(lambda e, o=o, f=f, b=b: e.dma_start(out=self.DQKVT.t[f * 128:(f + 1) * 128, b * T:(b + 1) * T], in_=o.t[:, :]), o, reads=[o], writes=[self.DQKVT], eng="pool")
                pp = ps[n_ps % 4]
                n_ps += 1
                for kc in range(KC):
                    P.op("pe", lambda e, pp=pp, h=h, kc=kc: e.matmul(pp.t[0:32, :], lhsT=self.WB.t[:, kc * 32:(kc + 1) * 32], rhs=h.t[:, kc, :], start=(kc == 0), stop=(kc == KC - 1)),
                         reads=[h, self.WB], writes=[pp])
                o = tok32[n32 % 3]
                n32 += 1
                P.op("dve", lambda e, o=o, pp=pp: e.tensor_copy(out=o.t[0:32, :], in_=pp.t[0:32, :]), reads=[pp], writes=[o])
                P.dma(lambda e, o=o, b=b: e.dma_start(out=self.BAT.t[:, b * T:(b + 1) * T], in_=o.t[0:32, :]), o, reads=[o], writes=[self.BAT], eng="pool")
            self.barrier()
            P.stack = old

    def p4(self):
        P = self.P
        L = self.L
        SCALE = DH ** -0.5
        with contextlib.ExitStack() as es:
            old = P.stack
            P.stack = es
            rba = P.sbuf("rba", [33, NH], F32)
            ohs = P.sbuf("ohs", [33, 384], F32)
            brow = P.sbuf("brow", [NH, 384], F32)
            biasT = [P.sbuf("biasT%d" % p, [128, NH, 256], F32) for p in range(3)]
            psb = P.psum("psb", [128, 512], F32)
            P.op("pool", lambda e: e.memset(rba.t[:, :], 1.0), writes=[rba])
            P.dma(lambda e: e.dma_start(out=rba.t[0:32, :], in_=self.rel_bias.t[:, :]), rba, reads=[rba], writes=[rba])
            for p in range(3):
                P.dma(lambda e, p=p: e.dma_start(out=ohs.t[:, :], in_=self.onehot.t[p, :, :]), ohs, writes=[ohs])
                P.op("pe", lambda e: e.matmul(psb.t[0:NH, 0:384], lhsT=rba.t[:, :], rhs=ohs.t[:, :], start=True, stop=True), reads=[rba, ohs], writes=[psb])
                P.op("dve", lambda e: e.tensor_copy(out=brow.t[:, :], in_=psb.t[0:NH, 0:384]), reads=[psb], writes=[brow])
                P.dma(lambda e, p=p: e.dma_start(out=self.BIASR.t[p, :, :], in_=brow.t[:, :]), brow, reads=[brow], writes=[self.BIASR], eng="pool")
                for h in range(NH):
                    base = self.BIASR.t[p, h:h + 1, 0:1]
                    src0 = bass.AP(base.tensor, base.offset, [[0, 128], [1, 384]])
                    b2 = self.BIAS2.t[p, h:h + 1, 0:1]
                    dst0 = bass.AP(b2.tensor, b2.offset, [[385, 128], [1, 384]])
                    P.dma(lambda e, src0=src0, dst0=dst0: e.dma_start(out=dst0, in_=src0), self.BIAS2, reads=[self.BIASR], writes=[self.BIAS2])
                    src = bass.AP(b2.tensor, b2.offset + 127, [[384, 128], [1, 256]])
                    P.dma(lambda e, p=p, h=h, src=src: e.dma_start(out=biasT[p].t[:, h, :], in_=src), biasT[p], reads=[self.BIAS2], writes=[biasT[p]])
            MRW = min(L, 2048)
            mrow = P.sbuf("mrow", [1, MRW], F32)
            for mi in range(L // MRW):
                P.dma(lambda e, mi=mi: e.dma_start(out=mrow.t[:, :], in_=self.mask.t[:, mi * MRW:(mi + 1) * MRW]), mrow, reads=[self.mask], writes=[mrow])
                P.op("dve", lambda e: e.tensor_scalar(out=mrow.t[:, :], in0=mrow.t[:, :], scalar1=-1.0, scalar2=1e30, op0=ALU.add, op1=ALU.mult), reads=[mrow], writes=[mrow])
                P.dma(lambda e, mi=mi: e.dma_start(out=self.NEGM.t[:, mi * MRW:(mi + 1) * MRW], in_=mrow.t[:, :]), mrow, reads=[mrow], writes=[self.NEGM], eng="pool")
            ones1 = P.sbuf("ones1", [1, 128], F32)
            P.op("pool", lambda e: e.memset(ones1.t[:, :], 1.0), writes=[ones1])
            NBUF = 3
            qt = [P.sbuf("qt%d" % i, [128, AW], BF16) for i in range(NBUF)]
            kt = [[P.sbuf("kt%d_%d" % (i, a), [128, AW], BF16) for a in range(2)] for i in range(NBUF)]
            vt = [[P.sbuf("vt%d_%d" % (i, a), [128, AW], BF16) for a in range(2)] for i in range(NBUF)]
            kb = [P.sbuf("kb%d" % i, [1, 256], F32) for i in range(NBUF)]
            qT_l = [P.sbuf("qTs%d" % i, [128, 4, 128], BF16) for i in range(2)]
            kT_l = [P.sbuf("kTs%d" % i, [128, 4, 256], BF16) for i in range(2)]
            ssb_l = [P.sbuf("ssb%d" % i, [128, 4, 256], F32) for i in range(2)]
            psb16_l = [P.sbuf("p16_%d" % i, [128, 4, 256], BF16) for i in range(2)]
            pT_l = [P.sbuf("pTs%d" % i, [128, 4, 2, 128], BF16) for i in range(2)]
            nmx_l = [P.sbuf("nmx%d" % i, [128, 4], F32) for i in range(2)]
            osb = [P.sbuf("osb%d" % i, [128, AW], F32) for i in range(2)]
            stt = [P.sbuf("stt%d" % i, [128, 16], F32) for i in range(2)]
            p_q = P.psum("p_q", [128, 4 * 128], BF16)
            p_k = P.psum("p_k", [128, 4 * 256], BF16)
            p_s = P.psum("p_s", [128, 4 * 256], F32)
            p_p = P.psum("p_p", [128, 8 * 128], BF16)
            p_o = P.psum("p_o", [128, 4 * 128], F32)
            it = 0
            for p, dil in enumerate(DILS):
                sub = L // dil
                ntile = sub // 128
                for r in range(dil):
                    for m in range(ntile):
                        bi = it % NBUF
                        it += 1
                        Q, KA, KB_, VA, VB, kbr = qt[bi], kt[bi][0], kt[bi][1], vt[bi][0], vt[bi][1], kb[bi]
                        os_, st_ = osb[it % 2], stt[it % 2]
                        def rows(j0, n):
                            return slice(r + dil * j0, r + dil * (j0 + n - 1) + 1, dil)
                        P.dma(lambda e, Q=Q, sl=rows(128 * m, 128): e.dma_start(out=Q.t[:, :], in_=self.QKV.t[sl, 0:AW]), Q, reads=[self.QKV], writes=[Q])
                        first = (m == 0)
                        last = (m == ntile - 1)
                        if first or last:
                            P.op("pool", lambda e, kbr=kbr: e.memset(kbr.t[:, :], -1e30), reads=[kbr], writes=[kbr])
                        if first:
                            P.op("pool", lambda e, KA=KA: e.memset(KA.t[0:64, :], 0.0), reads=[KA], writes=[KA])
                            P.op("pool", lambda e, VA=VA: e.memset(VA.t[0:64, :], 0.0), reads=[VA], writes=[VA])
                            P.dma(lambda e, KA=KA, sl=rows(0, 64): e.dma_start(out=KA.t[64:128, :], in_=self.QKV.t[sl, AW:2 * AW]), KA, reads=[self.QKV, KA], writes=[KA])
                            P.dma(lambda e, VA=VA, sl=rows(0, 64): e.dma_start(out=VA.t[64:128, :], in_=self.QKV.t[sl, 2 * AW:3 * AW]), VA, reads=[self.QKV, VA], writes=[VA])
                        else:
                            P.dma(lambda e, KA=KA, sl=rows(128 * m - 64, 128): e.dma_start(out=KA.t[:, :], in_=self.QKV.t[sl, AW:2 * AW]), KA, reads=[self.QKV], writes=[KA])
                            P.dma(lambda e, VA=VA, sl=rows(128 * m - 64, 128): e.dma_start(out=VA.t[:, :], in_=self.QKV.t[sl, 2 * AW:3 * AW]), VA, reads=[self.QKV], writes=[VA])
                        if last:
                            P.op("pool", lambda e, KB_=KB_: e.memset(KB_.t[64:128, :], 0.0), reads=[KB_], writes=[KB_])
                            P.op("pool", lambda e, VB=VB: e.memset(VB.t[64:128, :], 0.0), reads=[VB], writes=[VB])
                            P.dma(lambda e, KB_=KB_, sl=rows(128 * m + 64, 64): e.dma_start(out=KB_.t[0:64, :], in_=self.QKV.t[sl, AW:2 * AW]), KB_, reads=[self.QKV, KB_], writes=[KB_])
                            P.dma(lambda e, VB=VB, sl=rows(128 * m + 64, 64): e.dma_start(out=VB.t[0:64, :], in_=self.QKV.t[sl, 2 * AW:3 * AW]), VB, reads=[self.QKV, VB], writes=[VB])
                        else:
                            P.dma(lambda e, KB_=KB_, sl=rows(128 * m + 64, 128): e.dma_start(out=KB_.t[:, :], in_=self.QKV.t[sl, AW:2 * AW]), KB_, reads=[self.QKV], writes=[KB_])
                            P.dma(lambda e, VB=VB, sl=rows(128 * m + 64, 128): e.dma_start(out=VB.t[:, :], in_=self.QKV.t[sl, 2 * AW:3 * AW]), VB, reads=[self.QKV], writes=[VB])
                        j_lo = max(128 * m - 64, 0)
                        j_hi = min(128 * m + 192, sub)
                        k_lo = j_lo - (128 * m - 64)
                        nk = j_hi - j_lo
                        base = self.NEGM.t[0:1, 0:1]
                        nsrc = bass.AP(base.tensor, base.offset + r + dil * j_lo, [[0, 1], [dil, nk]])
                        P.dma(lambda e, kbr=kbr, nsrc=nsrc, k_lo=k_lo, nk=nk: e.dma_start(out=kbr.t[0:1, k_lo:k_lo + nk], in_=nsrc, allow_slow_non_contiguous=True), kbr, reads=[self.NEGM, kbr], writes=[kbr])
                        for hg in range(2):
                            qT, kT, ssb, psb16, pT, nmx = qT_l[hg], kT_l[hg], ssb_l[hg], psb16_l[hg], pT_l[hg], nmx_l[hg]
                            for hl in range(4):
                                h = hg * 4 + hl
                                P.op("pe", lambda e, qT=qT, kT=kT, ssb=ssb, psb16=psb16, pT=pT, nmx=nmx, Q=Q, h=h, hl=hl: e.transpose(out=p_q.t[:, hl * 128:(hl + 1) * 128], in_=Q.t[:, h * 128:(h + 1) * 128], identity=self.identb.t[:, :]), reads=[Q, self.identb], writes=[p_q])
                                P.op("pe", lambda e, qT=qT, kT=kT, ssb=ssb, psb16=psb16, pT=pT, nmx=nmx, KA=KA, h=h, hl=hl: e.transpose(out=p_k.t[:, hl * 256:hl * 256 + 128], in_=KA.t[:, h * 128:(h + 1) * 128], identity=self.identb.t[:, :]), reads=[KA, self.identb], writes=[p_k])
                                P.op("pe", lambda e, qT=qT, kT=kT, ssb=ssb, psb16=psb16, pT=pT, nmx=nmx, KB_=KB_, h=h, hl=hl: e.transpose(out=p_k.t[:, hl * 256 + 128:hl * 256 + 256], in_=KB_.t[:, h * 128:(h + 1) * 128], identity=self.identb.t[:, :]), reads=[KB_, self.identb], writes=[p_k])
                            P.op("act", lambda e, qT=qT, kT=kT, ssb=ssb, psb16=psb16, pT=pT, nmx=nmx: e.activation(out=qT.t[:, :, :].rearrange("p a b -> p (a b)"), in_=p_q.t[:, :], func=AF.Copy), reads=[p_q], writes=[qT])
                            P.op("dve", lambda e, qT=qT, kT=kT, ssb=ssb, psb16=psb16, pT=pT, nmx=nmx: e.tensor_copy(out=kT.t[:, :, :].rearrange("p a b -> p (a b)"), in_=p_k.t[:, :]), reads=[p_k], writes=[kT])
                            for hl in range(4):
                                P.op("pe", lambda e, qT=qT, kT=kT, ssb=ssb, psb16=psb16, pT=pT, nmx=nmx, hl=hl: e.matmul(p_s.t[:, hl * 256:(hl + 1) * 256], lhsT=qT.t[:, hl, :], rhs=kT.t[:, hl, :], start=True, stop=False), reads=[qT, kT], writes=[p_s])
                                P.op("pe", lambda e, qT=qT, kT=kT, ssb=ssb, psb16=psb16, pT=pT, nmx=nmx, hl=hl, kbr=kbr: e.matmul(p_s.t[:, hl * 256:(hl + 1) * 256], lhsT=ones1.t[0:1, :], rhs=kbr.t[0:1, :], start=False, stop=True), reads=[ones1, kbr], writes=[p_s])
                            P.op("dve", lambda e, qT=qT, kT=kT, ssb=ssb, psb16=psb16, pT=pT, nmx=nmx, p=p, hg=hg: e.scalar_tensor_tensor(out=ssb.t[:, :, :].rearrange("p a b -> p (a b)"), in0=p_s.t[:, :], scalar=SCALE, in1=biasT[p].t[:, hg * 4:(hg + 1) * 4, :].rearrange("p a b -> p (a b)"), op0=ALU.mult, op1=ALU.add),
                                 reads=[p_s, biasT[p]], writes=[ssb])
                            P.op("dve", lambda e, qT=qT, kT=kT, ssb=ssb, psb16=psb16, pT=pT, nmx=nmx, st_=st_, hg=hg: e.tensor_reduce(out=st_.t[:, hg * 4:(hg + 1) * 4], in_=ssb.t[:, :, :], axis=AX.X, op=ALU.max), reads=[ssb], writes=[st_])
                            P.op("dve", lambda e, qT=qT, kT=kT, ssb=ssb, psb16=psb16, pT=pT, nmx=nmx, st_=st_, hg=hg: e.tensor_scalar(out=nmx.t[:, :], in0=st_.t[:, hg * 4:(hg + 1) * 4], scalar1=-1.0, scalar2=None, op0=ALU.mult), reads=[st_], writes=[nmx])
                            for hl in range(4):
                                h = hg * 4 + hl
                                P.op("act", lambda e, qT=qT, kT=kT, ssb=ssb, psb16=psb16, pT=pT, nmx=nmx, hl=hl, h=h, st_=st_: e.activation(out=psb16.t[:, hl, :], in_=ssb.t[:, hl, :], func=AF.Exp, bias=nmx.t[:, hl:hl + 1], scale=1.0, accum_out=st_.t[:, 8 + h:9 + h]), reads=[ssb, nmx], writes=[psb16, st_])
                            for hl in range(4):
                                for a in range(2):
                                    P.op("pe", lambda e, qT=qT, kT=kT, ssb=ssb, psb16=psb16, pT=pT, nmx=nmx, hl=hl, a=a: e.transpose(out=p_p.t[:, (hl * 2 + a) * 128:(hl * 2 + a + 1) * 128], in_=psb16.t[:, hl, a * 128:(a + 1) * 128], identity=self.identb.t[:, :]), reads=[psb16, self.identb], writes=[p_p])
                            P.op("dve", lambda e, qT=qT, kT=kT, ssb=ssb, psb16=psb16, pT=pT, nmx=nmx: e.tensor_copy(out=pT.t[:, :, :, :].rearrange("p a b c -> p (a b c)"), in_=p_p.t[:, :]), reads=[p_p], writes=[pT])
                            for hl in range(4):
                                h = hg * 4 + hl
                                P.op("pe", lambda e, qT=qT, kT=kT, ssb=ssb, psb16=psb16, pT=pT, nmx=nmx, hl=hl, h=h, VA=VA: e.matmul(p_o.t[:, hl * 128:(hl + 1) * 128], lhsT=pT.t[:, hl, 0, :], rhs=VA.t[:, h * 128:(h + 1) * 128], start=True, stop=False), reads=[pT, VA], writes=[p_o])
                                P.op("pe", lambda e, qT=qT, kT=kT, ssb=ssb, psb16=psb16, pT=pT, nmx=nmx, hl=hl, h=h, VB=VB: e.matmul(p_o.t[:, hl * 128:(hl + 1) * 128], lhsT=pT.t[:, hl, 1, :], rhs=VB.t[:, h * 128:(h + 1) * 128], start=False, stop=True), reads=[pT, VB], writes=[p_o])
                            P.op("act", lambda e, qT=qT, kT=kT, ssb=ssb, psb16=psb16, pT=pT, nmx=nmx, os_=os_, hg=hg: e.activation(out=os_.t[:, hg * 512:(hg + 1) * 512], in_=p_o.t[:, :], func=AF.Copy), reads=[p_o], writes=[os_])
                        sl = rows(128 * m, 128)
                        P.dma(lambda e, os_=os_, sl=sl, p=p: e.dma_start(out=self.AO.t[p, sl, :], in_=os_.t[:, :]), os_, reads=[os_], writes=[self.AO], eng="pool")
                        P.dma(lambda e, st_=st_, sl=sl, p=p: e.dma_start(out=self.AM.t[p, sl, :], in_=st_.t[:, :]), st_, reads=[st_], writes=[self.AM], eng="pool")
            self.barrier()
            P.stack = old

    def build(self):
        self.setup()
        for nm in ["p0", "p1", "p2", "p3", "p4", "p5", "p6", "p6a", "p6b", "p6c", "p7"]:
            if nm in self.stages:
                getattr(self, nm)()
        toks = self.barrier()
        st = self.P.emit(final_waits=toks)
        print("ops", st)
        return self.nc


def bc_last(ap, n):
    dims = [list(d) for d in ap.ap]
    assert dims[-1][1] == 1
    dims[-1] = [0, n]
    return bass.AP(ap.tensor, ap.offset, dims)


def bc_mid(ap, n):
    dims = [list(d) for d in ap.ap]
    dims = [dims[0], [0, n]] + dims[1:]
    return bass.AP(ap.tensor, ap.offset, dims)


class MK3(MK2):
    def __init__(self, L, stages, dump=(), lite=False, as_input=()):
        super().__init__(L, stages, dump, lite, as_input)
        P = self.P
        self.out = P.dram("out", [L, D], F32, kind="ExternalOutput")

    def y_to_YT(self, st, y16, row0, i):
        P = self.P
        pt = st["p_t"]
        yT = st["yT"][(i // 4) % 2]
        for c in range(8):
            P.op("pe", lambda e, c=c: e.transpose(out=pt.t[:, c * 128:(c + 1) * 128], in_=y16.t[:, c * 128:(c + 1) * 128], identity=self.identb.t[:, :]), reads=[y16, self.identb], writes=[pt])
        j = i % 4
        P.op("act", lambda e, j=j, yT=yT: e.activation(out=yT.t[:, :, j * 128:(j + 1) * 128], in_=pt.t[:, :].rearrange("p (c n) -> p c n", c=8), func=AF.Copy), reads=[pt, yT], writes=[yT])
        if j == 3:
            b = i // 4
            dst = self.YT.t[row0:row0 + AW, b * T:(b + 1) * T].rearrange("(c p) t -> p c t", p=128)
            P.dma(lambda e, dst=dst, yT=yT: e.dma_start(out=dst, in_=yT.t[:, :, :]), yT, reads=[yT], writes=[self.YT], eng="pool")

    def p5(self):
        P = self.P
        L = self.L
        with contextlib.ExitStack() as es:
            old = P.stack
            P.stack = es
            st = dict(p_t=P.psum("p5pt", [128, 8 * 128], BF16), yT=[P.sbuf("p5yT%d" % i, [128, 8, T], BF16) for i in range(2)])
            nb = P.sbuf("nattb", [128, AW], F32)
            P.dma(lambda e: e.dma_start(out=nb.t[:, :], in_=bcast_rows(self.nattn_in.t, 0, AW)), nb, writes=[nb])
            ao = [[P.sbuf("ao%d_%d" % (k, p), [128, AW], F32) for p in range(3)] for k in range(2)]
            am = [[P.sbuf("am%d_%d" % (k, p), [128, 16], F32) for p in range(3)] for k in range(2)]
            M = P.sbuf("p5M", [128, 8], F32)
            w = [P.sbuf("p5w%d" % p, [128, 8], F32) for p in range(3)]
            den = P.sbuf("p5den", [128, 8], F32)
            acc = P.sbuf("p5acc", [128, AW], F32)
            tmp = P.sbuf("p5tmp", [128, AW], F32)
            ss = P.sbuf("p5ss", [128, 1], F32)
            y16 = [P.sbuf("p5y%d" % i, [128, AW], BF16) for i in range(2)]
            for i in range(L // 128):
                k = i % 2
                for p in range(3):
                    P.dma(lambda e, k=k, p=p, i=i: e.dma_start(out=ao[k][p].t[:, :], in_=self.AO.t[p, i * 128:(i + 1) * 128, :]), ao[k][p], reads=[self.AO], writes=[ao[k][p]])
                    P.dma(lambda e, k=k, p=p, i=i: e.dma_start(out=am[k][p].t[:, :], in_=self.AM.t[p, i * 128:(i + 1) * 128, :]), am[k][p], reads=[self.AM], writes=[am[k][p]])
                a0, a1, a2 = am[k]
                P.op("dve", lambda e, a0=a0, a1=a1: e.tensor_tensor(out=M.t[:, :], in0=a0.t[:, 0:8], in1=a1.t[:, 0:8], op=ALU.max), reads=[a0, a1], writes=[M])
                P.op("dve", lambda e, a2=a2: e.tensor_tensor(out=M.t[:, :], in0=M.t[:, :], in1=a2.t[:, 0:8], op=ALU.max), reads=[M, a2], writes=[M])
                for p in range(3):
                    ap_ = am[k][p]
                    P.op("dve", lambda e, p=p, ap_=ap_: e.tensor_tensor(out=w[p].t[:, :], in0=ap_.t[:, 0:8], in1=M.t[:, :], op=ALU.subtract), reads=[ap_, M], writes=[w[p]])
                    P.op("act", lambda e, p=p: e.activation(out=w[p].t[:, :], in_=w[p].t[:, :], func=AF.Exp), reads=[w[p]], writes=[w[p]])
                P.op("dve", lambda e, a0=a0: e.tensor_tensor(out=den.t[:, :], in0=w[0].t[:, :], in1=a0.t[:, 8:16], op=ALU.mult), reads=[w[0], a0], writes=[den])
                for p in (1, 2):
                    ap_ = am[k][p]
                    P.op("dve", lambda e, p=p, ap_=ap_: e.tensor_tensor(out=M.t[:, :], in0=w[p].t[:, :], in1=ap_.t[:, 8:16], op=ALU.mult), reads=[w[p], ap_, M], writes=[M])
                    P.op("dve", lambda e: e.tensor_tensor(out=den.t[:, :], in0=den.t[:, :], in1=M.t[:, :], op=ALU.add), reads=[den, M], writes=[den])
                P.op("dve", lambda e: e.reciprocal(out=den.t[:, :], in_=den.t[:, :]), reads=[den], writes=[den])
                for p in range(3):
                    P.op("dve", lambda e, p=p: e.tensor_tensor(out=w[p].t[:, :], in0=w[p].t[:, :], in1=den.t[:, :], op=ALU.mult), reads=[w[p], den], writes=[w[p]])
                v3 = lambda t: t.t[:, :].rearrange("p (h d) -> p h d", h=8)
                wb = lambda p: bc_last(w[p].t[:, :].rearrange("p (h o) -> p h o", o=1), 128)
                P.op("dve", lambda e, k=k: e.tensor_tensor(out=v3(acc), in0=v3(ao[k][0]), in1=wb(0), op=ALU.mult), reads=[ao[k][0], w[0]], writes=[acc])
                for p in (1, 2):
                    P.op("dve", lambda e, k=k, p=p: e.tensor_tensor(out=v3(tmp), in0=v3(ao[k][p]), in1=wb(p), op=ALU.mult), reads=[ao[k][p], w[p], tmp], writes=[tmp])
                    P.op("dve", lambda e: e.tensor_tensor(out=acc.t[:, :], in0=acc.t[:, :], in1=tmp.t[:, :], op=ALU.add), reads=[acc, tmp], writes=[acc])
                P.op("act", lambda e: e.activation(out=tmp.t[:, :], in_=acc.t[:, :], func=AF.Square, accum_out=ss.t[:, 0:1]), reads=[acc, tmp], writes=[tmp, ss])
                P.op("act", lambda e: e.activation(out=ss.t[:, :], in_=ss.t[:, :], func=AF.Sqrt, bias=self.epsc.t[:, 0:1], scale=1.0 / AW), reads=[ss, self.epsc], writes=[ss])
                P.op("dve", lambda e: e.reciprocal(out=ss.t[:, :], in_=ss.t[:, :]), reads=[ss], writes=[ss])
                y = y16[i % 2]
                P.op("dve", lambda e, y=y: e.scalar_tensor_tensor(out=y.t[:, :], in0=acc.t[:, :], scalar=ss.t[:, 0:1], in1=nb.t[:, :], op0=ALU.mult, op1=ALU.mult), reads=[acc, ss, nb], writes=[y])
                self.y_to_YT(st, y, 0, i)
            self.barrier()
            P.stack = old

    def p7(self):
        P = self.P
        with contextlib.ExitStack() as es:
            old = P.stack
            P.stack = es
            s = self.alloc_ffn()
            xT, hT = s["xT"], s["hT"]
            for b in range(self.NB):
                src = self.X1T.t[:, b * T:(b + 1) * T].rearrange("(c p) t -> p c t", p=128)
                P.dma(lambda e, src=src: e.dma_start(out=xT.t[:, :, :], in_=src), xT, reads=[self.X1T], writes=[xT])
                srcy = self.YT.t[:, b * T:(b + 1) * T].rearrange("(c p) t -> p c t", p=128)
                P.dma(lambda e, srcy=srcy: e.dma_start(out=hT.t[:, :, :], in_=srcy), hT, reads=[self.YT], writes=[hT])
                for dc in range(KC):
                    w = s["wi"][s["wi_n"] % 6]
                    s["wi_n"] += 1
                    P.dma(lambda e, w=w, dc=dc: e.dma_start(out=w.t[:, :], in_=self.WOr.t[dc, :, :]), w, reads=[self.WOr], writes=[w])
                    py = s["pt"][s["pn"] % 2]
                    s["pn"] += 1
                    for kc in range(KC):
                        P.op("pe", lambda e, py=py, w=w, kc=kc: e.matmul(py.t[:, :], lhsT=w.t[:, kc * 128:(kc + 1) * 128], rhs=hT.t[:, kc, :], start=(kc == 0), stop=(kc == KC - 1)), reads=[w, hT], writes=[py])
                    Gc = self.AB.t[:, 5 * KC + dc:5 * KC + dc + 1]
                    P.op("dve", lambda e, py=py, dc=dc, Gc=Gc: e.scalar_tensor_tensor(out=xT.t[:, dc, :], in0=py.t[:, :], scalar=Gc, in1=xT.t[:, dc, :], op0=ALU.mult, op1=ALU.add),
                         reads=[py, xT, self.AB], writes=[xT])
                if "X2T" in self.dump:
                    dst = self.X2T.t[:, b * T:(b + 1) * T].rearrange("(c p) t -> p c t", p=128)
                    P.dma(lambda e, dst=dst: e.dma_start(out=dst, in_=xT.t[:, :, :]), xT, reads=[xT], writes=[self.X2T], eng="pool")
                self.rms_stats(s)
                self.norm_affine(s, 2)
                self.ffn(s, self.W2I, self.W2O, 2)
                self.rms_stats(s)
                rstd = s["rstd"]
                for c in range(KC):
                    nf = self.nrm.t[:, 3 * KC + c:3 * KC + c + 1]
                    P.op("dve", lambda e, c=c, nf=nf: e.scalar_tensor_tensor(out=xT.t[:, c, :], in0=xT.t[:, c, :], scalar=nf, in1=rstd.t[:, :], op0=ALU.mult, op1=ALU.mult), reads=[xT, rstd, self.nrm], writes=[xT])
                for j in range(T // 128):
                    xt = s["xtok"][j % 2]
                    for cg in range(KC // 4):
                        pt = s["pt"][s["pn"] % 2]
                        s["pn"] += 1
                        for ci in range(4):
                            c = cg * 4 + ci
                            P.op("pe", lambda e, pt=pt, c=c, ci=ci, j=j: e.transpose(out=pt.t[:, ci * 128:(ci + 1) * 128], in_=xT.t[:, c, j * 128:(j + 1) * 128], identity=self.ident.t[:, :]), reads=[xT, self.ident], writes=[pt])
                        if cg % 2 == 0:
                            P.op("dve", lambda e, pt=pt, xt=xt, cg=cg: e.tensor_copy(out=xt.t[:, cg * 512:(cg + 1) * 512], in_=pt.t[:, :]), reads=[pt, xt], writes=[xt])
                        else:
                            P.op("act", lambda e, pt=pt, xt=xt, cg=cg: e.activation(out=xt.t[:, cg * 512:(cg + 1) * 512], in_=pt.t[:, :], func=AF.Copy), reads=[pt, xt], writes=[xt])
                    r0 = b * T + j * 128
                    P.dma(lambda e, xt=xt, r0=r0: e.dma_start(out=self.out.t[r0:r0 + 128, :], in_=xt.t[:, :]), xt, reads=[xt], writes=[self.out], eng="pool")
            self.barrier()
            P.stack = old


import os
CH = 128
DN_STOP = int(os.environ.get('DN_STOP', '99'))


def make_dnconst():
    a = np.arange(128)
    low_i = (a[:, None] >= a[None, :]).astype(np.float32)
    up_i = (a[:, None] <= a[None, :]).astype(np.float32)
    low_s = (a[:, None] > a[None, :]).astype(np.float32)
    up_s = (a[:, None] < a[None, :]).astype(np.float32)
    sel = np.zeros((128, 8 * 128), np.float32)
    for h in range(8):
        sel[h, h * 128:(h + 1) * 128] = 1.0
    blk = ((a[:, None] // 32) == (a[None, :] // 32)).astype(np.float32)
    return np.ascontiguousarray(np.concatenate([low_i, up_i, low_s, up_s, sel, blk, 1.0 - blk], axis=1))


class MK4(MK3):
    def __init__(self, L, stages, dump=(), lite=False, as_input=()):
        super().__init__(L, stages, dump, lite, as_input)
        P = self.P
        def ein(n, s, dt=F32):
            self.ext_inputs.append(n)
            return P.dram(n, s, dt, kind="ExternalInput")
        dmp = lambda n: ("ExternalOutput" if n in self.dump else ("ExternalInput" if n in self.as_input else None))
        self.conv_wT = ein("conv_wT", [128, 120])
        self.gate_par = ein("gate_par", [16, 2])
        self.ndn_in = ein("norm_dn_out", [1, 128])
        self.dnconst = ein("dnconst", [128, 14 * 128])
        self.QT = P.dram("QT", [8, 128, L], F32, kind=dmp("QT"))
        self.KT = P.dram("KT", [8, 128, L], F32, kind=dmp("KT"))
        self.QTOK = P.dram("QTOK", [L, 8, 128], F32, kind=dmp("QTOK"))
        self.KTOK = P.dram("KTOK", [L, 8, 128], F32, kind=dmp("KTOK"))
        self.VTOK = P.dram("VTOK", [L, 8, 128], F32, kind=dmp("VTOK"))
        self.GB = P.dram("GB", [L, 32], F32, kind=dmp("GB"))
        self.ODN = P.dram("ODN", [2, L, AW], F32, kind=dmp("ODN"))

    def p6a(self):
        P = self.P
        L, NB = self.L, self.NB
        QS = DH ** -0.5
        with contextlib.ExitStack() as es:
            old = P.stack
            P.stack = es
            cw = P.sbuf("cw", [128, 120], F32)
            P.dma(lambda e: e.dma_start(out=cw.t[:, :], in_=self.conv_wT.t[:, :]), cw, writes=[cw])
            gp = P.sbuf("gp", [16, 2], F32)
            P.dma(lambda e: e.dma_start(out=gp.t[:, :], in_=self.gate_par.t[:, :]), gp, writes=[gp])
            negA = P.sbuf("negA", [16, 1], F32)
            one16 = P.sbuf("one16", [16, 1], F32)
            P.op("pool", lambda e: e.memset(one16.t[:, :], 1.0), writes=[one16])
            P.op("act", lambda e: e.activation(out=negA.t[:, :], in_=gp.t[:, 0:1], func=AF.Exp), reads=[gp], writes=[negA])
            P.op("dve", lambda e: e.tensor_scalar(out=negA.t[:, :], in0=negA.t[:, :], scalar1=-1.0, scalar2=None, op0=ALU.mult), reads=[negA], writes=[negA])
            xin = [P.sbuf("xin%d" % i, [128, T + 4], F32) for i in range(3)]
            acc = [P.sbuf("cacc%d" % i, [128, T], F32) for i in range(2)]
            sv = [P.sbuf("csv%d" % i, [128, T], F32) for i in range(2)]
            sq = [P.sbuf("csq%d" % i, [128, T], F32) for i in range(2)]
            rs = [P.sbuf("crs%d" % i, [128, T], F32) for i in range(2)]
            xn = [P.sbuf("cxn%d" % i, [128, T], F32) for i in range(2)]
            tk = [P.sbuf("ctk%d" % i, [128, 4, 128], F32) for i in range(2)]
            mk = P.sbuf("cmk", [128, T], F32)
            pss = [P.psum("cpss%d" % i, [128, T], F32) for i in range(2)]
            ptt = [P.psum("cptt%d" % i, [128, T], F32) for i in range(2)]
            braw = P.sbuf("braw", [16, T], F32)
            araw = P.sbuf("araw", [16, T], F32)
            gtk = P.sbuf("gtk", [128, 4, 32], F32)
            psg = P.psum("psg", [128, T], F32)
            it = 0
            for b in range(NB):
                P.dma(lambda e, b=b: e.dma_start(out=mk.t[:, :], in_=bcast_rows(self.mask.t, b * T, T)), mk, reads=[self.mask], writes=[mk])
                for f in range(24):
                    kind, h = f // 8, f % 8
                    xi = xin[it % 3]
                    k2 = it % 2
                    it += 1
                    lo = b * T - 2
                    hi = b * T + T + 2
                    c_lo, c_hi = 0, T + 4
                    if b == 0:
                        P.op("pool", lambda e, xi=xi: e.memset(xi.t[:, 0:2], 0.0), reads=[xi], writes=[xi])
                        lo, c_lo = 0, 2
                    if b == NB - 1:
                        P.op("pool", lambda e, xi=xi: e.memset(xi.t[:, T + 2:T + 4], 0.0), reads=[xi], writes=[xi])
                        hi, c_hi = L, T + 2
                    P.dma(lambda e, xi=xi, f=f, lo=lo, hi=hi, c_lo=c_lo, c_hi=c_hi: e.dma_start(out=xi.t[:, c_lo:c_hi], in_=self.DQKVT.t[f * 128:(f + 1) * 128, lo:hi]), xi, reads=[self.DQKVT, xi], writes=[xi])
                    ac = acc[k2]
                    P.op("dve", lambda e, ac=ac, xi=xi, f=f: e.tensor_scalar(out=ac.t[:, :], in0=xi.t[:, 0:T], scalar1=cw.t[:, f * 5:f * 5 + 1], scalar2=None, op0=ALU.mult), reads=[xi, cw], writes=[ac])
                    for j in range(1, 5):
                        P.op("dve", lambda e, ac=ac, xi=xi, f=f, j=j: e.scalar_tensor_tensor(out=ac.t[:, :], in0=xi.t[:, j:j + T], scalar=cw.t[:, f * 5 + j:f * 5 + j + 1], in1=ac.t[:, :], op0=ALU.mult, op1=ALU.add), reads=[xi, cw, ac], writes=[ac])
                    P.op("dve", lambda e, ac=ac: e.tensor_tensor(out=ac.t[:, :], in0=ac.t[:, :], in1=mk.t[:, :], op=ALU.mult), reads=[ac, mk], writes=[ac])
                    s_ = sv[k2]
                    P.op("act", lambda e, ac=ac, s_=s_: e.activation(out=s_.t[:, :], in_=ac.t[:, :], func=AF.Silu), reads=[ac], writes=[s_])
                    if kind < 2:
                        q_ = sq[k2]
                        ps = pss[k2]
                        r_ = rs[k2]
                        x_ = xn[k2]
                        P.op("act", lambda e, q_=q_, s_=s_: e.activation(out=q_.t[:, :], in_=s_.t[:, :], func=AF.Square), reads=[s_], writes=[q_])
                        P.op("pe", lambda e, ps=ps, q_=q_: e.matmul(ps.t[:, :], lhsT=self.ones.t[:, :], rhs=q_.t[:, :], start=True, stop=True), reads=[q_, self.ones], writes=[ps])
                        P.op("act", lambda e, r_=r_, ps=ps: e.activation(out=r_.t[:, :], in_=ps.t[:, :], func=AF.Sqrt, bias=self.epsc.t[:, 0:1], scale=1.0), reads=[ps, self.epsc], writes=[r_])
                        P.op("dve", lambda e, r_=r_: e.reciprocal(out=r_.t[:, :], in_=r_.t[:, :]), reads=[r_], writes=[r_])
                        scl = QS if kind == 0 else 1.0
                        P.op("dve", lambda e, x_=x_, s_=s_, r_=r_, scl=scl: e.scalar_tensor_tensor(out=x_.t[:, :], in0=s_.t[:, :], scalar=scl, in1=r_.t[:, :], op0=ALU.mult, op1=ALU.mult), reads=[s_, r_], writes=[x_])
                        dstT = (self.QT if kind == 0 else self.KT)
                        P.dma(lambda e, x_=x_, dstT=dstT, h=h, b=b: e.dma_start(out=dstT.t[h, :, b * T:(b + 1) * T], in_=x_.t[:, :]), x_, reads=[x_], writes=[dstT], eng="pool")
                        src_fm = x_
                    else:
                        src_fm = s_
                    pt = ptt[k2]
                    for j in range(4):
                        P.op("pe", lambda e, pt=pt, src_fm=src_fm, j=j: e.transpose(out=pt.t[:, j * 128:(j + 1) * 128], in_=src_fm.t[:, j * 128:(j + 1) * 128], identity=self.ident.t[:, :]), reads=[src_fm, self.ident], writes=[pt])
                    t_ = tk[k2]
                    P.op("act", lambda e, t_=t_, pt=pt: e.activation(out=t_.t[:, :, :].rearrange("p a b -> p (a b)"), in_=pt.t[:, :], func=AF.Copy), reads=[pt], writes=[t_])
                    dtok = [self.QTOK, self.KTOK, self.VTOK][kind]
                    dst = dtok.t[b * T:(b + 1) * T, h, :].rearrange("(j p) d -> p j d", p=128)
                    P.dma(lambda e, t_=t_, dst=dst: e.dma_start(out=dst, in_=t_.t[:, :, :]), t_, reads=[t_], writes=[dtok], eng="pool")
                P.dma(lambda e, b=b: e.dma_start(out=braw.t[:, :], in_=self.BAT.t[0:16, b * T:(b + 1) * T]), braw, reads=[self.BAT], writes=[braw])
                P.dma(lambda e, b=b: e.dma_start(out=araw.t[:, :], in_=self.BAT.t[16:32, b * T:(b + 1) * T]), araw, reads=[self.BAT], writes=[araw])
                P.op("act", lambda e: e.activation(out=braw.t[:, :], in_=braw.t[:, :], func=AF.Sigmoid), reads=[braw], writes=[braw])
                P.op("act", lambda e: e.activation(out=araw.t[:, :], in_=araw.t[:, :], func=AF.Exp, bias=gp.t[:, 1:2], scale=1.0), reads=[araw, gp], writes=[araw])
                P.op("act", lambda e: e.activation(out=araw.t[:, :], in_=araw.t[:, :], func=AF.Ln, bias=one16.t[:, 0:1], scale=1.0), reads=[araw, one16], writes=[araw])
                P.op("dve", lambda e: e.tensor_scalar(out=araw.t[:, :], in0=araw.t[:, :], scalar1=negA.t[:, 0:1], scalar2=None, op0=ALU.mult), reads=[araw, negA], writes=[araw])
                for j in range(4):
                    P.op("pe", lambda e, j=j: e.transpose(out=psg.t[:, j * 32:j * 32 + 16], in_=braw.t[:, j * 128:(j + 1) * 128], identity=self.ident.t[0:16, 0:16]), reads=[braw, self.ident], writes=[psg])
                    P.op("pe", lambda e, j=j: e.transpose(out=psg.t[:, j * 32 + 16:j * 32 + 32], in_=araw.t[:, j * 128:(j + 1) * 128], identity=self.ident.t[0:16, 0:16]), reads=[araw, self.ident], writes=[psg])
                P.op("dve", lambda e: e.tensor_copy(out=gtk.t[:, :, :].rearrange("p a b -> p (a b)"), in_=psg.t[:, 0:128]), reads=[psg], writes=[gtk])
                dstg = self.GB.t[b * T:(b + 1) * T, :].rearrange("(j p) c -> p j c", p=128)
                P.dma(lambda e, dstg=dstg: e.dma_start(out=dstg, in_=gtk.t[:, :, :]), gtk, reads=[gtk], writes=[self.GB], eng="pool")
            self.barrier()
            P.stack = old

    def p6b(self):
        P = self.P
        L = self.L
        NCH = L // CH
        with contextlib.ExitStack() as es:
            old = P.stack
            P.stack = es
            NHC = 4
            big = lambda n, dt=F32: P.sbuf(n, [128, NHC, 128], dt)
            v3 = lambda t: t.t[:, :, :]
            f2 = lambda t: t.t[:, :, :].rearrange("p a b -> p (a b)")
            pv3 = lambda t: t.t[:, :].rearrange("p (a b) -> p a b", a=NHC)
            col8 = lambda ap: bc_last(ap.rearrange("p (h o) -> p h o", o=1), 128)
            dnc = P.sbuf("dnc", [128, 14 * 128], F32)
            P.dma(lambda e: e.dma_start(out=dnc.t[:, :], in_=self.dnconst.t[:, :]), dnc, writes=[dnc])
            LOWI, UPI, LOWS, UPS = [dnc.t[:, i * 128:(i + 1) * 128] for i in range(4)]
            SEL = lambda h: dnc.t[0:NHC, 512 + h * 128:512 + (h + 1) * 128]
            BLK = dnc.t[:, 1536:1664]
            NBLK = dnc.t[:, 1664:1792]

            def act_evac(dst, src):
                P.op("act", lambda e, dst=dst, src=src: e.activation(out=f2(dst), in_=src.t[:, :], func=AF.Copy), reads=[src], writes=[dst])

            def mm8(dst, lh, rh):
                for h in range(NHC):
                    P.op("pe", lambda e, h=h, dst=dst, lh=lh, rh=rh: e.matmul(dst.t[:, h * 128:(h + 1) * 128], lhsT=lh.t[:, h, :], rhs=rh.t[:, h, :], start=True, stop=True), reads=[lh, rh], writes=[dst])

            def dve_evac(dst, src):
                P.op("dve", lambda e, dst=dst, src=src: e.tensor_copy(out=f2(dst), in_=src.t[:, :]), reads=[src], writes=[dst])

            def dve_acc(dst, a_, src):
                P.op("dve", lambda e, dst=dst, a_=a_, src=src: e.tensor_tensor(out=f2(dst), in0=f2(a_), in1=src.t[:, :], op=ALU.add), reads=[a_, src], writes=[dst])

            def tt(dst_ap, in0, in1, op, reads, writes):
                P.op("dve", lambda e: e.tensor_tensor(out=dst_ap, in0=in0, in1=in1, op=op), reads=reads, writes=writes)

            chains = []
            for c in range(4):
                W = dict(
                    inb=[dict(qT=big("c%d_qT%d" % (c, i)), kT=big("c%d_kT%d" % (c, i)), ktok=big("c%d_ktok%d" % (c, i)), vtok=big("c%d_vtok%d" % (c, i)),
                              gb=P.sbuf("c%d_gb%d" % (c, i), [128, 32], F32)) for i in range(2)],
                    t=[big("c%d_t%d" % (c, i)) for i in range(6)],
                    En=big("c%d_En" % c), Ens=big("c%d_Ens" % c), egrow=big("c%d_egrow" % c),
                    gsm=P.sbuf("c%d_gsm" % c, [128, 2 * NHC], F32), gcrow=P.sbuf("c%d_gcrow" % c, [NHC, 128], F32), nbrow=P.sbuf("c%d_nbrow" % c, [NHC, 128], F32),
                    sm={n: P.sbuf("c%d_%s" % (c, n), [128, NHC], F32) for n in ("egc", "rev", "erev", "egl", "nbeta", "bege")},
                    o_sb=[big("c%d_o%d" % (c, i)) for i in range(2)], S=big("c%d_S" % c), it=0,
                    p=[P.psum("c%d_p%d" % (c, i), [128, NHC * 128], F32) for i in range(2)])
                chains.append(W)

            def chunk_gen(W, dirv, n, h0):
                MR_s = LOWS if dirv == 0 else UPS
                MC = UPI if dirv == 0 else LOWI
                MC_s = UPS if dirv == 0 else LOWS
                ib = W["inb"][W["it"] % 2]
                ob = W["o_sb"][W["it"] % 2]
                W["it"] += 1
                qT, kT, ktok, vtok, gb = ib["qT"], ib["kT"], ib["ktok"], ib["vtok"], ib["gb"]
                t0, t1, t2, t3, t4, t5 = W["t"]
                En, Ens, egrow, gsm, gcrow, nbrow, S = W["En"], W["Ens"], W["egrow"], W["gsm"], W["gcrow"], W["nbrow"], W["S"]
                egc, rev, erev, egl, nbeta, bege = [W["sm"][k_] for k_ in ("egc", "rev", "erev", "egl", "nbeta", "bege")]
                p0, p1 = W["p"]
                c0, c1 = n * CH, (n + 1) * CH
                P.dma(lambda e: e.dma_start(out=v3(qT), in_=self.QT.t[h0:h0 + NHC, :, c0:c1].rearrange("h d a -> d h a")), qT, reads=[self.QT], writes=[qT])
                P.dma(lambda e: e.dma_start(out=v3(kT), in_=self.KT.t[h0:h0 + NHC, :, c0:c1].rearrange("h d a -> d h a")), kT, reads=[self.KT], writes=[kT])
                P.dma(lambda e: e.dma_start(out=v3(ktok), in_=self.KTOK.t[c0:c1, h0:h0 + NHC, :]), ktok, reads=[self.KTOK], writes=[ktok])
                P.dma(lambda e: e.dma_start(out=v3(vtok), in_=self.VTOK.t[c0:c1, h0:h0 + NHC, :]), vtok, reads=[self.VTOK], writes=[vtok])
                P.dma(lambda e: e.dma_start(out=gb.t[:, :], in_=self.GB.t[c0:c1, :]), gb, reads=[self.GB], writes=[gb])
                g8 = gb.t[:, 16 + dirv * 8 + h0:16 + dirv * 8 + h0 + NHC]
                b8 = gb.t[:, dirv * 8 + h0:dirv * 8 + h0 + NHC]
                gc = gsm.t[:, 0:NHC]
                tot = gsm.t[:, NHC:2 * NHC]
                P.op("pe", lambda e: e.matmul(p0.t[:, 0:NHC], lhsT=MC, rhs=g8, start=True, stop=True), reads=[gb, dnc], writes=[p0])
                P.op("pe", lambda e: e.matmul(p0.t[:, NHC:2 * NHC], lhsT=self.ones.t[:, :], rhs=g8, start=True, stop=True), reads=[gb, self.ones], writes=[p0])
                P.op("pe", lambda e: e.matmul(p1.t[0:NHC, 0:128], lhsT=g8, rhs=MC, start=True, stop=True), reads=[gb, dnc], writes=[p1])
                P.op("dve", lambda e: e.tensor_copy(out=gsm.t[:, :], in_=p0.t[:, 0:2 * NHC]), reads=[p0], writes=[gsm])
                P.op("dve", lambda e: e.tensor_copy(out=gcrow.t[:, :], in_=p1.t[0:NHC, 0:128]), reads=[p1], writes=[gcrow])
                P.op("dve", lambda e: e.tensor_scalar(out=nbeta.t[:, :], in0=b8, scalar1=-1.0, scalar2=None, op0=ALU.mult), reads=[gb], writes=[nbeta])
                yield
                P.op("act", lambda e: e.activation(out=egc.t[:, :], in_=gc, func=AF.Exp), reads=[gsm], writes=[egc])
                tt(rev.t[:, :], tot, gc, ALU.subtract, [gsm], [rev])
                P.op("act", lambda e: e.activation(out=erev.t[:, :], in_=rev.t[:, :], func=AF.Exp), reads=[rev], writes=[erev])
                P.op("act", lambda e: e.activation(out=egl.t[:, :], in_=tot, func=AF.Exp), reads=[gsm], writes=[egl])
                tt(bege.t[:, :], b8, egc.t[:, :], ALU.mult, [gb, egc], [bege])
                P.op("pe", lambda e: e.matmul(p1.t[0:NHC, 128:256], lhsT=nbeta.t[:, :], rhs=self.ident.t[:, :], start=True, stop=True), reads=[nbeta, self.ident], writes=[p1])
                P.op("dve", lambda e: e.tensor_copy(out=nbrow.t[:, :], in_=p1.t[0:NHC, 128:256]), reads=[p1], writes=[nbrow])
                for h in range(NHC):
                    P.op("pe", lambda e, h=h: e.matmul(p0.t[:, h * 128:(h + 1) * 128], lhsT=SEL(h), rhs=gcrow.t[:, :], start=True, stop=True), reads=[dnc, gcrow], writes=[p0])
                yield
                tt(v3(t0), col8(gc), pv3(p0), ALU.subtract, [gsm, p0], [t0])
                P.op("act", lambda e: e.activation(out=f2(egrow), in_=p0.t[:, :], func=AF.Exp), reads=[p0], writes=[egrow])
                P.op("dve", lambda e: e.tensor_scalar(out=f2(t1), in0=f2(t0), scalar1=0.0, scalar2=None, op0=ALU.min), reads=[t0], writes=[t1])
                P.op("act", lambda e: e.activation(out=f2(t1), in_=f2(t1), func=AF.Exp), reads=[t1], writes=[t1])
                tt(v3(t1), v3(t1), bc_mid(MR_s, NHC), ALU.mult, [t1, dnc], [t1])
                P.op("dve", lambda e: e.tensor_scalar(out=f2(En), in0=f2(t0), scalar1=0.0, scalar2=-1.0, op0=ALU.max, op1=ALU.mult), reads=[t0], writes=[En])
                P.op("act", lambda e: e.activation(out=f2(En), in_=f2(En), func=AF.Exp), reads=[En], writes=[En])
                tt(v3(Ens), v3(En), bc_mid(MC_s, NHC), ALU.mult, [En, dnc], [Ens])
                tt(v3(En), v3(En), bc_mid(MC, NHC), ALU.mult, [En, dnc], [En])
                yield
                for h in range(NHC):
                    P.op("pe", lambda e, h=h: e.matmul(p1.t[:, h * 128:(h + 1) * 128], lhsT=SEL(h), rhs=nbrow.t[:, :], start=True, stop=True), reads=[dnc, nbrow], writes=[p1])
                for h in range(NHC):
                    P.op("pe", lambda e, h=h: e.matmul(p0.t[:, h * 128:(h + 1) * 128], lhsT=kT.t[:, h, :], rhs=kT.t[:, h, :], start=True, stop=True), reads=[kT], writes=[p0])
                yield
                tt(f2(t0), p0.t[:, :], f2(t1), ALU.mult, [p0, t1], [t0])
                tt(v3(t0), v3(t0), col8(nbeta.t[:, :]), ALU.mult, [t0, nbeta], [t0])
                tt(f2(Ens), p0.t[:, :], f2(Ens), ALU.mult, [p0, Ens], [Ens])
                tt(f2(Ens), f2(Ens), p1.t[:, :], ALU.mult, [Ens, p1], [Ens])
                tt(v3(t0), v3(t0), bc_mid(BLK, NHC), ALU.mult, [t0, dnc], [t0])
                tt(v3(t1), v3(Ens), bc_mid(BLK, NHC), ALU.mult, [Ens, dnc], [t1])
                tt(v3(Ens), v3(Ens), bc_mid(NBLK, NHC), ALU.mult, [Ens, dnc], [Ens])
                tt(v3(t2), v3(t1), bc_mid(self.ident.t[:, :], NHC), ALU.add, [t1, self.ident], [t2])
                tt(v3(t3), v3(t0), bc_mid(self.ident.t[:, :], NHC), ALU.add, [t0, self.ident], [t3])
                yield
                Xc, Xn_, Yc, Yn_, Dt, DtT, Uo = t1, t4, t0, t5, t2, t3, Ens
                for lvl in range(4):
                    mm8(p0, Xc, Yc)
                    mm8(p1, Yc, Xc)
                    yield
                    act_evac(Yn_, p0)
                    dve_evac(Xn_, p1)
                    mm8(p0, Yn_, Dt)
                    mm8(p1, Xn_, DtT)
                    yield
                    dve_acc(Dt, Dt, p0)
                    dve_acc(DtT, DtT, p1)
                    Xc, Xn_ = Xn_, Xc
                    Yc, Yn_ = Yn_, Yc
                Mt, MtT, MtT2, P1, T32 = t0, t1, t4, t5, t0
                mm8(p0, DtT, Uo)
                mm8(p1, Uo, DtT)
                yield
                act_evac(Mt, p0)
                dve_evac(MtT, p1)
                mm8(p0, Mt, MtT)
                yield
                act_evac(MtT2, p0)
                mm8(p1, MtT2, Dt)
                yield
                dve_acc(P1, Dt, p1)
                mm8(p0, MtT, P1)
                yield
                dve_acc(T32, P1, p0)
                rhs_v, rhs_w, u_sb, wT_sb, kdec, vnew = t1, t2, t3, t4, t5, Ens
                tt(v3(rhs_v), v3(vtok), col8(b8), ALU.mult, [vtok, gb], [rhs_v])
                tt(v3(rhs_w), v3(ktok), col8(bege.t[:, :]), ALU.mult, [ktok, bege], [rhs_w])
                mm8(p0, T32, rhs_v)
                mm8(p1, rhs_w, T32)
                yield
                act_evac(u_sb, p0)
                dve_evac(wT_sb, p1)
                mm8(p0, kT, qT)
                yield
                tt(f2(En), p0.t[:, :], f2(En), ALU.mult, [p0, En], [En])
                tt(f2(egrow), f2(qT), f2(egrow), ALU.mult, [qT, egrow], [egrow])
                tt(v3(kdec), v3(ktok), col8(erev.t[:, :]), ALU.mult, [ktok, erev], [kdec])
                yield
                mm8(p1, wT_sb, S)
                yield
                tt(f2(vnew), f2(u_sb), p1.t[:, :], ALU.subtract, [u_sb, p1], [vnew])
                for h in range(NHC):
                    P.op("pe", lambda e, h=h: e.matmul(p0.t[:, h * 128:(h + 1) * 128], lhsT=egrow.t[:, h, :], rhs=S.t[:, h, :], start=True, stop=False), reads=[egrow, S], writes=[p0])
                    P.op("pe", lambda e, h=h: e.matmul(p0.t[:, h * 128:(h + 1) * 128], lhsT=En.t[:, h, :], rhs=vnew.t[:, h, :], start=False, stop=True), reads=[En, vnew], writes=[p0])
                mm8(p1, kdec, vnew)
                yield
                act_evac(ob, p0)
                P.dma(lambda e: e.dma_start(out=self.ODN.t[dirv, c0:c1, h0 * 128:(h0 + NHC) * 128], in_=f2(ob)), ob, reads=[ob], writes=[self.ODN], eng="pool")
                tt(v3(S), v3(S), col8(egl.t[:, :]), ALU.mult, [S, egl], [S])
                tt(f2(S), f2(S), p1.t[:, :], ALU.add, [S, p1], [S])
                yield

            for c in range(4):
                S_ = chains[c]["S"]
                P.op("dve", lambda e, S_=S_: e.memset(f2(S_), 0.0), reads=[S_], writes=[S_])
            for i in range(NCH):
                gens = [chunk_gen(chains[0], 0, i, 0), chunk_gen(chains[1], 1, NCH - 1 - i, 0),
                        chunk_gen(chains[2], 0, i, NHC), chunk_gen(chains[3], 1, NCH - 1 - i, NHC)]
                alive = [True] * 4
                while any(alive):
                    for c in range(4):
                        if alive[c]:
                            try:
                                next(gens[c])
                            except StopIteration:
                                alive[c] = False
            self.barrier()
            P.stack = old

    def p6c(self):
        P = self.P
        L = self.L
        with contextlib.ExitStack() as es:
            old = P.stack
            P.stack = es
            st = dict(p_t=P.psum("p6pt", [128, 8 * 128], BF16), yT=[P.sbuf("p6yT%d" % i, [128, 8, T], BF16) for i in range(2)])
            nd = P.sbuf("ndnb", [128, 128], F32)
            P.dma(lambda e: e.dma_start(out=nd.t[:, :], in_=bcast_rows(self.ndn_in.t, 0, 128)), nd, writes=[nd])
            of = [P.sbuf("of%d" % i, [128, AW], F32) for i in range(2)]
            obk = [P.sbuf("obk%d" % i, [128, AW], F32) for i in range(2)]
            zt = [P.sbuf("zt%d" % i, [128, AW], F32) for i in range(2)]
            tmp = P.sbuf("p6tmp", [128, AW], F32)
            ss = P.sbuf("p6ss", [128, 8], F32)
            y16 = [P.sbuf("p6y%d" % i, [128, AW], BF16) for i in range(2)]
            v3 = lambda t: t.t[:, :].rearrange("p (h d) -> p h d", h=8)
            for i in range(L // 128):
                k = i % 2
                a, bb, z = of[k], obk[k], zt[k]
                r0, r1 = i * 128, (i + 1) * 128
                P.dma(lambda e, a=a, r0=r0, r1=r1: e.dma_start(out=a.t[:, :], in_=self.ODN.t[0, r0:r1, :]), a, reads=[self.ODN], writes=[a])
                P.dma(lambda e, bb=bb, r0=r0, r1=r1: e.dma_start(out=bb.t[:, :], in_=self.ODN.t[1, r0:r1, :]), bb, reads=[self.ODN], writes=[bb])
                P.dma(lambda e, z=z, r0=r0, r1=r1: e.dma_start(out=z.t[:, :], in_=self.Z.t[r0:r1, :]), z, reads=[self.Z], writes=[z])
                P.op("dve", lambda e, a=a, bb=bb: e.tensor_tensor(out=a.t[:, :], in0=a.t[:, :], in1=bb.t[:, :], op=ALU.add), reads=[a, bb], writes=[a])
                P.op("dve", lambda e, a=a: e.tensor_tensor(out=tmp.t[:, :], in0=a.t[:, :], in1=a.t[:, :], op=ALU.mult), reads=[a, tmp], writes=[tmp])
                P.op("dve", lambda e: e.tensor_reduce(out=ss.t[:, :], in_=v3(tmp), axis=AX.X, op=ALU.add), reads=[tmp], writes=[ss])
                P.op("act", lambda e: e.activation(out=ss.t[:, :], in_=ss.t[:, :], func=AF.Sqrt, bias=self.epsc.t[:, 0:1], scale=1.0 / DH), reads=[ss, self.epsc], writes=[ss])
                P.op("dve", lambda e: e.reciprocal(out=ss.t[:, :], in_=ss.t[:, :]), reads=[ss], writes=[ss])
                P.op("act", lambda e, z=z: e.activation(out=z.t[:, :], in_=z.t[:, :], func=AF.Silu), reads=[z], writes=[z])
                P.op("dve", lambda e, a=a: e.tensor_tensor(out=v3(a), in0=v3(a), in1=bc_last(ss.t[:, :].rearrange("p (h o) -> p h o", o=1), 128), op=ALU.mult), reads=[a, ss], writes=[a])
                P.op("dve", lambda e, a=a: e.tensor_tensor(out=v3(a), in0=v3(a), in1=bc_mid(nd.t[:, :], 8), op=ALU.mult), reads=[a, nd], writes=[a])
                y = y16[k]
                P.op("dve", lambda e, a=a, z=z, y=y: e.tensor_tensor(out=y.t[:, :], in0=a.t[:, :], in1=z.t[:, :], op=ALU.mult), reads=[a, z], writes=[y])
                self.y_to_YT(st, y, AW, i)
            self.barrier()
            P.stack = old

    def p6(self):
        self.p6a()
        self.p6b()
        self.p6c()


def _fm(v):
    return np.ascontiguousarray(np.asarray(v, np.float32).reshape(-1, 128).T)


def _prep_core(inp, x, c, Lreal, L, shared):
    xp = np.zeros((L, D), np.float32)
    mask = np.zeros((1, L), np.float32)
    if Lreal > 0:
        xp[:Lreal] = x
        mask[0, :Lreal] = 1.0
    m = dict(shared)
    m["x"] = xp
    m["cT"] = _fm(c)
    m["mask"] = mask
    return m


_NC_CACHE = {}


def kernel(x_prompt, x_sample, c_prompt, c_sample, w_mod, b_mod, norm_ffn1, w_ffn1_in, w_ffn1_out, norm_mix, w_in, conv_w,
           a_log, dt_bias, norm_attn_out, norm_dn_out, w_out, norm_ffn2, w_ffn2_in, w_ffn2_out, rel_bias, norm_final):
    f32 = lambda a: np.ascontiguousarray(np.asarray(a, dtype=np.float32))
    x_prompt, x_sample, c_prompt, c_sample = f32(x_prompt), f32(x_sample), f32(c_prompt), f32(c_sample)
    L = x_prompt.shape[1]
    Ls = x_sample.shape[1]
    norms = [f32(norm_ffn1)[0], f32(norm_mix)[0], f32(norm_ffn2)[0], f32(norm_final)]
    shared = dict(
        w_mod=f32(w_mod)[0], b_modT=_fm(f32(b_mod)[0]),
        normsT=np.ascontiguousarray(np.concatenate([_fm(n) for n in norms], axis=1)),
        w_ffn1_in=f32(w_ffn1_in)[0], w_ffn1_out=f32(w_ffn1_out)[0], w_in=f32(w_in)[0], w_out=f32(w_out)[0],
        w_ffn2_in=f32(w_ffn2_in)[0], w_ffn2_out=f32(w_ffn2_out)[0], ident=np.eye(128, dtype=np.float32),
        rel_bias=f32(rel_bias), onehot=make_onehot(), norm_attn_out=f32(norm_attn_out)[0].reshape(1, 1024),
        conv_wT=np.ascontiguousarray(f32(conv_w)[0].T.reshape(24, 128, 5).transpose(1, 0, 2).reshape(128, 120)),
        gate_par=np.ascontiguousarray(np.stack([f32(a_log)[0].reshape(16), f32(dt_bias)[0].reshape(16)], axis=1)),
        norm_dn_out=f32(norm_dn_out)[0].reshape(1, 128), dnconst=make_dnconst(),
    )
    if L not in _NC_CACHE:
        mk = MK4(L, stages=["p0", "p1", "p2", "p3", "p4", "p5", "p6", "p7"])
        _NC_CACHE[L] = (mk.build(), list(mk.ext_inputs))
    nc, names = _NC_CACHE[L]
    zc = np.zeros((D,), np.float32)
    cores = [(x_prompt[0], c_prompt[0], L), (x_sample[0], c_sample[0], Ls), (None, zc, 0), (None, zc, 0),
             (x_prompt[1], c_prompt[1], L), (x_sample[1], c_sample[1], Ls), (None, zc, 0), (None, zc, 0)]
    in_maps = []
    for (x, c, lr) in cores:
        m = _prep_core(None, x, c, lr, L, shared)
        in_maps.append({k: m[k] for k in names})
    res = run_bass_kernel_spmd(nc, in_maps, core_ids=list(range(8)))
    outs = [np.asarray(r["out"], dtype=np.float32) for r in res.results]
    y_prompt = np.stack([outs[0], outs[4]], axis=0)
    y_sample = np.stack([outs[1][:Ls], outs[5][:Ls]], axis=0)
    return (y_prompt, y_sample)
```

```python
from concourse.bass_utils import run_bass_kernel_spmd
import contextlib
import numpy as np
import concourse.bass as bass
import concourse.mybir as mybir

F32 = mybir.dt.float32
BF16 = mybir.dt.bfloat16
AF = mybir.ActivationFunctionType
ALU = mybir.AluOpType
AX = mybir.AxisListType

ENGS = ["pe", "act", "dve", "pool", "sp"]


class Buf:
    def __init__(self, name, t=None):
        self.name = name
        self.t = t
        self.writers = []
        self.readers = []
        self.dsem = None
        self.dcount = 0
        self.war = []
        self.is_psum = False

    def __getitem__(self, idx):
        return self.t[idx]


class Op:
    __slots__ = ("eng", "fn", "waits", "dma", "dsem", "dval", "needed", "mval", "seq")
    _n = [0]

    def __init__(self, eng, fn):
        Op._n[0] += 1
        self.seq = Op._n[0]
        self.eng = eng
        self.fn = fn
        self.waits = []
        self.dma = False
        self.dsem = None
        self.dval = 0
        self.needed = False
        self.mval = 0


class Prog:
    def __init__(self, nc):
        self.nc = nc
        self.stack = contextlib.ExitStack()
        self.ops = {e: [] for e in ENGS}
        self.sems = {}
        self.nbuf = 0
        self.pstack = self.stack
        self.sem_pool = []
        self.phase_bufs = []

    def sem(self, name):
        self.nbuf += 1
        name = "%s_%d" % (name, self.nbuf)
        return self.pstack.enter_context(self.nc.semaphore(name))

    def dsem_get(self, buf):
        if self.sem_pool:
            h, cnt = self.sem_pool.pop()
        else:
            h, cnt = self.sem("dq"), 0
        buf.dsem = h
        buf.dcount = cnt
        self.phase_bufs.append(buf)

    def release_phase_sems(self):
        for b in self.phase_bufs:
            self.sem_pool.append((b.dsem, b.dcount))
            b.dsem = None
        self.phase_bufs = []

    def sbuf(self, name, shape, dt=F32):
        self.nbuf += 1
        name = "%s_%d" % (name, self.nbuf)
        t = self.stack.enter_context(self.nc.sbuf_tensor(name, list(shape), dt))
        return Buf(name, t)

    def psum(self, name, shape, dt=F32):
        self.nbuf += 1
        name = "%s_%d" % (name, self.nbuf)
        t = self.stack.enter_context(self.nc.psum_tensor(name, list(shape), dt))
        b = Buf(name, t)
        b.is_psum = True
        return b

    def dram(self, name, shape, dt=F32, kind=None):
        if kind is None:
            t = self.nc.dram_tensor(name, list(shape), dt)
        else:
            t = self.nc.dram_tensor(name, list(shape), dt, kind=kind)
        return Buf(name, t)

    def _deps(self, op, reads, writes):
        for b in reads:
            for w in b.writers:
                op.waits.append(w)
            if b.is_psum:
                for r in b.readers:
                    if r.eng != op.eng:
                        op.waits.append(r)
        for b in writes:
            if b.readers:
                b.war = _compress(list(b.readers) + list(b.writers))
                op.waits.extend(b.war)
                b.readers = []
                b.writers = [op]
            else:
                op.waits.extend(b.war)
                if op.dma or any(w.dma for w in b.writers):
                    op.waits.extend(b.writers)
                b.writers.append(op)
                if len(b.writers) > 64:
                    b.writers = _compress(b.writers)
        for b in reads:
            b.readers.append(op)
            if len(b.readers) > 64:
                b.readers = _compress(b.readers)

    def op(self, eng, fn, reads=(), writes=()):
        o = Op(eng, fn)
        self._deps(o, reads, writes)
        self.ops[eng].append(o)
        return o

    def dma(self, fn, sb, reads=(), writes=(), eng="sp"):
        o = Op(eng, fn)
        o.dma = True
        if sb.dsem is None:
            self.dsem_get(sb)
        sb.dcount += 16
        o.dsem = sb.dsem
        o.dval = sb.dcount
        self._deps(o, reads, writes)
        self.ops[eng].append(o)
        return o

    def emit(self, final_waits=()):
        nc = self.nc
        esem = {e: self.sem("e_" + e) for e in ENGS}
        for e in ENGS:
            for o in self.ops[e]:
                for w in o.waits:
                    if not w.dma:
                        w.needed = True
        for e in ENGS:
            c = 0
            for o in self.ops[e]:
                if o.needed and not o.dma:
                    c += 1
                    o.mval = c
        engmap = {"pe": "tensor", "act": "scalar", "dve": "vector", "pool": "gpsimd", "sp": "sync"}
        nops = {e: len(self.ops[e]) for e in ENGS}
        nwaits = [0]
        with nc.Block() as block:
            def make(e):
                def body(eng):
                    waited = {}
                    for o in self.ops[e]:
                        req = {}
                        for w in o.waits:
                            if w.dma:
                                key, val = w.dsem, w.dval
                            else:
                                if w.eng == e and e == "pe":
                                    continue
                                if w is o:
                                    continue
                                key, val = esem[w.eng], w.mval
                            if req.get(id(key), (None, 0))[1] < val:
                                req[id(key)] = (key, val)
                        for k, (key, val) in req.items():
                            if waited.get(k, 0) < val:
                                eng.wait_ge(key, val)
                                waited[k] = val
                                nwaits[0] += 1
                        inst = o.fn(eng)
                        if o.dma:
                            inst.then_inc(o.dsem, 16)
                        elif o.needed:
                            inst.then_inc(esem[e], 1)
                    if e == "sp":
                        req = {}
                        for w in final_waits:
                            if w.dma:
                                key, val = w.dsem, w.dval
                            else:
                                key, val = esem[w.eng], w.mval
                            if req.get(id(key), (None, 0))[1] < val:
                                req[id(key)] = (key, val)
                        for k, (key, val) in req.items():
                            eng.wait_ge(key, val)
                return body
            for e in ENGS:
                if not self.ops[e] and e != "sp":
                    continue
                getattr(block, engmap[e])(make(e))
        self.stats = dict(nops=nops, nwaits=nwaits[0])
        return self.stats

    def close(self):
        self.stack.close()


def _compress(toks):
    best = {}
    for o in toks:
        key = ("d", id(o.dsem)) if o.dma else ("e", o.eng)
        cur = best.get(key)
        if cur is None:
            best[key] = o
        elif o.dma:
            if o.dval > cur.dval:
                best[key] = o
        elif o.seq > cur.seq:
            best[key] = o
    return list(best.values())


import contextlib
import numpy as np
import concourse.bass as bass
import concourse.mybir as mybir

D = 2048
KC = 16
DFF = 5632
FC = 44
NMOD = 9
INC = 7200
EPS = 1e-6
T = 512


class MK:
    def __init__(self, L, stages, dump=(), lite=False, as_input=()):
        self.lite = lite
        self.as_input = set(as_input)
        self.L = L
        self.NB = L // T
        self.stages = stages
        self.dump = set(dump)
        nc = bass.Bass("TRN2", target_bir_lowering=False)
        self.nc = nc
        self.P = Prog(nc)
        P = self.P
        self.ext_inputs = []
        def ein(n, s, dt=F32):
            self.ext_inputs.append(n)
            if lite and n.startswith("w_"):
                s = [128, 128]
            return P.dram(n, s, dt, kind="ExternalInput")
        self.x = ein("x", [L, D])
        self.cT = ein("cT", [128, KC])
        self.mask = ein("mask", [1, L])
        self.w_mod = ein("w_mod", [D, NMOD * D])
        self.b_modT = ein("b_modT", [128, NMOD * KC])
        self.normsT = ein("normsT", [128, 4 * KC])
        self.w_ffn1_in = ein("w_ffn1_in", [D, 2 * DFF])
        self.w_ffn1_out = ein("w_ffn1_out", [DFF, D])
        self.w_in = ein("w_in", [D, INC])
        self.w_out = ein("w_out", [D, D])
        self.w_ffn2_in = ein("w_ffn2_in", [D, 2 * DFF])
        self.w_ffn2_out = ein("w_ffn2_out", [DFF, D])
        self.ident_in = ein("ident", [128, 128])
        self.W1I = P.dram("W1I", [2 * FC, 128, KC * 128], BF16)
        self.W1O = P.dram("W1O", [KC, 128, FC * 128], BF16)
        self.W2I = P.dram("W2I", [2 * FC, 128, KC * 128], BF16)
        self.W2O = P.dram("W2O", [KC, 128, FC * 128], BF16)
        self.WOr = P.dram("WOr", [KC, 128, KC * 128], BF16)
        self.X1T = P.dram("X1T", [D, L], F32, kind="ExternalOutput" if "X1T" in self.dump else None)
        self.H2T = P.dram("H2T", [D, L], BF16, kind="ExternalOutput" if "H2T" in self.dump else None)
        self.outs = []
        self.ident = P.sbuf("identS", [128, 128], F32)
        self.ones = P.sbuf("onesS", [128, 128], F32)
        self.modT = P.sbuf("modT", [128, NMOD * KC], F32)
        self.nrm = P.sbuf("nrm", [128, 4 * KC], F32)
        self.AB = P.sbuf("AB", [128, 9 * KC], F32)
        self.epsc = P.sbuf("epsc", [128, 1], F32)
        self.last_tokens = []

    def barrier(self):
        P = self.P
        toks = []
        for e in ENGS:
            if P.ops[e]:
                for o in reversed(P.ops[e]):
                    if not o.dma:
                        toks.append(o)
                        break
        dl = {}
        for e in ENGS:
            for o in P.ops[e]:
                if o.dma:
                    dl[id(o.dsem)] = o
        toks += list(dl.values())
        for e in ENGS:
            o = P.op(e, lambda eng: eng.nop())
            o.waits = list(toks)
        P.release_phase_sems()
        return toks

    def setup(self):
        P = self.P
        P.dma(lambda e: e.dma_start(out=self.ident[:, :], in_=self.ident_in[:, :]), self.ident, writes=[self.ident])
        P.dma(lambda e: e.dma_start(out=self.nrm[:, :], in_=self.normsT[:, :]), self.nrm, writes=[self.nrm])
        P.op("pool", lambda e: e.memset(self.ones[:, :], 1.0), writes=[self.ones])
        P.op("pool", lambda e: e.memset(self.epsc[:, :], EPS), writes=[self.epsc])

    def convert_stationary(self, st, src, K, c0, ncols, dst, f0, tag):
        P = self.P
        kcn = K // 128
        kg = 16 if kcn == 16 else 11
        nkg = kcn // kg
        cb_n = (ncols + 511) // 512
        i = 0
        for cb in range(cb_n):
            cw = min(512, ncols - cb * 512)
            nf = cw // 128
            for g in range(nkg):
                t32 = st["t32"][i % 2]
                t16 = st["t16"][i % 2]
                srcap = src.t[g * kg * 128:(g + 1) * kg * 128, c0 + cb * 512:c0 + cb * 512 + cw].rearrange("(k p) n -> p k n", p=128)
                P.dma(lambda e, t32=t32, srcap=srcap, cw=cw, kg=kg: e.dma_start(out=t32.t[:, 0:kg, 0:cw], in_=srcap), t32, reads=[src], writes=[t32])
                eng = "dve" if i % 2 == 0 else "act"
                def cast(e, t32=t32, t16=t16, cw=cw, nf=nf, kg=kg, eng=eng):
                    o = t16.t[:, 0:nf * kg * 128].rearrange("p (f k n) -> p k f n", f=nf, k=kg)
                    i_ = t32.t[:, 0:kg, 0:cw].rearrange("p k (f n) -> p k f n", f=nf)
                    if eng == "dve":
                        return e.tensor_copy(out=o, in_=i_)
                    return e.activation(out=o, in_=i_, func=AF.Copy)
                P.op(eng, cast, reads=[t32], writes=[t16])
                dstap = dst.t[f0 + cb * 4:f0 + cb * 4 + nf, :, g * kg * 128:(g + 1) * kg * 128].rearrange("f p x -> p f x")
                P.dma(lambda e, t16=t16, dstap=dstap, nf=nf, kg=kg: e.dma_start(out=dstap, in_=t16.t[:, 0:nf * kg * 128].rearrange("p (f x) -> p f x", f=nf)), t16, reads=[t16], writes=[dst], eng="pool")
                i += 1

    def p0(self):
        P = self.P
        with contextlib.ExitStack() as es:
            old = P.stack
            P.stack = es
            st = dict(t32=[P.sbuf("cv32_%d" % i, [128, 16, 512], F32) for i in range(2)],
                      t16=[P.sbuf("cv16_%d" % i, [128, 16 * 512], BF16) for i in range(2)])
            self.convert_stationary(st, self.w_ffn1_in, D, 0, 2 * DFF, self.W1I, 0, "w1i")
            self.convert_stationary(st, self.w_ffn1_out, DFF, 0, D, self.W1O, 0, "w1o")
            if "p7" in self.stages:
                self.convert_stationary(st, self.w_ffn2_in, D, 0, 2 * DFF, self.W2I, 0, "w2i")
                self.convert_stationary(st, self.w_ffn2_out, DFF, 0, D, self.W2O, 0, "w2o")
                self.convert_stationary(st, self.w_out, D, 0, D, self.WOr, 0, "wo")
            self.barrier()
            P.stack = old

    def p1(self):
        P = self.P
        with contextlib.ExitStack() as es:
            old = P.stack
            P.stack = es
            cs = P.sbuf("cS", [128, KC], F32)
            sc = P.sbuf("scS", [128, KC], F32)
            bm = P.sbuf("bmS", [128, NMOD * KC], F32)
            wm = [P.sbuf("wm%d" % i, [128, KC, 512], F32) for i in range(2)]
            ps = P.psum("ps_mod", [128, 512], F32)
            P.dma(lambda e: e.dma_start(out=cs[:, :], in_=self.cT[:, :]), cs, writes=[cs])
            P.dma(lambda e: e.dma_start(out=bm[:, :], in_=self.b_modT[:, :]), bm, writes=[bm])
            P.op("act", lambda e: e.activation(out=sc[:, :], in_=cs[:, :], func=AF.Silu), reads=[cs], writes=[sc])
            ng = NMOD * D // 512
            for g in range(ng):
                w = wm[g % 2]
                srcap = self.w_mod.t[:, g * 512:(g + 1) * 512].rearrange("(k p) n -> p k n", p=128)
                P.dma(lambda e, w=w, srcap=srcap: e.dma_start(out=w.t[:, :, :], in_=srcap), w, writes=[w])
                for jj in range(4):
                    j = 4 * g + jj
                    for kc in range(KC):
                        P.op("pe", lambda e, w=w, jj=jj, kc=kc, j=j: e.matmul(ps.t[:, j:j + 1], lhsT=w.t[:, kc, jj * 128:(jj + 1) * 128], rhs=sc.t[:, kc:kc + 1], start=(kc == 0), stop=(kc == KC - 1)),
                             reads=[w, sc], writes=[ps])
            P.op("dve", lambda e: e.tensor_tensor(out=self.modT[:, :], in0=ps.t[:, 0:NMOD * KC], in1=bm[:, :], op=ALU.add), reads=[ps, bm], writes=[self.modT])
            for i in range(3):
                sh = self.modT.t[:, (3 * i) * KC:(3 * i + 1) * KC]
                scl = self.modT.t[:, (3 * i + 1) * KC:(3 * i + 2) * KC]
                gt = self.modT.t[:, (3 * i + 2) * KC:(3 * i + 3) * KC]
                nr = self.nrm.t[:, i * KC:(i + 1) * KC]
                A = self.AB.t[:, (3 * i) * KC:(3 * i + 1) * KC]
                Bv = self.AB.t[:, (3 * i + 1) * KC:(3 * i + 2) * KC]
                G = self.AB.t[:, (3 * i + 2) * KC:(3 * i + 3) * KC]
                P.op("dve", lambda e, A=A, scl=scl, nr=nr: e.scalar_tensor_tensor(out=A, in0=scl, scalar=1.0, in1=nr, op0=ALU.add, op1=ALU.mult), reads=[self.modT, self.nrm], writes=[self.AB])
                P.op("dve", lambda e, Bv=Bv, sh=sh: e.tensor_copy(out=Bv, in_=sh), reads=[self.modT], writes=[self.AB])
                gs = 1.0 if i == 1 else 0.5
                P.op("dve", lambda e, G=G, gt=gt, gs=gs: e.tensor_scalar(out=G, in0=gt, scalar1=gs, scalar2=None, op0=ALU.mult), reads=[self.modT], writes=[self.AB])
            self.barrier()
            P.stack = old

    def alloc_ffn(self):
        P = self.P
        s = {}
        s["xtok"] = [P.sbuf("xtok%d" % i, [128, D], F32) for i in range(2)]
        s["xT"] = P.sbuf("xT", [128, KC, T], F32)
        s["hT"] = P.sbuf("hT", [128, KC, T], BF16)
        s["aT"] = P.sbuf("aT", [128, FC, T], BF16)
        s["sq"] = [P.sbuf("sq%d" % i, [128, T], F32) for i in range(2)]
        s["tmp"] = [P.sbuf("tmp%d" % i, [128, T], F32) for i in range(2)]
        s["rstd"] = P.sbuf("rstd", [128, T], F32)
        s["sg"] = [P.sbuf("sg%d" % i, [128, T], F32) for i in range(2)]
        s["wi"] = [P.sbuf("wi%d" % i, [128, KC * 128], BF16) for i in range(8)]
        s["wo"] = [P.sbuf("wo%d" % i, [128, FC * 128], BF16) for i in range(2)]
        s["mk"] = P.sbuf("mk", [128, T], F32)
        s["pt"] = [P.psum("pt%d" % i, [128, T], F32) for i in range(2)]
        s["pg"] = [P.psum("pg%d" % i, [128, T], F32) for i in range(2)]
        s["pu"] = [P.psum("pu%d" % i, [128, T], F32) for i in range(2)]
        s["pss"] = P.psum("pss", [128, T], F32)
        s["wi_n"] = 0
        s["wo_n"] = 0
        s["pn"] = 0
        return s

    def load_xT_from_tokens(self, s, xd, b):
        P = self.P
        xT = s["xT"]
        for j in range(T // 128):
            xt = s["xtok"][j % 2]
            r0 = b * T + j * 128
            P.dma(lambda e, xt=xt, r0=r0: e.dma_start(out=xt.t[:, :], in_=xd.t[r0:r0 + 128, :]), xt, reads=[xd], writes=[xt])
            for cg in range(KC // 4):
                pt = s["pt"][s["pn"] % 2]
                s["pn"] += 1
                for ci in range(4):
                    c = cg * 4 + ci
                    P.op("pe", lambda e, pt=pt, xt=xt, c=c, ci=ci: e.transpose(out=pt.t[:, ci * 128:(ci + 1) * 128], in_=xt.t[:, c * 128:(c + 1) * 128], identity=self.ident.t[:, :]),
                         reads=[xt, self.ident], writes=[pt])
                eng = "dve" if (cg % 2 == 0) else "act"
                def ev(e, pt=pt, cg=cg, j=j, eng=eng):
                    o = xT.t[:, cg * 4:(cg + 1) * 4, j * 128:(j + 1) * 128]
                    i_ = pt.t[:, :].rearrange("p (c n) -> p c n", c=4)
                    if eng == "dve":
                        return e.tensor_copy(out=o, in_=i_)
                    return e.activation(out=o, in_=i_, func=AF.Copy)
                P.op(eng, ev, reads=[pt], writes=[xT])

    def rms_stats(self, s):
        P = self.P
        xT = s["xT"]
        pss = s["pss"]
        for c in range(KC):
            sq = s["sq"][c % 2]
            P.op("act", lambda e, sq=sq, c=c: e.activation(out=sq.t[:, :], in_=xT.t[:, c, :], func=AF.Square), reads=[xT], writes=[sq])
            P.op("pe", lambda e, sq=sq, c=c: e.matmul(pss.t[:, :], lhsT=self.ones.t[:, :], rhs=sq.t[:, :], start=(c == 0), stop=(c == KC - 1)), reads=[sq, self.ones], writes=[pss])
        rstd = s["rstd"]
        P.op("act", lambda e: e.activation(out=rstd.t[:, :], in_=pss.t[:, :], func=AF.Sqrt, bias=self.epsc.t[:, 0:1], scale=1.0 / D), reads=[pss, self.epsc], writes=[rstd])
        P.op("dve", lambda e: e.reciprocal(out=rstd.t[:, :], in_=rstd.t[:, :]), reads=[rstd], writes=[rstd])

    def norm_affine(self, s, i, mask_b=None):
        P = self.P
        xT, hT, rstd = s["xT"], s["hT"], s["rstd"]
        for c in range(KC):
            tmp = s["tmp"][c % 2]
            Ac = self.AB.t[:, 3 * i * KC + c:3 * i * KC + c + 1]
            Bc = self.AB.t[:, (3 * i + 1) * KC + c:(3 * i + 1) * KC + c + 1]
            P.op("dve", lambda e, tmp=tmp, c=c, Ac=Ac: e.scalar_tensor_tensor(out=tmp.t[:, :], in0=xT.t[:, c, :], scalar=Ac, in1=rstd.t[:, :], op0=ALU.mult, op1=ALU.mult),
                 reads=[xT, rstd, self.AB], writes=[tmp])
            if mask_b is None:
                P.op("act", lambda e, tmp=tmp, c=c, Bc=Bc: e.activation(out=hT.t[:, c, :], in_=tmp.t[:, :], func=AF.Identity, bias=Bc, scale=1.0), reads=[tmp, self.AB], writes=[hT])
            else:
                P.op("dve", lambda e, tmp=tmp, c=c, Bc=Bc: e.scalar_tensor_tensor(out=hT.t[:, c, :], in0=tmp.t[:, :], scalar=Bc, in1=mask_b.t[:, :], op0=ALU.add, op1=ALU.mult),
                     reads=[tmp, self.AB, mask_b], writes=[hT])

    def ffn(self, s, WI, WO, gi):
        P = self.P
        xT, hT, aT = s["xT"], s["hT"], s["aT"]
        for f in range(FC):
            wg = s["wi"][s["wi_n"] % 8]
            wu = s["wi"][(s["wi_n"] + 1) % 8]
            s["wi_n"] += 2
            P.dma(lambda e, wg=wg, f=f: e.dma_start(out=wg.t[:, :], in_=WI.t[f, :, :]), wg, reads=[WI], writes=[wg])
            P.dma(lambda e, wu=wu, f=f: e.dma_start(out=wu.t[:, :], in_=WI.t[FC + f, :, :]), wu, reads=[WI], writes=[wu])
            pg = s["pg"][f % 2]
            pu = s["pu"][f % 2]
            for kc in range(KC):
                P.op("pe", lambda e, pg=pg, wg=wg, kc=kc: e.matmul(pg.t[:, :], lhsT=wg.t[:, kc * 128:(kc + 1) * 128], rhs=hT.t[:, kc, :], start=(kc == 0), stop=(kc == KC - 1)), reads=[wg, hT], writes=[pg])
            for kc in range(KC):
                P.op("pe", lambda e, pu=pu, wu=wu, kc=kc: e.matmul(pu.t[:, :], lhsT=wu.t[:, kc * 128:(kc + 1) * 128], rhs=hT.t[:, kc, :], start=(kc == 0), stop=(kc == KC - 1)), reads=[wu, hT], writes=[pu])
            sg = s["sg"][f % 2]
            P.op("act", lambda e, sg=sg, pg=pg: e.activation(out=sg.t[:, :], in_=pg.t[:, :], func=AF.Silu), reads=[pg], writes=[sg])
            P.op("dve", lambda e, sg=sg, pu=pu, f=f: e.tensor_tensor(out=aT.t[:, f, :], in0=sg.t[:, :], in1=pu.t[:, :], op=ALU.mult), reads=[sg, pu], writes=[aT])
        for dc in range(KC):
            wo = s["wo"][s["wo_n"] % 2]
            s["wo_n"] += 1
            P.dma(lambda e, wo=wo, dc=dc: e.dma_start(out=wo.t[:, :], in_=WO.t[dc, :, :]), wo, reads=[WO], writes=[wo])
            py = s["pt"][s["pn"] % 2]
            s["pn"] += 1
            for f in range(FC):
                P.op("pe", lambda e, py=py, wo=wo, f=f: e.matmul(py.t[:, :], lhsT=wo.t[:, f * 128:(f + 1) * 128], rhs=aT.t[:, f, :], start=(f == 0), stop=(f == FC - 1)), reads=[wo, aT], writes=[py])
            Gc = self.AB.t[:, (3 * gi + 2) * KC + dc:(3 * gi + 2) * KC + dc + 1]
            P.op("dve", lambda e, py=py, dc=dc, Gc=Gc: e.scalar_tensor_tensor(out=xT.t[:, dc, :], in0=py.t[:, :], scalar=Gc, in1=xT.t[:, dc, :], op0=ALU.mult, op1=ALU.add),
                 reads=[py, xT, self.AB], writes=[xT])

    def p2(self):
        P = self.P
        with contextlib.ExitStack() as es:
            old = P.stack
            P.stack = es
            s = self.alloc_ffn()
            for b in range(self.NB):
                self.load_xT_from_tokens(s, self.x, b)
                self.rms_stats(s)
                self.norm_affine(s, 0)
                self.ffn(s, self.W1I, self.W1O, 0)
                xT = s["xT"]
                dst = self.X1T.t[:, b * T:(b + 1) * T].rearrange("(c p) t -> p c t", p=128)
                P.dma(lambda e, dst=dst: e.dma_start(out=dst, in_=xT.t[:, :, :]), xT, reads=[xT], writes=[self.X1T], eng="pool")
                mk = s["mk"]
                P.dma(lambda e, b=b: e.dma_start(out=mk.t[:, :], in_=bcast_rows(self.mask.t, b * T, T)), mk, reads=[self.mask], writes=[mk])
                self.rms_stats(s)
                self.norm_affine(s, 1, mask_b=mk)
                hT = s["hT"]
                dsth = self.H2T.t[:, b * T:(b + 1) * T].rearrange("(c p) t -> p c t", p=128)
                P.dma(lambda e, dsth=dsth: e.dma_start(out=dsth, in_=hT.t[:, :, :]), hT, reads=[hT], writes=[self.H2T], eng="pool")
            self.last_tokens = self.barrier()
            P.stack = old

    def build(self):
        self.setup()
        if "p0" in self.stages:
            self.p0()
        if "p1" in self.stages:
            self.p1()
        if "p2" in self.stages:
            self.p2()
        toks = self.barrier()
        st = self.P.emit(final_waits=toks)
        print("ops", st)
        return self.nc


def bcast_rows(t, c0, n):
    ap = t[0:1, c0:c0 + n]
    return bass.AP(ap.tensor, ap.offset, [[0, 128], [1, n]])


AW = 1024
NH = 8
DH = 128


def t5_bucket_np(rel):
    half = 16
    max_exact = 8
    n = np.abs(rel)
    large = max_exact + (np.log(np.maximum(n, 1) / max_exact) / np.log(1024 / max_exact) * (half - max_exact)).astype(np.int32)
    large = np.minimum(large, half - 1)
    return (np.where(rel > 0, half, 0) + np.where(n < max_exact, n, large)).astype(np.int32)


DILS = (1, 4, 16)


def make_onehot():
    oh = np.zeros((3, 33, 384), np.float32)
    for p, dil in enumerate(DILS):
        for m in range(383):
            rel = m - 191
            if abs(rel) <= 64:
                oh[p, int(t5_bucket_np(np.array(rel * dil))), m] = 1.0
            else:
                oh[p, 32, m] = -1e30
        oh[p, 32, 383] = -1e30
    return oh


class MK2(MK):
    def __init__(self, L, stages, dump=(), lite=False, as_input=()):
        super().__init__(L, stages, dump, lite, as_input)
        P = self.P
        def ein(n, s, dt=F32):
            self.ext_inputs.append(n)
            return P.dram(n, s, dt, kind="ExternalInput")
        dmp = lambda n: ("ExternalOutput" if n in self.dump else None)
        self.rel_bias = ein("rel_bias", [32, NH])
        self.onehot = ein("onehot", [3, 33, 384])
        self.nattn_in = ein("norm_attn_out", [1, AW])
        self.WA = P.dram("WA", [8, 128, KC * 512], BF16)
        self.WD = P.dram("WD", [24, 128, KC * 128], BF16)
        self.QKV = P.dram("QKV", [L, 3 * AW], BF16, kind=dmp("QKV"))
        self.Z = P.dram("Z", [L, AW], F32, kind=dmp("Z"))
        self.DQKVT = P.dram("DQKVT", [3 * AW, L], F32, kind=dmp("DQKVT"))
        self.BAT = P.dram("BAT", [32, L], F32, kind=dmp("BAT"))
        self.AO = P.dram("AO", [3, L, AW], F32, kind=dmp("AO"))
        self.AM = P.dram("AM", [3, L, 16], F32, kind=dmp("AM"))
        self.YT = P.dram("YT", [D, L], BF16, kind=dmp("YT"))
        self.NEGM = P.dram("NEGM", [1, L], F32)
        self.BIASR = P.dram("BIASR", [3, NH, 384], F32)
        self.BIAS2 = P.dram("BIAS2", [3, NH, 128 * 385], F32)
        self.identb = P.sbuf("identb", [128, 128], BF16)
        self.WB = P.sbuf("WB", [128, KC * 32], BF16)

    def setup(self):
        super().setup()
        P = self.P
        P.op("dve", lambda e: e.tensor_copy(out=self.identb[:, :], in_=self.ident[:, :]), reads=[self.ident], writes=[self.identb])

    def convert_moving(self, st, src, c0, ngroups, dst, g0):
        P = self.P
        for g in range(ngroups):
            t32 = st["t32"][g % 2]
            t16 = st["t16"][g % 2]
            srcap = src.t[:, c0 + g * 512:c0 + (g + 1) * 512].rearrange("(k p) n -> p k n", p=128)
            P.dma(lambda e, t32=t32, srcap=srcap: e.dma_start(out=t32.t[:, :, :], in_=srcap), t32, reads=[src], writes=[t32])
            eng = "dve" if g % 2 == 0 else "act"
            def cast(e, t32=t32, t16=t16, eng=eng):
                o = t16.t[:, :]
                i_ = t32.t[:, :, :].rearrange("p k n -> p (k n)")
                if eng == "dve":
                    return e.tensor_copy(out=o, in_=i_)
                return e.activation(out=o, in_=i_, func=AF.Copy)
            P.op(eng, cast, reads=[t32], writes=[t16])
            P.dma(lambda e, t16=t16, g=g: e.dma_start(out=dst.t[g0 + g, :, :], in_=t16.t[:, :]), t16, reads=[t16], writes=[dst], eng="pool")

    def p0(self):
        P = self.P
        super().p0()
        if "p3" not in self.stages:
            return
        with contextlib.ExitStack() as es:
            old = P.stack
            P.stack = es
            st = dict(t32=[P.sbuf("cw32_%d" % i, [128, 16, 512], F32) for i in range(2)],
                      t16=[P.sbuf("cw16_%d" % i, [128, 16 * 512], BF16) for i in range(2)])
            self.convert_moving(st, self.w_in, 0, 6, self.WA, 0)
            self.convert_moving(st, self.w_in, 6 * AW, 2, self.WA, 6)
            self.convert_stationary(st, self.w_in, D, 3 * AW, 3 * AW, self.WD, 0, "wd")
            t32 = st["t32"][0]
            srcap = self.w_in.t[:, 7 * AW:7 * AW + 32].rearrange("(k p) n -> p k n", p=128)
            P.dma(lambda e: e.dma_start(out=t32.t[:, :, 0:32], in_=srcap), t32, reads=[self.w_in, t32], writes=[t32])
            P.op("dve", lambda e: e.tensor_copy(out=self.WB.t[:, :].rearrange("p (k n) -> p k n", k=KC), in_=t32.t[:, :, 0:32]), reads=[t32], writes=[self.WB])
            self.barrier()
            P.stack = old

    def p3(self):
        P = self.P
        L = self.L
        with contextlib.ExitStack() as es:
            old = P.stack
            P.stack = es
            hT = [P.sbuf("h2b%d" % i, [128, KC, T], BF16) for i in range(2)]
            wa = [P.sbuf("wa%d" % i, [128, KC * 512], BF16) for i in range(2)]
            wd = [P.sbuf("wd%d" % i, [128, KC * 128], BF16) for i in range(3)]
            tok16 = [P.sbuf("tok16_%d" % i, [128, 512], BF16) for i in range(3)]
            tok32 = [P.sbuf("tok32_%d" % i, [128, 512], F32) for i in range(3)]
            ps = [P.psum("p3ps%d" % i, [128, 512], F32) for i in range(4)]
            n_ps = 0
            n16 = 0
            n32 = 0
            nwa = 0
            nwd = 0
            for b in range(self.NB):
                h = hT[b % 2]
                src = self.H2T.t[:, b * T:(b + 1) * T].rearrange("(c p) t -> p c t", p=128)
                P.dma(lambda e, h=h, src=src: e.dma_start(out=h.t[:, :, :], in_=src), h, reads=[self.H2T], writes=[h])
                for g in range(8):
                    w = wa[nwa % 2]
                    nwa += 1
                    P.dma(lambda e, w=w, g=g: e.dma_start(out=w.t[:, :], in_=self.WA.t[g, :, :]), w, reads=[self.WA], writes=[w])
                    for j in range(4):
                        pp = ps[n_ps % 4]
                        n_ps += 1
                        for kc in range(KC):
                            P.op("pe", lambda e, pp=pp, h=h, w=w, kc=kc, j=j: e.matmul(pp.t[:, :], lhsT=h.t[:, kc, j * 128:(j + 1) * 128], rhs=w.t[:, kc * 512:(kc + 1) * 512], start=(kc == 0), stop=(kc == KC - 1)),
                                 reads=[h, w], writes=[pp])
                        r0 = b * T + j * 128
                        if g < 6:
                            o = tok16[n16 % 3]
                            n16 += 1
                            eng = "act" if n16 % 2 else "dve"
                            if eng == "act":
                                P.op("act", lambda e, o=o, pp=pp: e.activation(out=o.t[:, :], in_=pp.t[:, :], func=AF.Copy), reads=[pp], writes=[o])
                            else:
                                P.op("dve", lambda e, o=o, pp=pp: e.tensor_copy(out=o.t[:, :], in_=pp.t[:, :]), reads=[pp], writes=[o])
                            P.dma(lambda e, o=o, r0=r0, g=g: e.dma_start(out=self.QKV.t[r0:r0 + 128, g * 512:(g + 1) * 512], in_=o.t[:, :]), o, reads=[o], writes=[self.QKV], eng="pool")
                        else:
                            o = tok32[n32 % 3]
                            n32 += 1
                            P.op("act", lambda e, o=o, pp=pp: e.activation(out=o.t[:, :], in_=pp.t[:, :], func=AF.Copy), reads=[pp], writes=[o])
                            P.dma(lambda e, o=o, r0=r0, g=g: e.dma_start(out=self.Z.t[r0:r0 + 128, (g - 6) * 512:(g - 5) * 512], in_=o.t[:, :]), o, reads=[o], writes=[self.Z], eng="pool")
                for f in range(24):
                    w = wd[nwd % 3]
                    nwd += 1
                    P.dma(lambda e, w=w, f=f: e.dma_start(out=w.t[:, :], in_=self.WD.t[f, :, :]), w, reads=[self.WD], writes=[w])
                    pp = ps[n_ps % 4]
                    n_ps += 1
                    for kc in range(KC):
                        P.op("pe", lambda e, pp=pp, h=h, w=w, kc=kc: e.matmul(pp.t[:, :], lhsT=w.t[:, kc * 128:(kc + 1) * 128], rhs=h.t[:, kc, :], start=(kc == 0), stop=(kc == KC - 1)),
                             reads=[h, w], writes=[pp])
                    o = tok32[n32 % 3]
                    n32 += 1
                    P.op("dve", lambda e, o=o, pp=pp: e.tensor_copy(out=o.t[:, :], in_=pp.t[:, :]), reads=[pp], writes=[o])
                    P.dma(lambda e, o=o, f=f, b=b: e.dma_start(out=self.DQKVT.t[f * 128:(f + 1) * 128, b * T:(b + 1) * T], in_=o.t[:, :]), o, reads=[o], writes=[self.DQKVT], eng="pool")
                pp = ps[n_ps % 4]
                n_ps += 1
                for kc in range(KC):
                    P.op("pe", lambda e, pp=pp, h=h, kc=kc: e.matmul(pp.t[0:32, :], lhsT=self.WB.t[:, kc * 32:(kc + 1) * 32], rhs=h.t[:, kc, :], start=(kc == 0), stop=(kc == KC - 1)),
                         reads=[h, self.WB], writes=[pp])
                o = tok32[n32 % 3]
                n32 += 1
                P.op("dve", lambda e, o=o, pp=pp: e.tensor_copy(out=o.t[0:32, :], in_=pp.t[0:32, :]), reads=[pp], writes=[o])
                P.dma(lambda e, o=o, b=b: e.dma_start(out=self.BAT.t[:, b * T:(b + 1) * T], in_=o.t[0:32, :]), o, reads=[o], writes=[self.BAT], eng="pool")
            self.barrier()
            P.stack = old

    def p4(self):
        P = self.P
        L = self.L
        SCALE = DH ** -0.5
        with contextlib.ExitStack() as es:
            old = P.stack
            P.stack = es
            rba = P.sbuf("rba", [33, NH], F32)
            ohs = P.sbuf("ohs", [33, 384], F32)
            brow = P.sbuf("brow", [NH, 384], F32)
            biasT = [P.sbuf("biasT%d" % p, [128, NH, 256], F32) for p in range(3)]
            psb = P.psum("psb", [128, 512], F32)
            P.op("pool", lambda e: e.memset(rba.t[:, :], 1.0), writes=[rba])
            P.dma(lambda e: e.dma_start(out=rba.t[0:32, :], in_=self.rel_bias.t[:, :]), rba, reads=[rba], writes=[rba])
            for p in range(3):
                P.dma(lambda e, p=p: e.dma_start(out=ohs.t[:, :], in_=self.onehot.t[p, :, :]), ohs, writes=[ohs])
                P.op("pe", lambda e: e.matmul(psb.t[0:NH, 0:384], lhsT=rba.t[:, :], rhs=ohs.t[:, :], start=True, stop=True), reads=[rba, ohs], writes=[psb])
                P.op("dve", lambda e: e.tensor_copy(out=brow.t[:, :], in_=psb.t[0:NH, 0:384]), reads=[psb], writes=[brow])
                P.dma(lambda e, p=p: e.dma_start(out=self.BIASR.t[p, :, :], in_=brow.t[:, :]), brow, reads=[brow], writes=[self.BIASR], eng="pool")
                for h in range(NH):
                    base = self.BIASR.t[p, h:h + 1, 0:1]
                    src0 = bass.AP(base.tensor, base.offset, [[0, 128], [1, 384]])
                    b2 = self.BIAS2.t[p, h:h + 1, 0:1]
                    dst0 = bass.AP(b2.tensor, b2.offset, [[385, 128], [1, 384]])
                    P.dma(lambda e, src0=src0, dst0=dst0: e.dma_start(out=dst0, in_=src0), self.BIAS2, reads=[self.BIASR], writes=[self.BIAS2])
                    src = bass.AP(b2.tensor, b2.offset + 127, [[384, 128], [1, 256]])
                    P.dma(lambda e, p=p, h=h, src=src: e.dma_start(out=biasT[p].t[:, h, :], in_=src), biasT[p], reads=[self.BIAS2], writes=[biasT[p]])
            MRW = min(L, 2048)
            mrow = P.sbuf("mrow", [1, MRW], F32)
            for mi in range(L // MRW):
                P.dma(lambda e, mi=mi: e.dma_start(out=mrow.t[:, :], in_=self.mask.t[:, mi * MRW:(mi + 1) * MRW]), mrow, reads=[self.mask], writes=[mrow])
                P.op("dve", lambda e: e.tensor_scalar(out=mrow.t[:, :], in0=mrow.t[:, :], scalar1=-1.0, scalar2=1e30, op0=ALU.add, op1=ALU.mult), reads=[mrow], writes=[mrow])
                P.dma(lambda e, mi=mi: e.dma_start(out=self.NEGM.t[:, mi * MRW:(mi + 1) * MRW], in_=mrow.t[:, :]), mrow, reads=[mrow], writes=[self.NEGM], eng="pool")
            ones1 = P.sbuf("ones1", [1, 128], F32)
            P.op("pool", lambda e: e.memset(ones1.t[:, :], 1.0), writes=[ones1])
            NBUF = 3
            qt = [P.sbuf("qt%d" % i, [128, AW], BF16) for i in range(NBUF)]
            kt = [[P.sbuf("kt%d_%d" % (i, a), [128, AW], BF16) for a in range(2)] for i in range(NBUF)]
            vt = [[P.sbuf("vt%d_%d" % (i, a), [128, AW], BF16) for a in range(2)] for i in range(NBUF)]
            kb = [P.sbuf("kb%d" % i, [1, 256], F32) for i in range(NBUF)]
            qT_l = [P.sbuf("qTs%d" % i, [128, 4, 128], BF16) for i in range(2)]
            kT_l = [P.sbuf("kTs%d" % i, [128, 4, 256], BF16) for i in range(2)]
            ssb_l = [P.sbuf("ssb%d" % i, [128, 4, 256], F32) for i in range(2)]
            psb16_l = [P.sbuf("p16_%d" % i, [128, 4, 256], BF16) for i in range(2)]
            pT_l = [P.sbuf("pTs%d" % i, [128, 4, 2, 128], BF16) for i in range(2)]
            nmx_l = [P.sbuf("nmx%d" % i, [128, 4], F32) for i in range(2)]
            osb = [P.sbuf("osb%d" % i, [128, AW], F32) for i in range(2)]
            stt = [P.sbuf("stt%d" % i, [128, 16], F32) for i in range(2)]
            p_q = P.psum("p_q", [128, 4 * 128], BF16)
            p_k = P.psum("p_k", [128, 4 * 256], BF16)
            p_s = P.psum("p_s", [128, 4 * 256], F32)
            p_p = P.psum("p_p", [128, 8 * 128], BF16)
            p_o = P.psum("p_o", [128, 4 * 128], F32)
            it = 0
            for p, dil in enumerate(DILS):
                sub = L // dil
                ntile = sub // 128
                for r in range(dil):
                    for m in range(ntile):
                        bi = it % NBUF
                        it += 1
                        Q, KA, KB_, VA, VB, kbr = qt[bi], kt[bi][0], kt[bi][1], vt[bi][0], vt[bi][1], kb[bi]
                        os_, st_ = osb[it % 2], stt[it % 2]
                        def rows(j0, n):
                            return slice(r + dil * j0, r + dil * (j0 + n - 1) + 1, dil)
                        P.dma(lambda e, Q=Q, sl=rows(128 * m, 128): e.dma_start(out=Q.t[:, :], in_=self.QKV.t[sl, 0:AW]), Q, reads=[self.QKV], writes=[Q])
                        first = (m == 0)
                        last = (m == ntile - 1)
                        if first or last:
                            P.op("pool", lambda e, kbr=kbr: e.memset(kbr.t[:, :], -1e30), reads=[kbr], writes=[kbr])
                        if first:
                            P.op("pool", lambda e, KA=KA: e.memset(KA.t[0:64, :], 0.0), reads=[KA], writes=[KA])
                            P.op("pool", lambda e, VA=VA: e.memset(VA.t[0:64, :], 0.0), reads=[VA], writes=[VA])
                            P.dma(lambda e, KA=KA, sl=rows(0, 64): e.dma_start(out=KA.t[64:128, :], in_=self.QKV.t[sl, AW:2 * AW]), KA, reads=[self.QKV, KA], writes=[KA])
                            P.dma(lambda e, VA=VA, sl=rows(0, 64): e.dma_start(out=VA.t[64:128, :], in_=self.QKV.t[sl, 2 * AW:3 * AW]), VA, reads=[self.QKV, VA], writes=[VA])
                        else:
                            P.dma(lambda e, KA=KA, sl=rows(128 * m - 64, 128): e.dma_start(out=KA.t[:, :], in_=self.QKV.t[sl, AW:2 * AW]), KA, reads=[self.QKV], writes=[KA])
                            P.dma(lambda e, VA=VA, sl=rows(128 * m - 64, 128): e.dma_start(out=VA.t[:, :], in_=self.QKV.t[sl, 2 * AW:3 * AW]), VA, reads=[self.QKV], writes=[VA])
                        if last:
                            P.op("pool", lambda e, KB_=KB_: e.memset(KB_.t[64:128, :], 0.0), reads=[KB_], writes=[KB_])
                            P.op("pool", lambda e, VB=VB: e.memset(VB.t[64:128, :], 0.0), reads=[VB], writes=[VB])
                            P.dma(lambda e, KB_=KB_, sl=rows(128 * m + 64, 64): e.dma_start(out=KB_.t[0:64, :], in_=self.QKV.t[sl, AW:2 * AW]), KB_, reads=[self.QKV, KB_], writes=[KB_])
                            P.dma(lambda e, VB=VB, sl=rows(128 * m + 64, 64): e.dma_start(out=VB.t[0:64, :], in_=self.QKV.t[sl, 2 * AW:3 * AW]), VB, reads=[self.QKV, VB], writes=[VB])
                        else:
                            P.dma(lambda e, KB_=KB_, sl=rows(128 * m + 64, 128): e.dma_start(out=KB_.t[:, :], in_=self.QKV.t[sl, AW:2 * AW]), KB_, reads=[self.QKV], writes=[KB_])
                            P.dma(lambda e, VB=VB, sl=rows(128 * m + 64, 128): e.dma_start(out=VB.t[:, :], in_=self.QKV.t[sl, 2 * AW:3 * AW]), VB, reads=[self.QKV], writes=[VB])
                        j_lo = max(128 * m - 64, 0)
                        j_hi = min(128 * m + 192, sub)
                        k_lo = j_lo - (128 * m - 64)
                        nk = j_hi - j_lo
                        base = self.NEGM.t[0:1, 0:1]
                        nsrc = bass.AP(base.tensor, base.offset + r + dil * j_lo, [[0, 1], [dil, nk]])
                        P.dma(lambda e, kbr=kbr, nsrc=nsrc, k_lo=k_lo, nk=nk: e.dma_start(out=kbr.t[0:1, k_lo:k_lo + nk], in_=nsrc, allow_slow_non_contiguous=True), kbr, reads=[self.NEGM, kbr], writes=[kbr])
                        for hg in range(2):
                            qT, kT, ssb, psb16, pT, nmx = qT_l[hg], kT_l[hg], ssb_l[hg], psb16_l[hg], pT_l[hg], nmx_l[hg]
                            for hl in range(4):
                                h = hg * 4 + hl
                                P.op("pe", lambda e, qT=qT, kT=kT, ssb=ssb, psb16=psb16, pT=pT, nmx=nmx, Q=Q, h=h, hl=hl: e.transpose(out=p_q.t[:, hl * 128:(hl + 1) * 128], in_=Q.t[:, h * 128:(h + 1) * 128], identity=self.identb.t[:, :]), reads=[Q, self.identb], writes=[p_q])
                                P.op("pe", lambda e, qT=qT, kT=kT, ssb=ssb, psb16=psb16, pT=pT, nmx=nmx, KA=KA, h=h, hl=hl: e.transpose(out=p_k.t[:, hl * 256:hl * 256 + 128], in_=KA.t[:, h * 128:(h + 1) * 128], identity=self.identb.t[:, :]), reads=[KA, self.identb], writes=[p_k])
                                P.op("pe", lambda e, qT=qT, kT=kT, ssb=ssb, psb16=psb16, pT=pT, nmx=nmx, KB_=KB_, h=h, hl=hl: e.transpose(out=p_k.t[:, hl * 256 + 128:hl * 256 + 256], in_=KB_.t[:, h * 128:(h + 1) * 128], identity=self.identb.t[:, :]), reads=[KB_, self.identb], writes=[p_k])
                            P.op("act", lambda e, qT=qT, kT=kT, ssb=ssb, psb16=psb16, pT=pT, nmx=nmx: e.activation(out=qT.t[:, :, :].rearrange("p a b -> p (a b)"), in_=p_q.t[:, :], func=AF.Copy), reads=[p_q], writes=[qT])
                            P.op("dve", lambda e, qT=qT, kT=kT, ssb=ssb, psb16=psb16, pT=pT, nmx=nmx: e.tensor_copy(out=kT.t[:, :, :].rearrange("p a b -> p (a b)"), in_=p_k.t[:, :]), reads=[p_k], writes=[kT])
                            for hl in range(4):
                                P.op("pe", lambda e, qT=qT, kT=kT, ssb=ssb, psb16=psb16, pT=pT, nmx=nmx, hl=hl: e.matmul(p_s.t[:, hl * 256:(hl + 1) * 256], lhsT=qT.t[:, hl, :], rhs=kT.t[:, hl, :], start=True, stop=False), reads=[qT, kT], writes=[p_s])
                                P.op("pe", lambda e, qT=qT, kT=kT, ssb=ssb, psb16=psb16, pT=pT, nmx=nmx, hl=hl, kbr=kbr: e.matmul(p_s.t[:, hl * 256:(hl + 1) * 256], lhsT=ones1.t[0:1, :], rhs=kbr.t[0:1, :], start=False, stop=True), reads=[ones1, kbr], writes=[p_s])
                            P.op("dve", lambda e, qT=qT, kT=kT, ssb=ssb, psb16=psb16, pT=pT, nmx=nmx, p=p, hg=hg: e.scalar_tensor_tensor(out=ssb.t[:, :, :].rearrange("p a b -> p (a b)"), in0=p_s.t[:, :], scalar=SCALE, in1=biasT[p].t[:, hg * 4:(hg + 1) * 4, :].rearrange("p a b -> p (a b)"), op0=ALU.mult, op1=ALU.add),
                                 reads=[p_s, biasT[p]], writes=[ssb])
                            P.op("dve", lambda e, qT=qT, kT=kT, ssb=ssb, psb16=psb16, pT=pT, nmx=nmx, st_=st_, hg=hg: e.tensor_reduce(out=st_.t[:, hg * 4:(hg + 1) * 4], in_=ssb.t[:, :, :], axis=AX.X, op=ALU.max), reads=[ssb], writes=[st_])
                            P.op("dve", lambda e, qT=qT, kT=kT, ssb=ssb, psb16=psb16, pT=pT, nmx=nmx, st_=st_, hg=hg: e.tensor_scalar(out=nmx.t[:, :], in0=st_.t[:, hg * 4:(hg + 1) * 4], scalar1=-1.0, scalar2=None, op0=ALU.mult), reads=[st_], writes=[nmx])
                            for hl in range(4):
                                h = hg * 4 + hl
                                P.op("act", lambda e, qT=qT, kT=kT, ssb=ssb, psb16=psb16, pT=pT, nmx=nmx, hl=hl, h=h, st_=st_: e.activation(out=psb16.t[:, hl, :], in_=ssb.t[:, hl, :], func=AF.Exp, bias=nmx.t[:, hl:hl + 1], scale=1.0, accum_out=st_.t[:, 8 + h:9 + h]), reads=[ssb, nmx], writes=[psb16, st_])
                            for hl in range(4):
                                for a in range(2):
                                    P.op("pe", lambda e, qT=qT, kT=kT, ssb=ssb, psb16=psb16, pT=pT, nmx=nmx, hl=hl, a=a: e.transpose(out=p_p.t[:, (hl * 2 + a) * 128:(hl * 2 + a + 1) * 128], in_=psb16.t[:, hl, a * 128:(a + 1) * 128], identity=self.identb.t[:, :]), reads=[psb16, self.identb], writes=[p_p])
                            P.op("dve", lambda e, qT=qT, kT=kT, ssb=ssb, psb16=psb16, pT=pT, nmx=nmx: e.tensor_copy(out=pT.t[:, :, :, :].rearrange("p a b c -> p (a b c)"), in_=p_p.t[:, :]), reads=[p_p], writes=[pT])
                            for hl in range(4):
                                h = hg * 4 + hl
                                P.op("pe", lambda e, qT=qT, kT=kT, ssb=ssb, psb16=psb16, pT=pT, nmx=nmx, hl=hl, h=h, VA=VA: e.matmul(p_o.t[:, hl * 128:(hl + 1) * 128], lhsT=pT.t[:, hl, 0, :], rhs=VA.t[:, h * 128:(h + 1) * 128], start=True, stop=False), reads=[pT, VA], writes=[p_o])
                                P.op("pe", lambda e, qT=qT, kT=kT, ssb=ssb, psb16=psb16, pT=pT, nmx=nmx, hl=hl, h=h, VB=VB: e.matmul(p_o.t[:, hl * 128:(hl + 1) * 128], lhsT=pT.t[:, hl, 1, :], rhs=VB.t[:, h * 128:(h + 1) * 128], start=False, stop=True), reads=[pT, VB], writes=[p_o])
                            P.op("act", lambda e, qT=qT, kT=kT, ssb=ssb, psb16=psb16, pT=pT, nmx=nmx, os_=os_, hg=hg: e.activation(out=os_.t[:, hg * 512:(hg + 1) * 512], in_=p_o.t[:, :], func=AF.Copy), reads=[p_o], writes=[os_])
                        sl = rows(128 * m, 128)
                        P.dma(lambda e, os_=os_, sl=sl, p=p: e.dma_start(out=self.AO.t[p, sl, :], in_=os_.t[:, :]), os_, reads=[os_], writes=[self.AO], eng="pool")
                        P.dma(lambda e, st_=st_, sl=sl, p=p: e.dma_start(out=self.AM.t[p, sl, :], in_=st_.t[:, :]), st_, reads=[st_], writes=[self.AM], eng="pool")
            self.barrier()
            P.stack = old

    def build(self):
        self.setup()
        for nm in ["p0", "p1", "p2", "p3", "p4", "p5", "p6", "p6a", "p6b", "p6c", "p7"]:
            if nm in self.stages:
                getattr(self, nm)()
        toks = self.barrier()
        st = self.P.emit(final_waits=toks)
        print("ops", st)
        return self.nc


def bc_last(ap, n):
    dims = [list(d) for d in ap.ap]
    assert dims[-1][1] == 1
    dims[-1] = [0, n]
    return bass.AP(ap.tensor, ap.offset, dims)


def bc_mid(ap, n):
    dims = [list(d) for d in ap.ap]
    dims = [dims[0], [0, n]] + dims[1:]
    return bass.AP(ap.tensor, ap.offset, dims)


class MK3(MK2):
    def __init__(self, L, stages, dump=(), lite=False, as_input=()):
        super().__init__(L, stages, dump, lite, as_input)
        P = self.P
        self.out = P.dram("out", [L, D], F32, kind="ExternalOutput")

    def y_to_YT(self, st, y16, row0, i):
        P = self.P
        pt = st["p_t"]
        yT = st["yT"][(i // 4) % 2]
        for c in range(8):
            P.op("pe", lambda e, c=c: e.transpose(out=pt.t[:, c * 128:(c + 1) * 128], in_=y16.t[:, c * 128:(c + 1) * 128], identity=self.identb.t[:, :]), reads=[y16, self.identb], writes=[pt])
        j = i % 4
        P.op("act", lambda e, j=j, yT=yT: e.activation(out=yT.t[:, :, j * 128:(j + 1) * 128], in_=pt.t[:, :].rearrange("p (c n) -> p c n", c=8), func=AF.Copy), reads=[pt, yT], writes=[yT])
        if j == 3:
            b = i // 4
            dst = self.YT.t[row0:row0 + AW, b * T:(b + 1) * T].rearrange("(c p) t -> p c t", p=128)
            P.dma(lambda e, dst=dst, yT=yT: e.dma_start(out=dst, in_=yT.t[:, :, :]), yT, reads=[yT], writes=[self.YT], eng="pool")

    def p5(self):
        P = self.P
        L = self.L
        with contextlib.ExitStack() as es:
            old = P.stack
            P.stack = es
            st = dict(p_t=P.psum("p5pt", [128, 8 * 128], BF16), yT=[P.sbuf("p5yT%d" % i, [128, 8, T], BF16) for i in range(2)])
            nb = P.sbuf("nattb", [128, AW], F32)
            P.dma(lambda e: e.dma_start(out=nb.t[:, :], in_=bcast_rows(self.nattn_in.t, 0, AW)), nb, writes=[nb])
            ao = [[P.sbuf("ao%d_%d" % (k, p), [128, AW], F32) for p in range(3)] for k in range(2)]
            am = [[P.sbuf("am%d_%d" % (k, p), [128, 16], F32) for p in range(3)] for k in range(2)]
            M = P.sbuf("p5M", [128, 8], F32)
            w = [P.sbuf("p5w%d" % p, [128, 8], F32) for p in range(3)]
            den = P.sbuf("p5den", [128, 8], F32)
            acc = P.sbuf("p5acc", [128, AW], F32)
            tmp = P.sbuf("p5tmp", [128, AW], F32)
            ss = P.sbuf("p5ss", [128, 1], F32)
            y16 = [P.sbuf("p5y%d" % i, [128, AW], BF16) for i in range(2)]
            for i in range(L // 128):
                k = i % 2
                for p in range(3):
                    P.dma(lambda e, k=k, p=p, i=i: e.dma_start(out=ao[k][p].t[:, :], in_=self.AO.t[p, i * 128:(i + 1) * 128, :]), ao[k][p], reads=[self.AO], writes=[ao[k][p]])
                    P.dma(lambda e, k=k, p=p, i=i: e.dma_start(out=am[k][p].t[:, :], in_=self.AM.t[p, i * 128:(i + 1) * 128, :]), am[k][p], reads=[self.AM], writes=[am[k][p]])
                a0, a1, a2 = am[k]
                P.op("dve", lambda e, a0=a0, a1=a1: e.tensor_tensor(out=M.t[:, :], in0=a0.t[:, 0:8], in1=a1.t[:, 0:8], op=ALU.max), reads=[a0, a1], writes=[M])
                P.op("dve", lambda e, a2=a2: e.tensor_tensor(out=M.t[:, :], in0=M.t[:, :], in1=a2.t[:, 0:8], op=ALU.max), reads=[M, a2], writes=[M])
                for p in range(3):
                    ap_ = am[k][p]
                    P.op("dve", lambda e, p=p, ap_=ap_: e.tensor_tensor(out=w[p].t[:, :], in0=ap_.t[:, 0:8], in1=M.t[:, :], op=ALU.subtract), reads=[ap_, M], writes=[w[p]])
                    P.op("act", lambda e, p=p: e.activation(out=w[p].t[:, :], in_=w[p].t[:, :], func=AF.Exp), reads=[w[p]], writes=[w[p]])
                P.op("dve", lambda e, a0=a0: e.tensor_tensor(out=den.t[:, :], in0=w[0].t[:, :], in1=a0.t[:, 8:16], op=ALU.mult), reads=[w[0], a0], writes=[den])
                for p in (1, 2):
                    ap_ = am[k][p]
                    P.op("dve", lambda e, p=p, ap_=ap_: e.tensor_tensor(out=M.t[:, :], in0=w[p].t[:, :], in1=ap_.t[:, 8:16], op=ALU.mult), reads=[w[p], ap_, M], writes=[M])
                    P.op("dve", lambda e: e.tensor_tensor(out=den.t[:, :], in0=den.t[:, :], in1=M.t[:, :], op=ALU.add), reads=[den, M], writes=[den])
                P.op("dve", lambda e: e.reciprocal(out=den.t[:, :], in_=den.t[:, :]), reads=[den], writes=[den])
                for p in range(3):
                    P.op("dve", lambda e, p=p: e.tensor_tensor(out=w[p].t[:, :], in0=w[p].t[:, :], in1=den.t[:, :], op=ALU.mult), reads=[w[p], den], writes=[w[p]])
                v3 = lambda t: t.t[:, :].rearrange("p (h d) -> p h d", h=8)
                wb = lambda p: bc_last(w[p].t[:, :].rearrange("p (h o) -> p h o", o=1), 128)
                P.op("dve", lambda e, k=k: e.tensor_tensor(out=v3(acc), in0=v3(ao[k][0]), in1=wb(0), op=ALU.mult), reads=[ao[k][0], w[0]], writes=[acc])
                for p in (1, 2):
                    P.op("dve", lambda e, k=k, p=p: e.tensor_tensor(out=v3(tmp), in0=v3(ao[k][p]), in1=wb(p), op=ALU.mult), reads=[ao[k][p], w[p], tmp], writes=[tmp])
                    P.op("dve", lambda e: e.tensor_tensor(out=acc.t[:, :], in0=acc.t[:, :], in1=tmp.t[:, :], op=ALU.add), reads=[acc, tmp], writes=[acc])
                P.op("act", lambda e: e.activation(out=tmp.t[:, :], in_=acc.t[:, :], func=AF.Square, accum_out=ss.t[:, 0:1]), reads=[acc, tmp], writes=[tmp, ss])
                P.op("act", lambda e: e.activation(out=ss.t[:, :], in_=ss.t[:, :], func=AF.Sqrt, bias=self.epsc.t[:, 0:1], scale=1.0 / AW), reads=[ss, self.epsc], writes=[ss])
                P.op("dve", lambda e: e.reciprocal(out=ss.t[:, :], in_=ss.t[:, :]), reads=[ss], writes=[ss])
                y = y16[i % 2]
                P.op("dve", lambda e, y=y: e.scalar_tensor_tensor(out=y.t[:, :], in0=acc.t[:, :], scalar=ss.t[:, 0:1], in1=nb.t[:, :], op0=ALU.mult, op1=ALU.mult), reads=[acc, ss, nb], writes=[y])
                self.y_to_YT(st, y, 0, i)
            self.barrier()
            P.stack = old

    def p7(self):
        P = self.P
        with contextlib.ExitStack() as es:
            old = P.stack
            P.stack = es
            s = self.alloc_ffn()
            xT, hT = s["xT"], s["hT"]
            for b in range(self.NB):
                src = self.X1T.t[:, b * T:(b + 1) * T].rearrange("(c p) t -> p c t", p=128)
                P.dma(lambda e, src=src: e.dma_start(out=xT.t[:, :, :], in_=src), xT, reads=[self.X1T], writes=[xT])
                srcy = self.YT.t[:, b * T:(b + 1) * T].rearrange("(c p) t -> p c t", p=128)
                P.dma(lambda e, srcy=srcy: e.dma_start(out=hT.t[:, :, :], in_=srcy), hT, reads=[self.YT], writes=[hT])
                for dc in range(KC):
                    w = s["wi"][s["wi_n"] % 8]
                    s["wi_n"] += 1
                    P.dma(lambda e, w=w, dc=dc: e.dma_start(out=w.t[:, :], in_=self.WOr.t[dc, :, :]), w, reads=[self.WOr], writes=[w])
                    py = s["pt"][s["pn"] % 2]
                    s["pn"] += 1
                    for kc in range(KC):
                        P.op("pe", lambda e, py=py, w=w, kc=kc: e.matmul(py.t[:, :], lhsT=w.t[:, kc * 128:(kc + 1) * 128], rhs=hT.t[:, kc, :], start=(kc == 0), stop=(kc == KC - 1)), reads=[w, hT], writes=[py])
                    Gc = self.AB.t[:, 5 * KC + dc:5 * KC + dc + 1]
                    P.op("dve", lambda e, py=py, dc=dc, Gc=Gc: e.scalar_tensor_tensor(out=xT.t[:, dc, :], in0=py.t[:, :], scalar=Gc, in1=xT.t[:, dc, :], op0=ALU.mult, op1=ALU.add),
                         reads=[py, xT, self.AB], writes=[xT])
                if "X2T" in self.dump:
                    dst = self.X2T.t[:, b * T:(b + 1) * T].rearrange("(c p) t -> p c t", p=128)
                    P.dma(lambda e, dst=dst: e.dma_start(out=dst, in_=xT.t[:, :, :]), xT, reads=[xT], writes=[self.X2T], eng="pool")
                self.rms_stats(s)
                self.norm_affine(s, 2)
                self.ffn(s, self.W2I, self.W2O, 2)
                self.rms_stats(s)
                rstd = s["rstd"]
                for c in range(KC):
                    nf = self.nrm.t[:, 3 * KC + c:3 * KC + c + 1]
                    P.op("dve", lambda e, c=c, nf=nf: e.scalar_tensor_tensor(out=xT.t[:, c, :], in0=xT.t[:, c, :], scalar=nf, in1=rstd.t[:, :], op0=ALU.mult, op1=ALU.mult), reads=[xT, rstd, self.nrm], writes=[xT])
                for j in range(T // 128):
                    xt = s["xtok"][j % 2]
                    for cg in range(KC // 4):
                        pt = s["pt"][s["pn"] % 2]
                        s["pn"] += 1
                        for ci in range(4):
                            c = cg * 4 + ci
                            P.op("pe", lambda e, pt=pt, c=c, ci=ci, j=j: e.transpose(out=pt.t[:, ci * 128:(ci + 1) * 128], in_=xT.t[:, c, j * 128:(j + 1) * 128], identity=self.ident.t[:, :]), reads=[xT, self.ident], writes=[pt])
                        if cg % 2 == 0:
                            P.op("dve", lambda e, pt=pt, xt=xt, cg=cg: e.tensor_copy(out=xt.t[:, cg * 512:(cg + 1) * 512], in_=pt.t[:, :]), reads=[pt, xt], writes=[xt])
                        else:
                            P.op("act", lambda e, pt=pt, xt=xt, cg=cg: e.activation(out=xt.t[:, cg * 512:(cg + 1) * 512], in_=pt.t[:, :], func=AF.Copy), reads=[pt, xt], writes=[xt])
                    r0 = b * T + j * 128
                    P.dma(lambda e, xt=xt, r0=r0: e.dma_start(out=self.out.t[r0:r0 + 128, :], in_=xt.t[:, :]), xt, reads=[xt], writes=[self.out], eng="pool")
            self.barrier()
            P.stack = old


import os
CH = 128
DN_STOP = int(os.environ.get('DN_STOP', '99'))


def make_dnconst():
    a = np.arange(128)
    low_i = (a[:, None] >= a[None, :]).astype(np.float32)
    up_i = (a[:, None] <= a[None, :]).astype(np.float32)
    low_s = (a[:, None] > a[None, :]).astype(np.float32)
    up_s = (a[:, None] < a[None, :]).astype(np.float32)
    sel = np.zeros((128, 8 * 128), np.float32)
    for h in range(8):
        sel[h, h * 128:(h + 1) * 128] = 1.0
    blk = ((a[:, None] // 32) == (a[None, :] // 32)).astype(np.float32)
    return np.ascontiguousarray(np.concatenate([low_i, up_i, low_s, up_s, sel, blk, 1.0 - blk], axis=1))


class MK4(MK3):
    def __init__(self, L, stages, dump=(), lite=False, as_input=()):
        super().__init__(L, stages, dump, lite, as_input)
        P = self.P
        def ein(n, s, dt=F32):
            self.ext_inputs.append(n)
            return P.dram(n, s, dt, kind="ExternalInput")
        dmp = lambda n: ("ExternalOutput" if n in self.dump else ("ExternalInput" if n in self.as_input else None))
        self.conv_wT = ein("conv_wT", [128, 120])
        self.gate_par = ein("gate_par", [16, 2])
        self.ndn_in = ein("norm_dn_out", [1, 128])
        self.dnconst = ein("dnconst", [128, 14 * 128])
        self.QT = P.dram("QT", [8, 128, L], F32, kind=dmp("QT"))
        self.KT = P.dram("KT", [8, 128, L], F32, kind=dmp("KT"))
        self.QTOK = P.dram("QTOK", [L, 8, 128], F32, kind=dmp("QTOK"))
        self.KTOK = P.dram("KTOK", [L, 8, 128], F32, kind=dmp("KTOK"))
        self.VTOK = P.dram("VTOK", [L, 8, 128], F32, kind=dmp("VTOK"))
        self.GB = P.dram("GB", [L, 32], F32, kind=dmp("GB"))
        self.ODN = P.dram("ODN", [2, L, AW], F32, kind=dmp("ODN"))

    def p6a(self):
        P = self.P
        L, NB = self.L, self.NB
        QS = DH ** -0.5
        with contextlib.ExitStack() as es:
            old = P.stack
            P.stack = es
            cw = P.sbuf("cw", [128, 120], F32)
            P.dma(lambda e: e.dma_start(out=cw.t[:, :], in_=self.conv_wT.t[:, :]), cw, writes=[cw])
            gp = P.sbuf("gp", [16, 2], F32)
            P.dma(lambda e: e.dma_start(out=gp.t[:, :], in_=self.gate_par.t[:, :]), gp, writes=[gp])
            negA = P.sbuf("negA", [16, 1], F32)
            one16 = P.sbuf("one16", [16, 1], F32)
            P.op("pool", lambda e: e.memset(one16.t[:, :], 1.0), writes=[one16])
            P.op("act", lambda e: e.activation(out=negA.t[:, :], in_=gp.t[:, 0:1], func=AF.Exp), reads=[gp], writes=[negA])
            P.op("dve", lambda e: e.tensor_scalar(out=negA.t[:, :], in0=negA.t[:, :], scalar1=-1.0, scalar2=None, op0=ALU.mult), reads=[negA], writes=[negA])
            xin = [P.sbuf("xin%d" % i, [128, T + 4], F32) for i in range(3)]
            acc = [P.sbuf("cacc%d" % i, [128, T], F32) for i in range(2)]
            sv = [P.sbuf("csv%d" % i, [128, T], F32) for i in range(2)]
            sq = [P.sbuf("csq%d" % i, [128, T], F32) for i in range(2)]
            rs = [P.sbuf("crs%d" % i, [128, T], F32) for i in range(2)]
            xn = [P.sbuf("cxn%d" % i, [128, T], F32) for i in range(2)]
            tk = [P.sbuf("ctk%d" % i, [128, 4, 128], F32) for i in range(2)]
            mk = P.sbuf("cmk", [128, T], F32)
            pss = [P.psum("cpss%d" % i, [128, T], F32) for i in range(2)]
            ptt = [P.psum("cptt%d" % i, [128, T], F32) for i in range(2)]
            braw = P.sbuf("braw", [16, T], F32)
            araw = P.sbuf("araw", [16, T], F32)
            gtk = P.sbuf("gtk", [128, 4, 32], F32)
            psg = P.psum("psg", [128, T], F32)
            it = 0
            for b in range(NB):
                P.dma(lambda e, b=b: e.dma_start(out=mk.t[:, :], in_=bcast_rows(self.mask.t, b * T, T)), mk, reads=[self.mask], writes=[mk])
                for f in range(24):
                    kind, h = f // 8, f % 8
                    xi = xin[it % 3]
                    k2 = it % 2
                    it += 1
                    lo = b * T - 2
                    hi = b * T + T + 2
                    c_lo, c_hi = 0, T + 4
                    if b == 0:
                        P.op("pool", lambda e, xi=xi: e.memset(xi.t[:, 0:2], 0.0), reads=[xi], writes=[xi])
                        lo, c_lo = 0, 2
                    if b == NB - 1:
                        P.op("pool", lambda e, xi=xi: e.memset(xi.t[:, T + 2:T + 4], 0.0), reads=[xi], writes=[xi])
                        hi, c_hi = L, T + 2
                    P.dma(lambda e, xi=xi, f=f, lo=lo, hi=hi, c_lo=c_lo, c_hi=c_hi: e.dma_start(out=xi.t[:, c_lo:c_hi], in_=self.DQKVT.t[f * 128:(f + 1) * 128, lo:hi]), xi, reads=[self.DQKVT, xi], writes=[xi])
                    ac = acc[k2]
                    P.op("dve", lambda e, ac=ac, xi=xi, f=f: e.tensor_scalar(out=ac.t[:, :], in0=xi.t[:, 0:T], scalar1=cw.t[:, f * 5:f * 5 + 1], scalar2=None, op0=ALU.mult), reads=[xi, cw], writes=[ac])
                    for j in range(1, 5):
                        P.op("dve", lambda e, ac=ac, xi=xi, f=f, j=j: e.scalar_tensor_tensor(out=ac.t[:, :], in0=xi.t[:, j:j + T], scalar=cw.t[:, f * 5 + j:f * 5 + j + 1], in1=ac.t[:, :], op0=ALU.mult, op1=ALU.add), reads=[xi, cw, ac], writes=[ac])
                    P.op("dve", lambda e, ac=ac: e.tensor_tensor(out=ac.t[:, :], in0=ac.t[:, :], in1=mk.t[:, :], op=ALU.mult), reads=[ac, mk], writes=[ac])
                    s_ = sv[k2]
                    P.op("act", lambda e, ac=ac, s_=s_: e.activation(out=s_.t[:, :], in_=ac.t[:, :], func=AF.Silu), reads=[ac], writes=[s_])
                    if kind < 2:
                        q_ = sq[k2]
                        ps = pss[k2]
                        r_ = rs[k2]
                        x_ = xn[k2]
                        P.op("act", lambda e, q_=q_, s_=s_: e.activation(out=q_.t[:, :], in_=s_.t[:, :], func=AF.Square), reads=[s_], writes=[q_])
                        P.op("pe", lambda e, ps=ps, q_=q_: e.matmul(ps.t[:, :], lhsT=self.ones.t[:, :], rhs=q_.t[:, :], start=True, stop=True), reads=[q_, self.ones], writes=[ps])
                        P.op("act", lambda e, r_=r_, ps=ps: e.activation(out=r_.t[:, :], in_=ps.t[:, :], func=AF.Sqrt, bias=self.epsc.t[:, 0:1], scale=1.0), reads=[ps, self.epsc], writes=[r_])
                        P.op("dve", lambda e, r_=r_: e.reciprocal(out=r_.t[:, :], in_=r_.t[:, :]), reads=[r_], writes=[r_])
                        scl = QS if kind == 0 else 1.0
                        P.op("dve", lambda e, x_=x_, s_=s_, r_=r_, scl=scl: e.scalar_tensor_tensor(out=x_.t[:, :], in0=s_.t[:, :], scalar=scl, in1=r_.t[:, :], op0=ALU.mult, op1=ALU.mult), reads=[s_, r_], writes=[x_])
                        dstT = (self.QT if kind == 0 else self.KT)
                        P.dma(lambda e, x_=x_, dstT=dstT, h=h, b=b: e.dma_start(out=dstT.t[h, :, b * T:(b + 1) * T], in_=x_.t[:, :]), x_, reads=[x_], writes=[dstT], eng="pool")
                        src_fm = x_
                    else:
                        src_fm = s_
                    pt = ptt[k2]
                    for j in range(4):
                        P.op("pe", lambda e, pt=pt, src_fm=src_fm, j=j: e.transpose(out=pt.t[:, j * 128:(j + 1) * 128], in_=src_fm.t[:, j * 128:(j + 1) * 128], identity=self.ident.t[:, :]), reads=[src_fm, self.ident], writes=[pt])
                    t_ = tk[k2]
                    P.op("act", lambda e, t_=t_, pt=pt: e.activation(out=t_.t[:, :, :].rearrange("p a b -> p (a b)"), in_=pt.t[:, :], func=AF.Copy), reads=[pt], writes=[t_])
                    dtok = [self.QTOK, self.KTOK, self.VTOK][kind]
                    dst = dtok.t[b * T:(b + 1) * T, h, :].rearrange("(j p) d -> p j d", p=128)
                    P.dma(lambda e, t_=t_, dst=dst: e.dma_start(out=dst, in_=t_.t[:, :, :]), t_, reads=[t_], writes=[dtok], eng="pool")
                P.dma(lambda e, b=b: e.dma_start(out=braw.t[:, :], in_=self.BAT.t[0:16, b * T:(b + 1) * T]), braw, reads=[self.BAT], writes=[braw])
                P.dma(lambda e, b=b: e.dma_start(out=araw.t[:, :], in_=self.BAT.t[16:32, b * T:(b + 1) * T]), araw, reads=[self.BAT], writes=[araw])
                P.op("act", lambda e: e.activation(out=braw.t[:, :], in_=braw.t[:, :], func=AF.Sigmoid), reads=[braw], writes=[braw])
                P.op("act", lambda e: e.activation(out=araw.t[:, :], in_=araw.t[:, :], func=AF.Exp, bias=gp.t[:, 1:2], scale=1.0), reads=[araw, gp], writes=[araw])
                P.op("act", lambda e: e.activation(out=araw.t[:, :], in_=araw.t[:, :], func=AF.Ln, bias=one16.t[:, 0:1], scale=1.0), reads=[araw, one16], writes=[araw])
                P.op("dve", lambda e: e.tensor_scalar(out=araw.t[:, :], in0=araw.t[:, :], scalar1=negA.t[:, 0:1], scalar2=None, op0=ALU.mult), reads=[araw, negA], writes=[araw])
                for j in range(4):
                    P.op("pe", lambda e, j=j: e.transpose(out=psg.t[:, j * 32:j * 32 + 16], in_=braw.t[:, j * 128:(j + 1) * 128], identity=self.ident.t[0:16, 0:16]), reads=[braw, self.ident], writes=[psg])
                    P.op("pe", lambda e, j=j: e.transpose(out=psg.t[:, j * 32 + 16:j * 32 + 32], in_=araw.t[:, j * 128:(j + 1) * 128], identity=self.ident.t[0:16, 0:16]), reads=[araw, self.ident], writes=[psg])
                P.op("dve", lambda e: e.tensor_copy(out=gtk.t[:, :, :].rearrange("p a b -> p (a b)"), in_=psg.t[:, 0:128]), reads=[psg], writes=[gtk])
                dstg = self.GB.t[b * T:(b + 1) * T, :].rearrange("(j p) c -> p j c", p=128)
                P.dma(lambda e, dstg=dstg: e.dma_start(out=dstg, in_=gtk.t[:, :, :]), gtk, reads=[gtk], writes=[self.GB], eng="pool")
            self.barrier()
            P.stack = old

    def p6b(self):
        P = self.P
        L = self.L
        NCH = L // CH
        with contextlib.ExitStack() as es:
            old = P.stack
            P.stack = es
            big = lambda n, dt=F32: P.sbuf(n, [128, 8, 128], dt)
            v3 = lambda t: t.t[:, :, :]
            f2 = lambda t: t.t[:, :, :].rearrange("p a b -> p (a b)")
            pv3 = lambda t: t.t[:, :].rearrange("p (a b) -> p a b", a=8)
            col8 = lambda ap: bc_last(ap.rearrange("p (h o) -> p h o", o=1), 128)
            dnc = P.sbuf("dnc", [128, 14 * 128], F32)
            P.dma(lambda e: e.dma_start(out=dnc.t[:, :], in_=self.dnconst.t[:, :]), dnc, writes=[dnc])
            LOWI, UPI, LOWS, UPS = [dnc.t[:, i * 128:(i + 1) * 128] for i in range(4)]
            SEL = lambda h: dnc.t[0:8, 512 + h * 128:512 + (h + 1) * 128]
            BLK = dnc.t[:, 1536:1664]
            NBLK = dnc.t[:, 1664:1792]

            def act_evac(dst, src):
                for hf in range(2):
                    P.op("act", lambda e, dst=dst, src=src, hf=hf: e.activation(out=f2(dst)[:, hf * 512:(hf + 1) * 512], in_=src.t[:, hf * 512:(hf + 1) * 512], func=AF.Copy), reads=[src], writes=[dst])

            def mm8(dst, lh, rh):
                for h in range(8):
                    P.op("pe", lambda e, h=h, dst=dst, lh=lh, rh=rh: e.matmul(dst.t[:, h * 128:(h + 1) * 128], lhsT=lh.t[:, h, :], rhs=rh.t[:, h, :], start=True, stop=True), reads=[lh, rh], writes=[dst])

            def dve_evac(dst, src):
                P.op("dve", lambda e, dst=dst, src=src: e.tensor_copy(out=f2(dst), in_=src.t[:, :]), reads=[src], writes=[dst])

            def dve_acc(dst, a_, src):
                P.op("dve", lambda e, dst=dst, a_=a_, src=src: e.tensor_tensor(out=f2(dst), in0=f2(a_), in1=src.t[:, :], op=ALU.add), reads=[a_, src], writes=[dst])

            def tt(dst_ap, in0, in1, op, reads, writes, eng="dve"):
                P.op(eng, lambda e: e.tensor_tensor(out=dst_ap, in0=in0, in1=in1, op=op), reads=reads, writes=writes)

            chains = []
            for c in range(2):
                W = dict(
                    inb=[dict(qT=big("c%d_qT%d" % (c, i)), kT=big("c%d_kT%d" % (c, i)), ktok=big("c%d_ktok%d" % (c, i)), vtok=big("c%d_vtok%d" % (c, i)),
                              gb=P.sbuf("c%d_gb%d" % (c, i), [128, 32], F32)) for i in range(2)],
                    t=[big("c%d_t%d" % (c, i)) for i in range(6)],
                    En=big("c%d_En" % c), Ens=big("c%d_Ens" % c), egrow=big("c%d_egrow" % c),
                    gsm=P.sbuf("c%d_gsm" % c, [128, 16], F32), gcrow=P.sbuf("c%d_gcrow" % c, [8, 128], F32), nbrow=P.sbuf("c%d_nbrow" % c, [8, 128], F32),
                    sm={n: P.sbuf("c%d_%s" % (c, n), [128, 8], F32) for n in ("egc", "rev", "erev", "egl", "nbeta", "bege")},
                    o_sb=[big("c%d_o%d" % (c, i)) for i in range(2)], S=big("c%d_S" % c), it=0,
                    p=[P.psum("c%d_p%d" % (c, i), [128, 1024], F32) for i in range(2)])
                chains.append(W)

            def chunk_gen(W, dirv, n):
                MR_s = LOWS if dirv == 0 else UPS
                MC = UPI if dirv == 0 else LOWI
                MC_s = UPS if dirv == 0 else LOWS
                ib = W["inb"][W["it"] % 2]
                ob = W["o_sb"][W["it"] % 2]
                W["it"] += 1
                qT, kT, ktok, vtok, gb = ib["qT"], ib["kT"], ib["ktok"], ib["vtok"], ib["gb"]
                t0, t1, t2, t3, t4, t5 = W["t"]
                En, Ens, egrow, gsm, gcrow, nbrow, S = W["En"], W["Ens"], W["egrow"], W["gsm"], W["gcrow"], W["nbrow"], W["S"]
                egc, rev, erev, egl, nbeta, bege = [W["sm"][k_] for k_ in ("egc", "rev", "erev", "egl", "nbeta", "bege")]
                p0, p1 = W["p"]
                c0, c1 = n * CH, (n + 1) * CH
                P.dma(lambda e: e.dma_start(out=v3(qT), in_=self.QT.t[:, :, c0:c1].rearrange("h d a -> d h a")), qT, reads=[self.QT], writes=[qT])
                P.dma(lambda e: e.dma_start(out=v3(kT), in_=self.KT.t[:, :, c0:c1].rearrange("h d a -> d h a")), kT, reads=[self.KT], writes=[kT])
                P.dma(lambda e: e.dma_start(out=v3(ktok), in_=self.KTOK.t[c0:c1, :, :]), ktok, reads=[self.KTOK], writes=[ktok])
                P.dma(lambda e: e.dma_start(out=v3(vtok), in_=self.VTOK.t[c0:c1, :, :]), vtok, reads=[self.VTOK], writes=[vtok])
                P.dma(lambda e: e.dma_start(out=gb.t[:, :], in_=self.GB.t[c0:c1, :]), gb, reads=[self.GB], writes=[gb])
                g8 = gb.t[:, 16 + dirv * 8:16 + dirv * 8 + 8]
                b8 = gb.t[:, dirv * 8:dirv * 8 + 8]
                gc = gsm.t[:, 0:8]
                tot = gsm.t[:, 8:16]
                P.op("pe", lambda e: e.matmul(p0.t[:, 0:8], lhsT=MC, rhs=g8, start=True, stop=True), reads=[gb, dnc], writes=[p0])
                P.op("pe", lambda e: e.matmul(p0.t[:, 8:16], lhsT=self.ones.t[:, :], rhs=g8, start=True, stop=True), reads=[gb, self.ones], writes=[p0])
                P.op("pe", lambda e: e.matmul(p1.t[0:8, 0:128], lhsT=g8, rhs=MC, start=True, stop=True), reads=[gb, dnc], writes=[p1])
                P.op("dve", lambda e: e.tensor_copy(out=gsm.t[:, :], in_=p0.t[:, 0:16]), reads=[p0], writes=[gsm])
                P.op("dve", lambda e: e.tensor_copy(out=gcrow.t[:, :], in_=p1.t[0:8, 0:128]), reads=[p1], writes=[gcrow])
                P.op("dve", lambda e: e.tensor_scalar(out=nbeta.t[:, :], in0=b8, scalar1=-1.0, scalar2=None, op0=ALU.mult), reads=[gb], writes=[nbeta])
                yield
                P.op("act", lambda e: e.activation(out=egc.t[:, :], in_=gc, func=AF.Exp), reads=[gsm], writes=[egc])
                tt(rev.t[:, :], tot, gc, ALU.subtract, [gsm], [rev])
                P.op("act", lambda e: e.activation(out=erev.t[:, :], in_=rev.t[:, :], func=AF.Exp), reads=[rev], writes=[erev])
                P.op("act", lambda e: e.activation(out=egl.t[:, :], in_=tot, func=AF.Exp), reads=[gsm], writes=[egl])
                tt(bege.t[:, :], b8, egc.t[:, :], ALU.mult, [gb, egc], [bege])
                P.op("pe", lambda e: e.matmul(p1.t[0:8, 128:256], lhsT=nbeta.t[:, :], rhs=self.ident.t[:, :], start=True, stop=True), reads=[nbeta, self.ident], writes=[p1])
                P.op("dve", lambda e: e.tensor_copy(out=nbrow.t[:, :], in_=p1.t[0:8, 128:256]), reads=[p1], writes=[nbrow])
                for h in range(8):
                    P.op("pe", lambda e, h=h: e.matmul(p0.t[:, h * 128:(h + 1) * 128], lhsT=SEL(h), rhs=gcrow.t[:, :], start=True, stop=True), reads=[dnc, gcrow], writes=[p0])
                yield
                tt(v3(t0), col8(gc), pv3(p0), ALU.subtract, [gsm, p0], [t0])
                for hf in range(2):
                    P.op("act", lambda e, hf=hf: e.activation(out=f2(egrow)[:, hf * 512:(hf + 1) * 512], in_=p0.t[:, hf * 512:(hf + 1) * 512], func=AF.Exp), reads=[p0], writes=[egrow])
                P.op("dve", lambda e: e.tensor_scalar(out=f2(t1), in0=f2(t0), scalar1=0.0, scalar2=None, op0=ALU.min), reads=[t0], writes=[t1])
                P.op("act", lambda e: e.activation(out=f2(t1), in_=f2(t1), func=AF.Exp), reads=[t1], writes=[t1])
                tt(v3(t1), v3(t1), bc_mid(MR_s, 8), ALU.mult, [t1, dnc], [t1], eng="pool")
                P.op("dve", lambda e: e.tensor_scalar(out=f2(En), in0=f2(t0), scalar1=0.0, scalar2=-1.0, op0=ALU.max, op1=ALU.mult), reads=[t0], writes=[En])
                P.op("act", lambda e: e.activation(out=f2(En), in_=f2(En), func=AF.Exp), reads=[En], writes=[En])
                tt(v3(Ens), v3(En), bc_mid(MC_s, 8), ALU.mult, [En, dnc], [Ens], eng="pool")
                tt(v3(En), v3(En), bc_mid(MC, 8), ALU.mult, [En, dnc], [En])
                yield
                for h in range(8):
                    P.op("pe", lambda e, h=h: e.matmul(p1.t[:, h * 128:(h + 1) * 128], lhsT=SEL(h), rhs=nbrow.t[:, :], start=True, stop=True), reads=[dnc, nbrow], writes=[p1])
                for h in range(8):
                    P.op("pe", lambda e, h=h: e.matmul(p0.t[:, h * 128:(h + 1) * 128], lhsT=kT.t[:, h, :], rhs=kT.t[:, h, :], start=True, stop=True), reads=[kT], writes=[p0])
                yield
                tt(f2(t0), p0.t[:, :], f2(t1), ALU.mult, [p0, t1], [t0])
                tt(v3(t0), v3(t0), col8(nbeta.t[:, :]), ALU.mult, [t0, nbeta], [t0])
                tt(f2(Ens), p0.t[:, :], f2(Ens), ALU.mult, [p0, Ens], [Ens])
                tt(f2(Ens), f2(Ens), p1.t[:, :], ALU.mult, [Ens, p1], [Ens])
                tt(v3(t0), v3(t0), bc_mid(BLK, 8), ALU.mult, [t0, dnc], [t0])
                tt(v3(t1), v3(Ens), bc_mid(BLK, 8), ALU.mult, [Ens, dnc], [t1])
                tt(v3(Ens), v3(Ens), bc_mid(NBLK, 8), ALU.mult, [Ens, dnc], [Ens])
                tt(v3(t2), v3(t1), bc_mid(self.ident.t[:, :], 8), ALU.add, [t1, self.ident], [t2])
                tt(v3(t3), v3(t0), bc_mid(self.ident.t[:, :], 8), ALU.add, [t0, self.ident], [t3])
                yield
                Xc, Xn_, Yc, Yn_, Dt, DtT, Uo = t1, t4, t0, t5, t2, t3, Ens
                for lvl in range(4):
                    mm8(p0, Xc, Yc)
                    mm8(p1, Yc, Xc)
                    yield
                    act_evac(Yn_, p0)
                    dve_evac(Xn_, p1)
                    mm8(p0, Yn_, Dt)
                    mm8(p1, Xn_, DtT)
                    yield
                    dve_acc(Dt, Dt, p0)
                    dve_acc(DtT, DtT, p1)
                    Xc, Xn_ = Xn_, Xc
                    Yc, Yn_ = Yn_, Yc
                Mt, MtT, MtT2, P1, T32 = t0, t1, t4, t5, t0
                mm8(p0, DtT, Uo)
                mm8(p1, Uo, DtT)
                yield
                act_evac(Mt, p0)
                dve_evac(MtT, p1)
                mm8(p0, Mt, MtT)
                yield
                act_evac(MtT2, p0)
                mm8(p1, MtT2, Dt)
                yield
                dve_acc(P1, Dt, p1)
                mm8(p0, MtT, P1)
                yield
                dve_acc(T32, P1, p0)
                rhs_v, rhs_w, u_sb, wT_sb, kdec, vnew = t1, t2, t3, t4, t5, Ens
                tt(v3(rhs_v), v3(vtok), col8(b8), ALU.mult, [vtok, gb], [rhs_v], eng="pool")
                tt(v3(rhs_w), v3(ktok), col8(bege.t[:, :]), ALU.mult, [ktok, bege], [rhs_w], eng="pool")
                mm8(p0, T32, rhs_v)
                mm8(p1, rhs_w, T32)
                yield
                act_evac(u_sb, p0)
                dve_evac(wT_sb, p1)
                mm8(p0, kT, qT)
                yield
                tt(f2(En), p0.t[:, :], f2(En), ALU.mult, [p0, En], [En])
                tt(f2(egrow), f2(qT), f2(egrow), ALU.mult, [qT, egrow], [egrow])
                tt(v3(kdec), v3(ktok), col8(erev.t[:, :]), ALU.mult, [ktok, erev], [kdec], eng="pool")
                yield
                mm8(p1, wT_sb, S)
                yield
                tt(f2(vnew), f2(u_sb), p1.t[:, :], ALU.subtract, [u_sb, p1], [vnew])
                for h in range(8):
                    P.op("pe", lambda e, h=h: e.matmul(p0.t[:, h * 128:(h + 1) * 128], lhsT=egrow.t[:, h, :], rhs=S.t[:, h, :], start=True, stop=False), reads=[egrow, S], writes=[p0])
                    P.op("pe", lambda e, h=h: e.matmul(p0.t[:, h * 128:(h + 1) * 128], lhsT=En.t[:, h, :], rhs=vnew.t[:, h, :], start=False, stop=True), reads=[En, vnew], writes=[p0])
                mm8(p1, kdec, vnew)
                yield
                act_evac(ob, p0)
                P.dma(lambda e: e.dma_start(out=self.ODN.t[dirv, c0:c1, :], in_=f2(ob)), ob, reads=[ob], writes=[self.ODN], eng="act")
                tt(v3(S), v3(S), col8(egl.t[:, :]), ALU.mult, [S, egl], [S])
                tt(f2(S), f2(S), p1.t[:, :], ALU.add, [S, p1], [S])
                yield

            for c in range(2):
                S_ = chains[c]["S"]
                P.op("dve", lambda e, S_=S_: e.memset(f2(S_), 0.0), reads=[S_], writes=[S_])
            for i in range(NCH):
                gens = [chunk_gen(chains[0], 0, i), chunk_gen(chains[1], 1, NCH - 1 - i)]
                alive = [True, True]
                while any(alive):
                    for c in range(2):
                        if alive[c]:
                            try:
                                next(gens[c])
                            except StopIteration:
                                alive[c] = False
            self.barrier()
            P.stack = old

    def p6c(self):
        P = self.P
        L = self.L
        with contextlib.ExitStack() as es:
            old = P.stack
            P.stack = es
            st = dict(p_t=P.psum("p6pt", [128, 8 * 128], BF16), yT=[P.sbuf("p6yT%d" % i, [128, 8, T], BF16) for i in range(2)])
            nd = P.sbuf("ndnb", [128, 128], F32)
            P.dma(lambda e: e.dma_start(out=nd.t[:, :], in_=bcast_rows(self.ndn_in.t, 0, 128)), nd, writes=[nd])
            of = [P.sbuf("of%d" % i, [128, AW], F32) for i in range(2)]
            obk = [P.sbuf("obk%d" % i, [128, AW], F32) for i in range(2)]
            zt = [P.sbuf("zt%d" % i, [128, AW], F32) for i in range(2)]
            tmp = P.sbuf("p6tmp", [128, AW], F32)
            ss = P.sbuf("p6ss", [128, 8], F32)
            y16 = [P.sbuf("p6y%d" % i, [128, AW], BF16) for i in range(2)]
            v3 = lambda t: t.t[:, :].rearrange("p (h d) -> p h d", h=8)
            for i in range(L // 128):
                k = i % 2
                a, bb, z = of[k], obk[k], zt[k]
                r0, r1 = i * 128, (i + 1) * 128
                P.dma(lambda e, a=a, r0=r0, r1=r1: e.dma_start(out=a.t[:, :], in_=self.ODN.t[0, r0:r1, :]), a, reads=[self.ODN], writes=[a])
                P.dma(lambda e, bb=bb, r0=r0, r1=r1: e.dma_start(out=bb.t[:, :], in_=self.ODN.t[1, r0:r1, :]), bb, reads=[self.ODN], writes=[bb])
                P.dma(lambda e, z=z, r0=r0, r1=r1: e.dma_start(out=z.t[:, :], in_=self.Z.t[r0:r1, :]), z, reads=[self.Z], writes=[z])
                P.op("dve", lambda e, a=a, bb=bb: e.tensor_tensor(out=a.t[:, :], in0=a.t[:, :], in1=bb.t[:, :], op=ALU.add), reads=[a, bb], writes=[a])
                P.op("dve", lambda e, a=a: e.tensor_tensor(out=tmp.t[:, :], in0=a.t[:, :], in1=a.t[:, :], op=ALU.mult), reads=[a, tmp], writes=[tmp])
                P.op("dve", lambda e: e.tensor_reduce(out=ss.t[:, :], in_=v3(tmp), axis=AX.X, op=ALU.add), reads=[tmp], writes=[ss])
                P.op("act", lambda e: e.activation(out=ss.t[:, :], in_=ss.t[:, :], func=AF.Sqrt, bias=self.epsc.t[:, 0:1], scale=1.0 / DH), reads=[ss, self.epsc], writes=[ss])
                P.op("dve", lambda e: e.reciprocal(out=ss.t[:, :], in_=ss.t[:, :]), reads=[ss], writes=[ss])
                P.op("act", lambda e, z=z: e.activation(out=z.t[:, :], in_=z.t[:, :], func=AF.Silu), reads=[z], writes=[z])
                P.op("dve", lambda e, a=a: e.tensor_tensor(out=v3(a), in0=v3(a), in1=bc_last(ss.t[:, :].rearrange("p (h o) -> p h o", o=1), 128), op=ALU.mult), reads=[a, ss], writes=[a])
                P.op("dve", lambda e, a=a: e.tensor_tensor(out=v3(a), in0=v3(a), in1=bc_mid(nd.t[:, :], 8), op=ALU.mult), reads=[a, nd], writes=[a])
                y = y16[k]
                P.op("dve", lambda e, a=a, z=z, y=y: e.tensor_tensor(out=y.t[:, :], in0=a.t[:, :], in1=z.t[:, :], op=ALU.mult), reads=[a, z], writes=[y])
                self.y_to_YT(st, y, AW, i)
            self.barrier()
            P.stack = old

    def p6(self):
        self.p6a()
        self.p6b()
        self.p6c()


def _fm(v):
    return np.ascontiguousarray(np.asarray(v, np.float32).reshape(-1, 128).T)


def _prep_core(inp, x, c, Lreal, L, shared):
    xp = np.zeros((L, D), np.float32)
    mask = np.zeros((1, L), np.float32)
    if Lreal > 0:
        xp[:Lreal] = x
        mask[0, :Lreal] = 1.0
    m = dict(shared)
    m["x"] = xp
    m["cT"] = _fm(c)
    m["mask"] = mask
    return m


_NC_CACHE = {}


def kernel(x_prompt, x_sample, c_prompt, c_sample, w_mod, b_mod, norm_ffn1, w_ffn1_in, w_ffn1_out, norm_mix, w_in, conv_w,
           a_log, dt_bias, norm_attn_out, norm_dn_out, w_out, norm_ffn2, w_ffn2_in, w_ffn2_out, rel_bias, norm_final):
    f32 = lambda a: np.ascontiguousarray(np.asarray(a, dtype=np.float32))
    x_prompt, x_sample, c_prompt, c_sample = f32(x_prompt), f32(x_sample), f32(c_prompt), f32(c_sample)
    L = x_prompt.shape[1]
    Ls = x_sample.shape[1]
    norms = [f32(norm_ffn1)[0], f32(norm_mix)[0], f32(norm_ffn2)[0], f32(norm_final)]
    shared = dict(
        w_mod=f32(w_mod)[0], b_modT=_fm(f32(b_mod)[0]),
        normsT=np.ascontiguousarray(np.concatenate([_fm(n) for n in norms], axis=1)),
        w_ffn1_in=f32(w_ffn1_in)[0], w_ffn1_out=f32(w_ffn1_out)[0], w_in=f32(w_in)[0], w_out=f32(w_out)[0],
        w_ffn2_in=f32(w_ffn2_in)[0], w_ffn2_out=f32(w_ffn2_out)[0], ident=np.eye(128, dtype=np.float32),
        rel_bias=f32(rel_bias), onehot=make_onehot(), norm_attn_out=f32(norm_attn_out)[0].reshape(1, 1024),
        conv_wT=np.ascontiguousarray(f32(conv_w)[0].T.reshape(24, 128, 5).transpose(1, 0, 2).reshape(128, 120)),
        gate_par=np.ascontiguousarray(np.stack([f32(a_log)[0].reshape(16), f32(dt_bias)[0].reshape(16)], axis=1)),
        norm_dn_out=f32(norm_dn_out)[0].reshape(1, 128), dnconst=make_dnconst(),
    )
    if L not in _NC_CACHE:
        mk = MK4(L, stages=["p0", "p1", "p2", "p3", "p4", "p5", "p6", "p7"])
        _NC_CACHE[L] = (mk.build(), list(mk.ext_inputs))
    nc, names = _NC_CACHE[L]
    zc = np.zeros((D,), np.float32)
    cores = [(x_prompt[0], c_prompt[0], L), (x_sample[0], c_sample[0], Ls), (None, zc, 0), (None, zc, 0),
             (x_prompt[1], c_prompt[1], L), (x_sample[1], c_sample[1], Ls), (None, zc, 0), (None, zc, 0)]
    in_maps = []
    for (x, c, lr) in cores:
        m = _prep_core(None, x, c, lr, L, shared)
        in_maps.append({k: m[k] for k in names})
    res = run_bass_kernel_spmd(nc, in_maps, core_ids=list(range(8)))
    outs = [np.asarray(r["out"], dtype=np.float32) for r in res.results]
    y_prompt = np.stack([outs[0], outs[4]], axis=0)
    y_sample = np.stack([outs[1][:Ls], outs[5][:Ls]], axis=0)
    return (y_prompt, y_sample)
```

```python
from concourse.bass_utils import run_bass_kernel_spmd
import contextlib
import numpy as np
import concourse.bass as bass
import concourse.mybir as mybir

F32 = mybir.dt.float32
BF16 = mybir.dt.bfloat16
AF = mybir.ActivationFunctionType
ALU = mybir.AluOpType
AX = mybir.AxisListType

ENGS = ["pe", "act", "dve", "pool", "sp"]


class Buf:
    def __init__(self, name, t=None):
        self.name = name
        self.t = t
        self.writers = []
        self.readers = []
        self.dsem = None
        self.dcount = 0
        self.war = []
        self.is_psum = False

    def __getitem__(self, idx):
        return self.t[idx]


class Op:
    __slots__ = ("eng", "fn", "waits", "dma", "dsem", "dval", "needed", "mval", "seq")
    _n = [0]

    def __init__(self, eng, fn):
        Op._n[0] += 1
        self.seq = Op._n[0]
        self.eng = eng
        self.fn = fn
        self.waits = []
        self.dma = False
        self.dsem = None
        self.dval = 0
        self.needed = False
        self.mval = 0


class Prog:
    def __init__(self, nc):
        self.nc = nc
        self.stack = contextlib.ExitStack()
        self.ops = {e: [] for e in ENGS}
        self.sems = {}
        self.nbuf = 0
        self.pstack = self.stack
        self.sem_pool = []
        self.phase_bufs = []

    def sem(self, name):
        self.nbuf += 1
        name = "%s_%d" % (name, self.nbuf)
        return self.pstack.enter_context(self.nc.semaphore(name))

    def dsem_get(self, buf):
        if self.sem_pool:
            h, cnt = self.sem_pool.pop()
        else:
            h, cnt = self.sem("dq"), 0
        buf.dsem = h
        buf.dcount = cnt
        self.phase_bufs.append(buf)

    def release_phase_sems(self):
        for b in self.phase_bufs:
            self.sem_pool.append((b.dsem, b.dcount))
            b.dsem = None
        self.phase_bufs = []

    def sbuf(self, name, shape, dt=F32):
        self.nbuf += 1
        name = "%s_%d" % (name, self.nbuf)
        t = self.stack.enter_context(self.nc.sbuf_tensor(name, list(shape), dt))
        return Buf(name, t)

    def psum(self, name, shape, dt=F32):
        self.nbuf += 1
        name = "%s_%d" % (name, self.nbuf)
        t = self.stack.enter_context(self.nc.psum_tensor(name, list(shape), dt))
        b = Buf(name, t)
        b.is_psum = True
        return b

    def dram(self, name, shape, dt=F32, kind=None):
        if kind is None:
            t = self.nc.dram_tensor(name, list(shape), dt)
        else:
            t = self.nc.dram_tensor(name, list(shape), dt, kind=kind)
        return Buf(name, t)

    def _deps(self, op, reads, writes):
        for b in reads:
            for w in b.writers:
                op.waits.append(w)
            if b.is_psum:
                for r in b.readers:
                    if r.eng != op.eng:
                        op.waits.append(r)
        for b in writes:
            if b.readers:
                b.war = _compress(list(b.readers) + list(b.writers))
                op.waits.extend(b.war)
                b.readers = []
                b.writers = [op]
            else:
                op.waits.extend(b.war)
                if op.dma or any(w.dma for w in b.writers):
                    op.waits.extend(b.writers)
                b.writers.append(op)
                if len(b.writers) > 64:
                    b.writers = _compress(b.writers)
        for b in reads:
            b.readers.append(op)
            if len(b.readers) > 64:
                b.readers = _compress(b.readers)

    def op(self, eng, fn, reads=(), writes=()):
        o = Op(eng, fn)
        self._deps(o, reads, writes)
        self.ops[eng].append(o)
        return o

    def dma(self, fn, sb, reads=(), writes=(), eng="sp"):
        o = Op(eng, fn)
        o.dma = True
        if sb.dsem is None:
            self.dsem_get(sb)
        sb.dcount += 16
        o.dsem = sb.dsem
        o.dval = sb.dcount
        self._deps(o, reads, writes)
        self.ops[eng].append(o)
        return o

    def emit(self, final_waits=()):
        nc = self.nc
        esem = {e: self.sem("e_" + e) for e in ENGS}
        for e in ENGS:
            for o in self.ops[e]:
                for w in o.waits:
                    if not w.dma:
                        w.needed = True
        for e in ENGS:
            c = 0
            for o in self.ops[e]:
                if o.needed and not o.dma:
                    c += 1
                    o.mval = c
        engmap = {"pe": "tensor", "act": "scalar", "dve": "vector", "pool": "gpsimd", "sp": "sync"}
        nops = {e: len(self.ops[e]) for e in ENGS}
        nwaits = [0]
        with nc.Block() as block:
            def make(e):
                def body(eng):
                    waited = {}
                    for o in self.ops[e]:
                        req = {}
                        for w in o.waits:
                            if w.dma:
                                key, val = w.dsem, w.dval
                            else:
                                if w.eng == e and e == "pe":
                                    continue
                                if w is o:
                                    continue
                                key, val = esem[w.eng], w.mval
                            if req.get(id(key), (None, 0))[1] < val:
                                req[id(key)] = (key, val)
                        for k, (key, val) in req.items():
                            if waited.get(k, 0) < val:
                                eng.wait_ge(key, val)
                                waited[k] = val
                                nwaits[0] += 1
                        inst = o.fn(eng)
                        if o.dma:
                            inst.then_inc(o.dsem, 16)
                        elif o.needed:
                            inst.then_inc(esem[e], 1)
                    if e == "sp":
                        req = {}
                        for w in final_waits:
                            if w.dma:
                                key, val = w.dsem, w.dval
                            else:
                                key, val = esem[w.eng], w.mval
                            if req.get(id(key), (None, 0))[1] < val:
                                req[id(key)] = (key, val)
                        for k, (key, val) in req.items():
                            eng.wait_ge(key, val)
                return body
            for e in ENGS:
                if not self.ops[e] and e != "sp":
                    continue
                getattr(block, engmap[e])(make(e))
        self.stats = dict(nops=nops, nwaits=nwaits[0])
        return self.stats

    def close(self):
        self.stack.close()


def _compress(toks):
    best = {}
    for o in toks:
        key = ("d", id(o.dsem)) if o.dma else ("e", o.eng)
        cur = best.get(key)
        if cur is None:
            best[key] = o
        elif o.dma:
            if o.dval > cur.dval:
                best[key] = o
        elif o.seq > cur.seq:
            best[key] = o
    return list(best.values())


import contextlib
import numpy as np
import concourse.bass as bass
import concourse.mybir as mybir

D = 2048
KC = 16
DFF = 5632
FC = 44
NMOD = 9
INC = 7200
EPS = 1e-6
T = 512


class MK:
    def __init__(self, L, stages, dump=(), lite=False, as_input=()):
        self.lite = lite
        self.as_input = set(as_input)
        self.L = L
        self.NB = L // T
        self.stages = stages
        self.dump = set(dump)
        nc = bass.Bass("TRN2", target_bir_lowering=False)
        self.nc = nc
        self.P = Prog(nc)
        P = self.P
        self.ext_inputs = []
        def ein(n, s, dt=F32):
            self.ext_inputs.append(n)
            if lite and n.startswith("w_"):
                s = [128, 128]
            return P.dram(n, s, dt, kind="ExternalInput")
        self.x = ein("x", [L, D])
        self.cT = ein("cT", [128, KC])
        self.mask = ein("mask", [1, L])
        self.w_mod = ein("w_mod", [D, NMOD * D])
        self.b_modT = ein("b_modT", [128, NMOD * KC])
        self.normsT = ein("normsT", [128, 4 * KC])
        self.w_ffn1_in = ein("w_ffn1_in", [D, 2 * DFF])
        self.w_ffn1_out = ein("w_ffn1_out", [DFF, D])
        self.w_in = ein("w_in", [D, INC])
        self.w_out = ein("w_out", [D, D])
        self.w_ffn2_in = ein("w_ffn2_in", [D, 2 * DFF])
        self.w_ffn2_out = ein("w_ffn2_out", [DFF, D])
        self.ident_in = ein("ident", [128, 128])
        self.W1I = P.dram("W1I", [2 * FC, 128, KC * 128], BF16)
        self.W1O = P.dram("W1O", [KC, 128, FC * 128], BF16)
        self.W2I = P.dram("W2I", [2 * FC, 128, KC * 128], BF16)
        self.W2O = P.dram("W2O", [KC, 128, FC * 128], BF16)
        self.WOr = P.dram("WOr", [KC, 128, KC * 128], BF16)
        self.X1T = P.dram("X1T", [D, L], F32, kind="ExternalOutput" if "X1T" in self.dump else None)
        self.H2T = P.dram("H2T", [D, L], BF16, kind="ExternalOutput" if "H2T" in self.dump else None)
        self.outs = []
        self.ident = P.sbuf("identS", [128, 128], F32)
        self.ones = P.sbuf("onesS", [128, 128], F32)
        self.ones16 = P.sbuf("ones16", [128, 128], BF16)
        self.modT = P.sbuf("modT", [128, NMOD * KC], F32)
        self.nrm = P.sbuf("nrm", [128, 4 * KC], F32)
        self.AB = P.sbuf("AB", [128, 9 * KC], F32)
        self.epsc = P.sbuf("epsc", [128, 1], F32)
        self.last_tokens = []

    def barrier(self):
        P = self.P
        toks = []
        for e in ENGS:
            if P.ops[e]:
                for o in reversed(P.ops[e]):
                    if not o.dma:
                        toks.append(o)
                        break
        dl = {}
        for e in ENGS:
            for o in P.ops[e]:
                if o.dma:
                    dl[id(o.dsem)] = o
        toks += list(dl.values())
        for e in ENGS:
            o = P.op(e, lambda eng: eng.nop())
            o.waits = list(toks)
        P.release_phase_sems()
        return toks

    def setup(self):
        P = self.P
        P.dma(lambda e: e.dma_start(out=self.ident[:, :], in_=self.ident_in[:, :]), self.ident, writes=[self.ident])
        P.dma(lambda e: e.dma_start(out=self.nrm[:, :], in_=self.normsT[:, :]), self.nrm, writes=[self.nrm])
        P.op("pool", lambda e: e.memset(self.ones[:, :], 1.0), writes=[self.ones])
        P.op("pool", lambda e: e.memset(self.ones16[:, :], 1.0), writes=[self.ones16])
        P.op("pool", lambda e: e.memset(self.epsc[:, :], EPS), writes=[self.epsc])

    def convert_stationary(self, st, src, K, c0, ncols, dst, f0, tag):
        P = self.P
        kcn = K // 128
        kg = 16 if kcn == 16 else 11
        nkg = kcn // kg
        cb_n = (ncols + 511) // 512
        i = 0
        for cb in range(cb_n):
            cw = min(512, ncols - cb * 512)
            nf = cw // 128
            for g in range(nkg):
                t32 = st["t32"][i % 2]
                t16 = st["t16"][i % 2]
                srcap = src.t[g * kg * 128:(g + 1) * kg * 128, c0 + cb * 512:c0 + cb * 512 + cw].rearrange("(k p) n -> p k n", p=128)
                P.dma(lambda e, t32=t32, srcap=srcap, cw=cw, kg=kg: e.dma_start(out=t32.t[:, 0:kg, 0:cw], in_=srcap), t32, reads=[src], writes=[t32])
                eng = "dve" if i % 2 == 0 else "act"
                def cast(e, t32=t32, t16=t16, cw=cw, nf=nf, kg=kg, eng=eng):
                    o = t16.t[:, 0:nf * kg * 128].rearrange("p (f k n) -> p k f n", f=nf, k=kg)
                    i_ = t32.t[:, 0:kg, 0:cw].rearrange("p k (f n) -> p k f n", f=nf)
                    if eng == "dve":
                        return e.tensor_copy(out=o, in_=i_)
                    return e.activation(out=o, in_=i_, func=AF.Copy)
                P.op(eng, cast, reads=[t32], writes=[t16])
                dstap = dst.t[f0 + cb * 4:f0 + cb * 4 + nf, :, g * kg * 128:(g + 1) * kg * 128].rearrange("f p x -> p f x")
                P.dma(lambda e, t16=t16, dstap=dstap, nf=nf, kg=kg: e.dma_start(out=dstap, in_=t16.t[:, 0:nf * kg * 128].rearrange("p (f x) -> p f x", f=nf)), t16, reads=[t16], writes=[dst], eng="pool")
                i += 1

    def p0(self):
        P = self.P
        with contextlib.ExitStack() as es:
            old = P.stack
            P.stack = es
            st = dict(t32=[P.sbuf("cv32_%d" % i, [128, 16, 512], F32) for i in range(2)],
                      t16=[P.sbuf("cv16_%d" % i, [128, 16 * 512], BF16) for i in range(2)])
            self.convert_stationary(st, self.w_ffn1_in, D, 0, 2 * DFF, self.W1I, 0, "w1i")
            self.convert_stationary(st, self.w_ffn1_out, DFF, 0, D, self.W1O, 0, "w1o")
            if "p7" in self.stages:
                self.convert_stationary(st, self.w_ffn2_in, D, 0, 2 * DFF, self.W2I, 0, "w2i")
                self.convert_stationary(st, self.w_ffn2_out, DFF, 0, D, self.W2O, 0, "w2o")
                self.convert_stationary(st, self.w_out, D, 0, D, self.WOr, 0, "wo")
            self.barrier()
            P.stack = old

    def p1(self):
        P = self.P
        with contextlib.ExitStack() as es:
            old = P.stack
            P.stack = es
            cs = P.sbuf("cS", [128, KC], F32)
            sc = P.sbuf("scS", [128, KC], F32)
            bm = P.sbuf("bmS", [128, NMOD * KC], F32)
            wm = [P.sbuf("wm%d" % i, [128, KC, 512], F32) for i in range(2)]
            ps = P.psum("ps_mod", [128, 512], F32)
            P.dma(lambda e: e.dma_start(out=cs[:, :], in_=self.cT[:, :]), cs, writes=[cs])
            P.dma(lambda e: e.dma_start(out=bm[:, :], in_=self.b_modT[:, :]), bm, writes=[bm])
            P.op("act", lambda e: e.activation(out=sc[:, :], in_=cs[:, :], func=AF.Silu), reads=[cs], writes=[sc])
            ng = NMOD * D // 512
            for g in range(ng):
                w = wm[g % 2]
                srcap = self.w_mod.t[:, g * 512:(g + 1) * 512].rearrange("(k p) n -> p k n", p=128)
                P.dma(lambda e, w=w, srcap=srcap: e.dma_start(out=w.t[:, :, :], in_=srcap), w, writes=[w])
                for jj in range(4):
                    j = 4 * g + jj
                    for kc in range(KC):
                        P.op("pe", lambda e, w=w, jj=jj, kc=kc, j=j: e.matmul(ps.t[:, j:j + 1], lhsT=w.t[:, kc, jj * 128:(jj + 1) * 128], rhs=sc.t[:, kc:kc + 1], start=(kc == 0), stop=(kc == KC - 1)),
                             reads=[w, sc], writes=[ps])
            P.op("dve", lambda e: e.tensor_tensor(out=self.modT[:, :], in0=ps.t[:, 0:NMOD * KC], in1=bm[:, :], op=ALU.add), reads=[ps, bm], writes=[self.modT])
            for i in range(3):
                sh = self.modT.t[:, (3 * i) * KC:(3 * i + 1) * KC]
                scl = self.modT.t[:, (3 * i + 1) * KC:(3 * i + 2) * KC]
                gt = self.modT.t[:, (3 * i + 2) * KC:(3 * i + 3) * KC]
                nr = self.nrm.t[:, i * KC:(i + 1) * KC]
                A = self.AB.t[:, (3 * i) * KC:(3 * i + 1) * KC]
                Bv = self.AB.t[:, (3 * i + 1) * KC:(3 * i + 2) * KC]
                G = self.AB.t[:, (3 * i + 2) * KC:(3 * i + 3) * KC]
                P.op("dve", lambda e, A=A, scl=scl, nr=nr: e.scalar_tensor_tensor(out=A, in0=scl, scalar=1.0, in1=nr, op0=ALU.add, op1=ALU.mult), reads=[self.modT, self.nrm], writes=[self.AB])
                P.op("dve", lambda e, Bv=Bv, sh=sh: e.tensor_copy(out=Bv, in_=sh), reads=[self.modT], writes=[self.AB])
                gs = 1.0 if i == 1 else 0.5
                P.op("dve", lambda e, G=G, gt=gt, gs=gs: e.tensor_scalar(out=G, in0=gt, scalar1=gs, scalar2=None, op0=ALU.mult), reads=[self.modT], writes=[self.AB])
            self.barrier()
            P.stack = old

    def alloc_ffn(self):
        P = self.P
        s = {}
        s["xtok"] = [P.sbuf("xtok%d" % i, [128, D], F32) for i in range(2)]
        s["xT"] = P.sbuf("xT", [128, KC, T], F32)
        s["hT"] = P.sbuf("hT", [128, KC, T], BF16)
        s["aT"] = P.sbuf("aT", [128, FC, T], BF16)
        s["sq"] = [P.sbuf("sq%d" % i, [128, T], BF16) for i in range(2)]
        s["tmp"] = [P.sbuf("tmp%d" % i, [128, T], F32) for i in range(2)]
        s["rstd"] = P.sbuf("rstd", [128, T], F32)
        s["sg"] = [P.sbuf("sg%d" % i, [128, T], F32) for i in range(2)]
        s["wi"] = [P.sbuf("wi%d" % i, [128, KC * 128], BF16) for i in range(8)]
        s["wo"] = [P.sbuf("wo%d" % i, [128, FC * 128], BF16) for i in range(2)]
        s["mk"] = P.sbuf("mk", [128, T], F32)
        s["pt"] = [P.psum("pt%d" % i, [128, T], F32) for i in range(2)]
        s["pg"] = [P.psum("pg%d" % i, [128, T], F32) for i in range(2)]
        s["pu"] = [P.psum("pu%d" % i, [128, T], F32) for i in range(2)]
        s["pss"] = P.psum("pss", [128, T], F32)
        s["wi_n"] = 0
        s["wo_n"] = 0
        s["pn"] = 0
        return s

    def load_xT_from_tokens(self, s, xd, b):
        P = self.P
        xT = s["xT"]
        for j in range(T // 128):
            xt = s["xtok"][j % 2]
            r0 = b * T + j * 128
            P.dma(lambda e, xt=xt, r0=r0: e.dma_start(out=xt.t[:, :], in_=xd.t[r0:r0 + 128, :]), xt, reads=[xd], writes=[xt])
            for cg in range(KC // 4):
                pt = s["pt"][s["pn"] % 2]
                s["pn"] += 1
                for ci in range(4):
                    c = cg * 4 + ci
                    P.op("pe", lambda e, pt=pt, xt=xt, c=c, ci=ci: e.transpose(out=pt.t[:, ci * 128:(ci + 1) * 128], in_=xt.t[:, c * 128:(c + 1) * 128], identity=self.ident.t[:, :]),
                         reads=[xt, self.ident], writes=[pt])
                eng = "dve" if (cg % 2 == 0) else "act"
                def ev(e, pt=pt, cg=cg, j=j, eng=eng):
                    o = xT.t[:, cg * 4:(cg + 1) * 4, j * 128:(j + 1) * 128]
                    i_ = pt.t[:, :].rearrange("p (c n) -> p c n", c=4)
                    if eng == "dve":
                        return e.tensor_copy(out=o, in_=i_)
                    return e.activation(out=o, in_=i_, func=AF.Copy)
                P.op(eng, ev, reads=[pt], writes=[xT])

    def rms_stats(self, s):
        P = self.P
        xT = s["xT"]
        pss = s["pss"]
        for c in range(KC):
            sq = s["sq"][c % 2]
            P.op("act", lambda e, sq=sq, c=c: e.activation(out=sq.t[:, :], in_=xT.t[:, c, :], func=AF.Square), reads=[xT], writes=[sq])
            P.op("pe", lambda e, sq=sq, c=c: e.matmul(pss.t[:, :], lhsT=self.ones16.t[:, :], rhs=sq.t[:, :], start=(c == 0), stop=(c == KC - 1)), reads=[sq, self.ones16], writes=[pss])
        rstd = s["rstd"]
        P.op("act", lambda e: e.activation(out=rstd.t[:, :], in_=pss.t[:, :], func=AF.Sqrt, bias=self.epsc.t[:, 0:1], scale=1.0 / D), reads=[pss, self.epsc], writes=[rstd])
        P.op("dve", lambda e: e.reciprocal(out=rstd.t[:, :], in_=rstd.t[:, :]), reads=[rstd], writes=[rstd])

    def norm_affine(self, s, i, mask_b=None):
        P = self.P
        xT, hT, rstd = s["xT"], s["hT"], s["rstd"]
        for c in range(KC):
            tmp = s["tmp"][c % 2]
            Ac = self.AB.t[:, 3 * i * KC + c:3 * i * KC + c + 1]
            Bc = self.AB.t[:, (3 * i + 1) * KC + c:(3 * i + 1) * KC + c + 1]
            P.op("dve", lambda e, tmp=tmp, c=c, Ac=Ac: e.scalar_tensor_tensor(out=tmp.t[:, :], in0=xT.t[:, c, :], scalar=Ac, in1=rstd.t[:, :], op0=ALU.mult, op1=ALU.mult),
                 reads=[xT, rstd, self.AB], writes=[tmp])
            if mask_b is None:
                P.op("act", lambda e, tmp=tmp, c=c, Bc=Bc: e.activation(out=hT.t[:, c, :], in_=tmp.t[:, :], func=AF.Identity, bias=Bc, scale=1.0), reads=[tmp, self.AB], writes=[hT])
            else:
                P.op("dve", lambda e, tmp=tmp, c=c, Bc=Bc: e.scalar_tensor_tensor(out=hT.t[:, c, :], in0=tmp.t[:, :], scalar=Bc, in1=mask_b.t[:, :], op0=ALU.add, op1=ALU.mult),
                     reads=[tmp, self.AB, mask_b], writes=[hT])

    def ffn(self, s, WI, WO, gi):
        P = self.P
        xT, hT, aT = s["xT"], s["hT"], s["aT"]
        for f in range(FC):
            wg = s["wi"][s["wi_n"] % 8]
            wu = s["wi"][(s["wi_n"] + 1) % 8]
            s["wi_n"] += 2
            P.dma(lambda e, wg=wg, f=f: e.dma_start(out=wg.t[:, :], in_=WI.t[f, :, :]), wg, reads=[WI], writes=[wg])
            P.dma(lambda e, wu=wu, f=f: e.dma_start(out=wu.t[:, :], in_=WI.t[FC + f, :, :]), wu, reads=[WI], writes=[wu])
            pg = s["pg"][f % 2]
            pu = s["pu"][f % 2]
            for kc in range(KC):
                P.op("pe", lambda e, pg=pg, wg=wg, kc=kc: e.matmul(pg.t[:, :], lhsT=wg.t[:, kc * 128:(kc + 1) * 128], rhs=hT.t[:, kc, :], start=(kc == 0), stop=(kc == KC - 1)), reads=[wg, hT], writes=[pg])
            for kc in range(KC):
                P.op("pe", lambda e, pu=pu, wu=wu, kc=kc: e.matmul(pu.t[:, :], lhsT=wu.t[:, kc * 128:(kc + 1) * 128], rhs=hT.t[:, kc, :], start=(kc == 0), stop=(kc == KC - 1)), reads=[wu, hT], writes=[pu])
            sg = s["sg"][f % 2]
            P.op("act", lambda e, sg=sg, pg=pg: e.activation(out=sg.t[:, :], in_=pg.t[:, :], func=AF.Silu), reads=[pg], writes=[sg])
            P.op("dve", lambda e, sg=sg, pu=pu, f=f: e.tensor_tensor(out=aT.t[:, f, :], in0=sg.t[:, :], in1=pu.t[:, :], op=ALU.mult), reads=[sg, pu], writes=[aT])
        for dc in range(KC):
            wo = s["wo"][s["wo_n"] % 2]
            s["wo_n"] += 1
            P.dma(lambda e, wo=wo, dc=dc: e.dma_start(out=wo.t[:, :], in_=WO.t[dc, :, :]), wo, reads=[WO], writes=[wo])
            py = s["pt"][s["pn"] % 2]
            s["pn"] += 1
            for f in range(FC):
                P.op("pe", lambda e, py=py, wo=wo, f=f: e.matmul(py.t[:, :], lhsT=wo.t[:, f * 128:(f + 1) * 128], rhs=aT.t[:, f, :], start=(f == 0), stop=(f == FC - 1)), reads=[wo, aT], writes=[py])
            Gc = self.AB.t[:, (3 * gi + 2) * KC + dc:(3 * gi + 2) * KC + dc + 1]
            P.op("dve", lambda e, py=py, dc=dc, Gc=Gc: e.scalar_tensor_tensor(out=xT.t[:, dc, :], in0=py.t[:, :], scalar=Gc, in1=xT.t[:, dc, :], op0=ALU.mult, op1=ALU.add),
                 reads=[py, xT, self.AB], writes=[xT])

    def p2(self):
        P = self.P
        with contextlib.ExitStack() as es:
            old = P.stack
            P.stack = es
            s = self.alloc_ffn()
            for b in range(self.NB):
                self.load_xT_from_tokens(s, self.x, b)
                self.rms_stats(s)
                self.norm_affine(s, 0)
                self.ffn(s, self.W1I, self.W1O, 0)
                xT = s["xT"]
                dst = self.X1T.t[:, b * T:(b + 1) * T].rearrange("(c p) t -> p c t", p=128)
                P.dma(lambda e, dst=dst: e.dma_start(out=dst, in_=xT.t[:, :, :]), xT, reads=[xT], writes=[self.X1T], eng="pool")
                mk = s["mk"]
                P.dma(lambda e, b=b: e.dma_start(out=mk.t[:, :], in_=bcast_rows(self.mask.t, b * T, T)), mk, reads=[self.mask], writes=[mk])
                self.rms_stats(s)
                self.norm_affine(s, 1, mask_b=mk)
                hT = s["hT"]
                dsth = self.H2T.t[:, b * T:(b + 1) * T].rearrange("(c p) t -> p c t", p=128)
                P.dma(lambda e, dsth=dsth: e.dma_start(out=dsth, in_=hT.t[:, :, :]), hT, reads=[hT], writes=[self.H2T], eng="pool")
            self.last_tokens = self.barrier()
            P.stack = old

    def build(self):
        self.setup()
        if "p0" in self.stages:
            self.p0()
        if "p1" in self.stages:
            self.p1()
        if "p2" in self.stages:
            self.p2()
        toks = self.barrier()
        st = self.P.emit(final_waits=toks)
        print("ops", st)
        return self.nc


def bcast_rows(t, c0, n):
    ap = t[0:1, c0:c0 + n]
    return bass.AP(ap.tensor, ap.offset, [[0, 128], [1, n]])


AW = 1024
NH = 8
DH = 128


def t5_bucket_np(rel):
    half = 16
    max_exact = 8
    n = np.abs(rel)
    large = max_exact + (np.log(np.maximum(n, 1) / max_exact) / np.log(1024 / max_exact) * (half - max_exact)).astype(np.int32)
    large = np.minimum(large, half - 1)
    return (np.where(rel > 0, half, 0) + np.where(n < max_exact, n, large)).astype(np.int32)


DILS = (1, 4, 16)


def make_onehot():
    oh = np.zeros((3, 33, 384), np.float32)
    for p, dil in enumerate(DILS):
        for m in range(383):
            rel = m - 191
            if abs(rel) <= 64:
                oh[p, int(t5_bucket_np(np.array(rel * dil))), m] = 1.0
            else:
                oh[p, 32, m] = -1e30
        oh[p, 32, 383] = -1e30
    return oh


class MK2(MK):
    def __init__(self, L, stages, dump=(), lite=False, as_input=()):
        super().__init__(L, stages, dump, lite, as_input)
        P = self.P
        def ein(n, s, dt=F32):
            self.ext_inputs.append(n)
            return P.dram(n, s, dt, kind="ExternalInput")
        dmp = lambda n: ("ExternalOutput" if n in self.dump else None)
        self.rel_bias = ein("rel_bias", [32, NH])
        self.onehot = ein("onehot", [3, 33, 384])
        self.nattn_in = ein("norm_attn_out", [1, AW])
        self.WA = P.dram("WA", [8, 128, KC * 512], BF16)
        self.WD = P.dram("WD", [24, 128, KC * 128], BF16)
        self.QKV = P.dram("QKV", [L, 3 * AW], BF16, kind=dmp("QKV"))
        self.Z = P.dram("Z", [L, AW], F32, kind=dmp("Z"))
        self.DQKVT = P.dram("DQKVT", [3 * AW, L], F32, kind=dmp("DQKVT"))
        self.BAT = P.dram("BAT", [32, L], F32, kind=dmp("BAT"))
        self.AO = P.dram("AO", [3, L, AW], F32, kind=dmp("AO"))
        self.AM = P.dram("AM", [3, L, 16], F32, kind=dmp("AM"))
        self.YT = P.dram("YT", [D, L], BF16, kind=dmp("YT"))
        self.NEGM = P.dram("NEGM", [1, L], F32)
        self.BIASR = P.dram("BIASR", [3, NH, 384], F32)
        self.BIAS2 = P.dram("BIAS2", [3, NH, 128 * 385], F32)
        self.identb = P.sbuf("identb", [128, 128], BF16)
        self.WB = P.sbuf("WB", [128, KC * 32], BF16)

    def setup(self):
        super().setup()
        P = self.P
        P.op("dve", lambda e: e.tensor_copy(out=self.identb[:, :], in_=self.ident[:, :]), reads=[self.ident], writes=[self.identb])

    def convert_moving(self, st, src, c0, ngroups, dst, g0):
        P = self.P
        for g in range(ngroups):
            t32 = st["t32"][g % 2]
            t16 = st["t16"][g % 2]
            srcap = src.t[:, c0 + g * 512:c0 + (g + 1) * 512].rearrange("(k p) n -> p k n", p=128)
            P.dma(lambda e, t32=t32, srcap=srcap: e.dma_start(out=t32.t[:, :, :], in_=srcap), t32, reads=[src], writes=[t32])
            eng = "dve" if g % 2 == 0 else "act"
            def cast(e, t32=t32, t16=t16, eng=eng):
                o = t16.t[:, :]
                i_ = t32.t[:, :, :].rearrange("p k n -> p (k n)")
                if eng == "dve":
                    return e.tensor_copy(out=o, in_=i_)
                return e.activation(out=o, in_=i_, func=AF.Copy)
            P.op(eng, cast, reads=[t32], writes=[t16])
            P.dma(lambda e, t16=t16, g=g: e.dma_start(out=dst.t[g0 + g, :, :], in_=t16.t[:, :]), t16, reads=[t16], writes=[dst], eng="pool")

    def p0(self):
        P = self.P
        super().p0()
        if "p3" not in self.stages:
            return
        with contextlib.ExitStack() as es:
            old = P.stack
            P.stack = es
            st = dict(t32=[P.sbuf("cw32_%d" % i, [128, 16, 512], F32) for i in range(2)],
                      t16=[P.sbuf("cw16_%d" % i, [128, 16 * 512], BF16) for i in range(2)])
            self.convert_moving(st, self.w_in, 0, 6, self.WA, 0)
            self.convert_moving(st, self.w_in, 6 * AW, 2, self.WA, 6)
            self.convert_stationary(st, self.w_in, D, 3 * AW, 3 * AW, self.WD, 0, "wd")
            t32 = st["t32"][0]
            srcap = self.w_in.t[:, 7 * AW:7 * AW + 32].rearrange("(k p) n -> p k n", p=128)
            P.dma(lambda e: e.dma_start(out=t32.t[:, :, 0:32], in_=srcap), t32, reads=[self.w_in, t32], writes=[t32])
            P.op("dve", lambda e: e.tensor_copy(out=self.WB.t[:, :].rearrange("p (k n) -> p k n", k=KC), in_=t32.t[:, :, 0:32]), reads=[t32], writes=[self.WB])
            self.barrier()
            P.stack = old

    def p3(self):
        P = self.P
        L = self.L
        with contextlib.ExitStack() as es:
            old = P.stack
            P.stack = es
            hT = [P.sbuf("h2b%d" % i, [128, KC, T], BF16) for i in range(2)]
            wa = [P.sbuf("wa%d" % i, [128, KC * 512], BF16) for i in range(2)]
            wd = [P.sbuf("wd%d" % i, [128, KC * 128], BF16) for i in range(3)]
            tok16 = [P.sbuf("tok16_%d" % i, [128, 512], BF16) for i in range(3)]
            tok32 = [P.sbuf("tok32_%d" % i, [128, 512], F32) for i in range(3)]
            ps = [P.psum("p3ps%d" % i, [128, 512], F32) for i in range(4)]
            n_ps = 0
            n16 = 0
            n32 = 0
            nwa = 0
            nwd = 0
            for b in range(self.NB):
                h = hT[b % 2]
                src = self.H2T.t[:, b * T:(b + 1) * T].rearrange("(c p) t -> p c t", p=128)
                P.dma(lambda e, h=h, src=src: e.dma_start(out=h.t[:, :, :], in_=src), h, reads=[self.H2T], writes=[h])
                for g in range(8):
                    w = wa[nwa % 2]
                    nwa += 1
                    P.dma(lambda e, w=w, g=g: e.dma_start(out=w.t[:, :], in_=self.WA.t[g, :, :]), w, reads=[self.WA], writes=[w])
                    for j in range(4):
                        pp = ps[n_ps % 4]
                        n_ps += 1
                        for kc in range(KC):
                            P.op("pe", lambda e, pp=pp, h=h, w=w, kc=kc, j=j: e.matmul(pp.t[:, :], lhsT=h.t[:, kc, j * 128:(j + 1) * 128], rhs=w.t[:, kc * 512:(kc + 1) * 512], start=(kc == 0), stop=(kc == KC - 1)),
                                 reads=[h, w], writes=[pp])
                        r0 = b * T + j * 128
                        if g < 6:
                            o = tok16[n16 % 3]
                            n16 += 1
                            eng = "act" if n16 % 2 else "dve"
                            if eng == "act":
                                P.op("act", lambda e, o=o, pp=pp: e.activation(out=o.t[:, :], in_=pp.t[:, :], func=AF.Copy), reads=[pp], writes=[o])
                            else:
                                P.op("dve", lambda e, o=o, pp=pp: e.tensor_copy(out=o.t[:, :], in_=pp.t[:, :]), reads=[pp], writes=[o])
                            P.dma(lambda e, o=o, r0=r0, g=g: e.dma_start(out=self.QKV.t[r0:r0 + 128, g * 512:(g + 1) * 512], in_=o.t[:, :]), o, reads=[o], writes=[self.QKV], eng="pool")
                        else:
                            o = tok32[n32 % 3]
                            n32 += 1
                            P.op("act", lambda e, o=o, pp=pp: e.activation(out=o.t[:, :], in_=pp.t[:, :], func=AF.Copy), reads=[pp], writes=[o])
                            P.dma(lambda e, o=o, r0=r0, g=g: e.dma_start(out=self.Z.t[r0:r0 + 128, (g - 6) * 512:(g - 5) * 512], in_=o.t[:, :]), o, reads=[o], writes=[self.Z], eng="pool")
                for f in range(24):
                    w = wd[nwd % 3]
                    nwd += 1
                    P.dma(lambda e, w=w, f=f: e.dma_start(out=w.t[:, :], in_=self.WD.t[f, :, :]), w, reads=[self.WD], writes=[w])
                    pp = ps[n_ps % 4]
                    n_ps += 1
                    for kc in range(KC):
                        P.op("pe", lambda e, pp=pp, h=h, w=w, kc=kc: e.matmul(pp.t[:, :], lhsT=w.t[:, kc * 128:(kc + 1) * 128], rhs=h.t[:, kc, :], start=(kc == 0), stop=(kc == KC - 1)),
                             reads=[h, w], writes=[pp])
                    o = tok32[n32 % 3]
                    n32 += 1
                    P.op("dve", lambda e, o=o, pp=pp: e.tensor_copy(out=o.t[:, :], in_=pp.t[:, :]), reads=[pp], writes=[o])
                    P.dma(lambda e, o=o, f=f, b=b: e.dma_start(out=self.DQKVT.t[f * 128:(f + 1) * 128, b * T:(b + 1) * T], in_=o.t[:, :]), o, reads=[o], writes=[self.DQKVT], eng="pool")
                pp = ps[n_ps % 4]
                n_ps += 1
                for kc in range(KC):
                    P.op("pe", lambda e, pp=pp, h=h, kc=kc: e.matmul(pp.t[0:32, :], lhsT=self.WB.t[:, kc * 32:(kc + 1) * 32], rhs=h.t[:, kc, :], start=(kc == 0), stop=(kc == KC - 1)),
                         reads=[h, self.WB], writes=[pp])
                o = tok32[n32 % 3]
                n32 += 1
                P.op("dve", lambda e, o=o, pp=pp: e.tensor_copy(out=o.t[0:32, :], in_=pp.t[0:32, :]), reads=[pp], writes=[o])
                P.dma(lambda e, o=o, b=b: e.dma_start(out=self.BAT.t[:, b * T:(b + 1) * T], in_=o.t[0:32, :]), o, reads=[o], writes=[self.BAT], eng="pool")
            self.barrier()
            P.stack = old

    def p4(self):
        P = self.P
        L = self.L
        SCALE = DH ** -0.5
        with contextlib.ExitStack() as es:
            old = P.stack
            P.stack = es
            rba = P.sbuf("rba", [33, NH], F32)
            ohs = P.sbuf("ohs", [33, 384], F32)
            brow = P.sbuf("brow", [NH, 384], F32)
            biasT = [P.sbuf("biasT%d" % p, [128, NH, 256], F32) for p in range(3)]
            psb = P.psum("psb", [128, 512], F32)
            P.op("pool", lambda e: e.memset(rba.t[:, :], 1.0), writes=[rba])
            P.dma(lambda e: e.dma_start(out=rba.t[0:32, :], in_=self.rel_bias.t[:, :]), rba, reads=[rba], writes=[rba])
            for p in range(3):
                P.dma(lambda e, p=p: e.dma_start(out=ohs.t[:, :], in_=self.onehot.t[p, :, :]), ohs, writes=[ohs])
                P.op("pe", lambda e: e.matmul(psb.t[0:NH, 0:384], lhsT=rba.t[:, :], rhs=ohs.t[:, :], start=True, stop=True), reads=[rba, ohs], writes=[psb])
                P.op("dve", lambda e: e.tensor_copy(out=brow.t[:, :], in_=psb.t[0:NH, 0:384]), reads=[psb], writes=[brow])
                P.dma(lambda e, p=p: e.dma_start(out=self.BIASR.t[p, :, :], in_=brow.t[:, :]), brow, reads=[brow], writes=[self.BIASR], eng="pool")
                for h in range(NH):
                    base = self.BIASR.t[p, h:h + 1, 0:1]
                    src0 = bass.AP(base.tensor, base.offset, [[0, 128], [1, 384]])
                    b2 = self.BIAS2.t[p, h:h + 1, 0:1]
                    dst0 = bass.AP(b2.tensor, b2.offset, [[385, 128], [1, 384]])
                    P.dma(lambda e, src0=src0, dst0=dst0: e.dma_start(out=dst0, in_=src0), self.BIAS2, reads=[self.BIASR], writes=[self.BIAS2])
                    src = bass.AP(b2.tensor, b2.offset + 127, [[384, 128], [1, 256]])
                    P.dma(lambda e, p=p, h=h, src=src: e.dma_start(out=biasT[p].t[:, h, :], in_=src), biasT[p], reads=[self.BIAS2], writes=[biasT[p]])
            MRW = min(L, 2048)
            mrow = P.sbuf("mrow", [1, MRW], F32)
            for mi in range(L // MRW):
                P.dma(lambda e, mi=mi: e.dma_start(out=mrow.t[:, :], in_=self.mask.t[:, mi * MRW:(mi + 1) * MRW]), mrow, reads=[self.mask], writes=[mrow])
                P.op("dve", lambda e: e.tensor_scalar(out=mrow.t[:, :], in0=mrow.t[:, :], scalar1=-1.0, scalar2=1e30, op0=ALU.add, op1=ALU.mult), reads=[mrow], writes=[mrow])
                P.dma(lambda e, mi=mi: e.dma_start(out=self.NEGM.t[:, mi * MRW:(mi + 1) * MRW], in_=mrow.t[:, :]), mrow, reads=[mrow], writes=[self.NEGM], eng="pool")
            ones1 = P.sbuf("ones1", [1, 128], F32)
            P.op("pool", lambda e: e.memset(ones1.t[:, :], 1.0), writes=[ones1])
            NBUF = 3
            qt = [P.sbuf("qt%d" % i, [128, AW], BF16) for i in range(NBUF)]
            kt = [[P.sbuf("kt%d_%d" % (i, a), [128, AW], BF16) for a in range(2)] for i in range(NBUF)]
            vt = [[P.sbuf("vt%d_%d" % (i, a), [128, AW], BF16) for a in range(2)] for i in range(NBUF)]
            kb = [P.sbuf("kb%d" % i, [1, 256], F32) for i in range(NBUF)]
            qT_l = [P.sbuf("qTs%d" % i, [128, 4, 128], BF16) for i in range(2)]
            kT_l = [P.sbuf("kTs%d" % i, [128, 4, 256], BF16) for i in range(2)]
            ssb_l = [P.sbuf("ssb%d" % i, [128, 4, 256], F32) for i in range(2)]
            psb16_l = [P.sbuf("p16_%d" % i, [128, 4, 256], BF16) for i in range(2)]
            pT_l = [P.sbuf("pTs%d" % i, [128, 4, 2, 128], BF16) for i in range(2)]
            nmx_l = [P.sbuf("nmx%d" % i, [128, 4], F32) for i in range(2)]
            osb = [P.sbuf("osb%d" % i, [128, AW], F32) for i in range(2)]
            stt = [P.sbuf("stt%d" % i, [128, 16], F32) for i in range(2)]
            p_q = P.psum("p_q", [128, 4 * 128], BF16)
            p_k = P.psum("p_k", [128, 4 * 256], BF16)
            p_s = P.psum("p_s", [128, 4 * 256], F32)
            p_p = P.psum("p_p", [128, 8 * 128], BF16)
            p_o = P.psum("p_o", [128, 4 * 128], F32)
            it = 0
            for p, dil in enumerate(DILS):
                sub = L // dil
                ntile = sub // 128
                for r in range(dil):
                    for m in range(ntile):
                        bi = it % NBUF
                        it += 1
                        Q, KA, KB_, VA, VB, kbr = qt[bi], kt[bi][0], kt[bi][1], vt[bi][0], vt[bi][1], kb[bi]
                        os_, st_ = osb[it % 2], stt[it % 2]
                        def rows(j0, n):
                            return slice(r + dil * j0, r + dil * (j0 + n - 1) + 1, dil)
                        P.dma(lambda e, Q=Q, sl=rows(128 * m, 128): e.dma_start(out=Q.t[:, :], in_=self.QKV.t[sl, 0:AW]), Q, reads=[self.QKV], writes=[Q])
                        first = (m == 0)
                        last = (m == ntile - 1)
                        if first or last:
                            P.op("pool", lambda e, kbr=kbr: e.memset(kbr.t[:, :], -1e30), reads=[kbr], writes=[kbr])
                        if first:
                            P.op("pool", lambda e, KA=KA: e.memset(KA.t[0:64, :], 0.0), reads=[KA], writes=[KA])
                            P.op("pool", lambda e, VA=VA: e.memset(VA.t[0:64, :], 0.0), reads=[VA], writes=[VA])
                            P.dma(lambda e, KA=KA, sl=rows(0, 64): e.dma_start(out=KA.t[64:128, :], in_=self.QKV.t[sl, AW:2 * AW]), KA, reads=[self.QKV, KA], writes=[KA])
                            P.dma(lambda e, VA=VA, sl=rows(0, 64): e.dma_start(out=VA.t[64:128, :], in_=self.QKV.t[sl, 2 * AW:3 * AW]), VA, reads=[self.QKV, VA], writes=[VA])
                        else:
                            P.dma(lambda e, KA=KA, sl=rows(128 * m - 64, 128): e.dma_start(out=KA.t[:, :], in_=self.QKV.t[sl, AW:2 * AW]), KA, reads=[self.QKV], writes=[KA])
                            P.dma(lambda e, VA=VA, sl=rows(128 * m - 64, 128): e.dma_start(out=VA.t[:, :], in_=self.QKV.t[sl, 2 * AW:3 * AW]), VA, reads=[self.QKV], writes=[VA])
                        if last:
                            P.op("pool", lambda e, KB_=KB_: e.memset(KB_.t[64:128, :], 0.0), reads=[KB_], writes=[KB_])
                            P.op("pool", lambda e, VB=VB: e.memset(VB.t[64:128, :], 0.0), reads=[VB], writes=[VB])
                            P.dma(lambda e, KB_=KB_, sl=rows(128 * m + 64, 64): e.dma_start(out=KB_.t[0:64, :], in_=self.QKV.t[sl, AW:2 * AW]), KB_, reads=[self.QKV, KB_], writes=[KB_])
                            P.dma(lambda e, VB=VB, sl=rows(128 * m + 64, 64): e.dma_start(out=VB.t[0:64, :], in_=self.QKV.t[sl, 2 * AW:3 * AW]), VB, reads=[self.QKV, VB], writes=[VB])
                        else:
                            P.dma(lambda e, KB_=KB_, sl=rows(128 * m + 64, 128): e.dma_start(out=KB_.t[:, :], in_=self.QKV.t[sl, AW:2 * AW]), KB_, reads=[self.QKV], writes=[KB_])
                            P.dma(lambda e, VB=VB, sl=rows(128 * m + 64, 128): e.dma_start(out=VB.t[:, :], in_=self.QKV.t[sl, 2 * AW:3 * AW]), VB, reads=[self.QKV], writes=[VB])
                        j_lo = max(128 * m - 64, 0)
                        j_hi = min(128 * m + 192, sub)
                        k_lo = j_lo - (128 * m - 64)
                        nk = j_hi - j_lo
                        base = self.NEGM.t[0:1, 0:1]
                        nsrc = bass.AP(base.tensor, base.offset + r + dil * j_lo, [[0, 1], [dil, nk]])
                        P.dma(lambda e, kbr=kbr, nsrc=nsrc, k_lo=k_lo, nk=nk: e.dma_start(out=kbr.t[0:1, k_lo:k_lo + nk], in_=nsrc, allow_slow_non_contiguous=True), kbr, reads=[self.NEGM, kbr], writes=[kbr])
                        for hg in range(2):
                            qT, kT, ssb, psb16, pT, nmx = qT_l[hg], kT_l[hg], ssb_l[hg], psb16_l[hg], pT_l[hg], nmx_l[hg]
                            for hl in range(4):
                                h = hg * 4 + hl
                                P.op("pe", lambda e, qT=qT, kT=kT, ssb=ssb, psb16=psb16, pT=pT, nmx=nmx, Q=Q, h=h, hl=hl: e.transpose(out=p_q.t[:, hl * 128:(hl + 1) * 128], in_=Q.t[:, h * 128:(h + 1) * 128], identity=self.identb.t[:, :]), reads=[Q, self.identb], writes=[p_q])
                                P.op("pe", lambda e, qT=qT, kT=kT, ssb=ssb, psb16=psb16, pT=pT, nmx=nmx, KA=KA, h=h, hl=hl: e.transpose(out=p_k.t[:, hl * 256:hl * 256 + 128], in_=KA.t[:, h * 128:(h + 1) * 128], identity=self.identb.t[:, :]), reads=[KA, self.identb], writes=[p_k])
                                P.op("pe", lambda e, qT=qT, kT=kT, ssb=ssb, psb16=psb16, pT=pT, nmx=nmx, KB_=KB_, h=h, hl=hl: e.transpose(out=p_k.t[:, hl * 256 + 128:hl * 256 + 256], in_=KB_.t[:, h * 128:(h + 1) * 128], identity=self.identb.t[:, :]), reads=[KB_, self.identb], writes=[p_k])
                            P.op("act", lambda e, qT=qT, kT=kT, ssb=ssb, psb16=psb16, pT=pT, nmx=nmx: e.activation(out=qT.t[:, :, :].rearrange("p a b -> p (a b)"), in_=p_q.t[:, :], func=AF.Copy), reads=[p_q], writes=[qT])
                            P.op("dve", lambda e, qT=qT, kT=kT, ssb=ssb, psb16=psb16, pT=pT, nmx=nmx: e.tensor_copy(out=kT.t[:, :, :].rearrange("p a b -> p (a b)"), in_=p_k.t[:, :]), reads=[p_k], writes=[kT])
                            for hl in range(4):
                                P.op("pe", lambda e, qT=qT, kT=kT, ssb=ssb, psb16=psb16, pT=pT, nmx=nmx, hl=hl: e.matmul(p_s.t[:, hl * 256:(hl + 1) * 256], lhsT=qT.t[:, hl, :], rhs=kT.t[:, hl, :], start=True, stop=False), reads=[qT, kT], writes=[p_s])
                                P.op("pe", lambda e, qT=qT, kT=kT, ssb=ssb, psb16=psb16, pT=pT, nmx=nmx, hl=hl, kbr=kbr: e.matmul(p_s.t[:, hl * 256:(hl + 1) * 256], lhsT=ones1.t[0:1, :], rhs=kbr.t[0:1, :], start=False, stop=True), reads=[ones1, kbr], writes=[p_s])
                            P.op("dve", lambda e, qT=qT, kT=kT, ssb=ssb, psb16=psb16, pT=pT, nmx=nmx, p=p, hg=hg: e.scalar_tensor_tensor(out=ssb.t[:, :, :].rearrange("p a b -> p (a b)"), in0=p_s.t[:, :], scalar=SCALE, in1=biasT[p].t[:, hg * 4:(hg + 1) * 4, :].rearrange("p a b -> p (a b)"), op0=ALU.mult, op1=ALU.add),
                                 reads=[p_s, biasT[p]], writes=[ssb])
                            P.op("dve", lambda e, qT=qT, kT=kT, ssb=ssb, psb16=psb16, pT=pT, nmx=nmx, st_=st_, hg=hg: e.tensor_reduce(out=st_.t[:, hg * 4:(hg + 1) * 4], in_=ssb.t[:, :, :], axis=AX.X, op=ALU.max), reads=[ssb], writes=[st_])
                            P.op("dve", lambda e, qT=qT, kT=kT, ssb=ssb, psb16=psb16, pT=pT, nmx=nmx, st_=st_, hg=hg: e.tensor_scalar(out=nmx.t[:, :], in0=st_.t[:, hg * 4:(hg + 1) * 4], scalar1=-1.0, scalar2=None, op0=ALU.mult), reads=[st_], writes=[nmx])
                            for hl in range(4):
                                h = hg * 4 + hl
                                P.op("act", lambda e, qT=qT, kT=kT, ssb=ssb, psb16=psb16, pT=pT, nmx=nmx, hl=hl, h=h, st_=st_: e.activation(out=psb16.t[:, hl, :], in_=ssb.t[:, hl, :], func=AF.Exp, bias=nmx.t[:, hl:hl + 1], scale=1.0, accum_out=st_.t[:, 8 + h:9 + h]), reads=[ssb, nmx], writes=[psb16, st_])
                            for hl in range(4):
                                for a in range(2):
                                    P.op("pe", lambda e, qT=qT, kT=kT, ssb=ssb, psb16=psb16, pT=pT, nmx=nmx, hl=hl, a=a: e.transpose(out=p_p.t[:, (hl * 2 + a) * 128:(hl * 2 + a + 1) * 128], in_=psb16.t[:, hl, a * 128:(a + 1) * 128], identity=self.identb.t[:, :]), reads=[psb16, self.identb], writes=[p_p])
                            P.op("dve", lambda e, qT=qT, kT=kT, ssb=ssb, psb16=psb16, pT=pT, nmx=nmx: e.tensor_copy(out=pT.t[:, :, :, :].rearrange("p a b c -> p (a b c)"), in_=p_p.t[:, :]), reads=[p_p], writes=[pT])
                            for hl in range(4):
                                h = hg * 4 + hl
                                P.op("pe", lambda e, qT=qT, kT=kT, ssb=ssb, psb16=psb16, pT=pT, nmx=nmx, hl=hl, h=h, VA=VA: e.matmul(p_o.t[:, hl * 128:(hl + 1) * 128], lhsT=pT.t[:, hl, 0, :], rhs=VA.t[:, h * 128:(h + 1) * 128], start=True, stop=False), reads=[pT, VA], writes=[p_o])
                                P.op("pe", lambda e, qT=qT, kT=kT, ssb=ssb, psb16=psb16, pT=pT, nmx=nmx, hl=hl, h=h, VB=VB: e.matmul(p_o.t[:, hl * 128:(hl + 1) * 128], lhsT=pT.t[:, hl, 1, :], rhs=VB.t[:, h * 128:(h + 1) * 128], start=False, stop=True), reads=[pT, VB], writes=[p_o])
                            P.op("act", lambda e, qT=qT, kT=kT, ssb=ssb, psb16=psb16, pT=pT, nmx=nmx, os_=os_, hg=hg: e.activation(out=os_.t[:, hg * 512:(hg + 1) * 512], in_=p_o.t[:, :], func=AF.Copy), reads=[p_o], writes=[os_])
                        sl = rows(128 * m, 128)
                        P.dma(lambda e, os_=os_, sl=sl, p=p: e.dma_start(out=self.AO.t[p, sl, :], in_=os_.t[:, :]), os_, reads=[os_], writes=[self.AO], eng="pool")
                        P.dma(lambda e, st_=st_, sl=sl, p=p: e.dma_start(out=self.AM.t[p, sl, :], in_=st_.t[:, :]), st_, reads=[st_], writes=[self.AM], eng="pool")
            self.barrier()
            P.stack = old

    def build(self):
        self.setup()
        for nm in ["p0", "p1", "p2", "p3", "p4", "p5", "p6", "p6a", "p6b", "p6c", "p7"]:
            if nm in self.stages:
                getattr(self, nm)()
        toks = self.barrier()
        st = self.P.emit(final_waits=toks)
        print("ops", st)
        return self.nc


def bc_last(ap, n):
    dims = [list(d) for d in ap.ap]
    assert dims[-1][1] == 1
    dims[-1] = [0, n]
    return bass.AP(ap.tensor, ap.offset, dims)


def bc_mid(ap, n):
    dims = [list(d) for d in ap.ap]
    dims = [dims[0], [0, n]] + dims[1:]
    return bass.AP(ap.tensor, ap.offset, dims)


class MK3(MK2):
    def __init__(self, L, stages, dump=(), lite=False, as_input=()):
        super().__init__(L, stages, dump, lite, as_input)
        P = self.P
        self.out = P.dram("out", [L, D], F32, kind="ExternalOutput")

    def y_to_YT(self, st, y16, row0, i):
        P = self.P
        pt = st["p_t"]
        yT = st["yT"][(i // 4) % 2]
        for c in range(8):
            P.op("pe", lambda e, c=c: e.transpose(out=pt.t[:, c * 128:(c + 1) * 128], in_=y16.t[:, c * 128:(c + 1) * 128], identity=self.identb.t[:, :]), reads=[y16, self.identb], writes=[pt])
        j = i % 4
        P.op("act", lambda e, j=j, yT=yT: e.activation(out=yT.t[:, :, j * 128:(j + 1) * 128], in_=pt.t[:, :].rearrange("p (c n) -> p c n", c=8), func=AF.Copy), reads=[pt, yT], writes=[yT])
        if j == 3:
            b = i // 4
            dst = self.YT.t[row0:row0 + AW, b * T:(b + 1) * T].rearrange("(c p) t -> p c t", p=128)
            P.dma(lambda e, dst=dst, yT=yT: e.dma_start(out=dst, in_=yT.t[:, :, :]), yT, reads=[yT], writes=[self.YT], eng="pool")

    def p5(self):
        P = self.P
        L = self.L
        with contextlib.ExitStack() as es:
            old = P.stack
            P.stack = es
            st = dict(p_t=P.psum("p5pt", [128, 8 * 128], BF16), yT=[P.sbuf("p5yT%d" % i, [128, 8, T], BF16) for i in range(2)])
            nb = P.sbuf("nattb", [128, AW], F32)
            P.dma(lambda e: e.dma_start(out=nb.t[:, :], in_=bcast_rows(self.nattn_in.t, 0, AW)), nb, writes=[nb])
            ao = [[P.sbuf("ao%d_%d" % (k, p), [128, AW], F32) for p in range(3)] for k in range(2)]
            am = [[P.sbuf("am%d_%d" % (k, p), [128, 16], F32) for p in range(3)] for k in range(2)]
            M = P.sbuf("p5M", [128, 8], F32)
            w = [P.sbuf("p5w%d" % p, [128, 8], F32) for p in range(3)]
            den = P.sbuf("p5den", [128, 8], F32)
            acc = P.sbuf("p5acc", [128, AW], F32)
            tmp = P.sbuf("p5tmp", [128, AW], F32)
            ss = P.sbuf("p5ss", [128, 1], F32)
            y16 = [P.sbuf("p5y%d" % i, [128, AW], BF16) for i in range(2)]
            for i in range(L // 128):
                k = i % 2
                for p in range(3):
                    P.dma(lambda e, k=k, p=p, i=i: e.dma_start(out=ao[k][p].t[:, :], in_=self.AO.t[p, i * 128:(i + 1) * 128, :]), ao[k][p], reads=[self.AO], writes=[ao[k][p]])
                    P.dma(lambda e, k=k, p=p, i=i: e.dma_start(out=am[k][p].t[:, :], in_=self.AM.t[p, i * 128:(i + 1) * 128, :]), am[k][p], reads=[self.AM], writes=[am[k][p]])
                a0, a1, a2 = am[k]
                P.op("dve", lambda e, a0=a0, a1=a1: e.tensor_tensor(out=M.t[:, :], in0=a0.t[:, 0:8], in1=a1.t[:, 0:8], op=ALU.max), reads=[a0, a1], writes=[M])
                P.op("dve", lambda e, a2=a2: e.tensor_tensor(out=M.t[:, :], in0=M.t[:, :], in1=a2.t[:, 0:8], op=ALU.max), reads=[M, a2], writes=[M])
                for p in range(3):
                    ap_ = am[k][p]
                    P.op("dve", lambda e, p=p, ap_=ap_: e.tensor_tensor(out=w[p].t[:, :], in0=ap_.t[:, 0:8], in1=M.t[:, :], op=ALU.subtract), reads=[ap_, M], writes=[w[p]])
                    P.op("act", lambda e, p=p: e.activation(out=w[p].t[:, :], in_=w[p].t[:, :], func=AF.Exp), reads=[w[p]], writes=[w[p]])
                P.op("dve", lambda e, a0=a0: e.tensor_tensor(out=den.t[:, :], in0=w[0].t[:, :], in1=a0.t[:, 8:16], op=ALU.mult), reads=[w[0], a0], writes=[den])
                for p in (1, 2):
                    ap_ = am[k][p]
                    P.op("dve", lambda e, p=p, ap_=ap_: e.tensor_tensor(out=M.t[:, :], in0=w[p].t[:, :], in1=ap_.t[:, 8:16], op=ALU.mult), reads=[w[p], ap_, M], writes=[M])
                    P.op("dve", lambda e: e.tensor_tensor(out=den.t[:, :], in0=den.t[:, :], in1=M.t[:, :], op=ALU.add), reads=[den, M], writes=[den])
                P.op("dve", lambda e: e.reciprocal(out=den.t[:, :], in_=den.t[:, :]), reads=[den], writes=[den])
                for p in range(3):
                    P.op("dve", lambda e, p=p: e.tensor_tensor(out=w[p].t[:, :], in0=w[p].t[:, :], in1=den.t[:, :], op=ALU.mult), reads=[w[p], den], writes=[w[p]])
                v3 = lambda t: t.t[:, :].rearrange("p (h d) -> p h d", h=8)
                wb = lambda p: bc_last(w[p].t[:, :].rearrange("p (h o) -> p h o", o=1), 128)
                P.op("dve", lambda e, k=k: e.tensor_tensor(out=v3(acc), in0=v3(ao[k][0]), in1=wb(0), op=ALU.mult), reads=[ao[k][0], w[0]], writes=[acc])
                for p in (1, 2):
                    P.op("dve", lambda e, k=k, p=p: e.tensor_tensor(out=v3(tmp), in0=v3(ao[k][p]), in1=wb(p), op=ALU.mult), reads=[ao[k][p], w[p], tmp], writes=[tmp])
                    P.op("dve", lambda e: e.tensor_tensor(out=acc.t[:, :], in0=acc.t[:, :], in1=tmp.t[:, :], op=ALU.add), reads=[acc, tmp], writes=[acc])
                P.op("act", lambda e: e.activation(out=tmp.t[:, :], in_=acc.t[:, :], func=AF.Square, accum_out=ss.t[:, 0:1]), reads=[acc, tmp], writes=[tmp, ss])
                P.op("act", lambda e: e.activation(out=ss.t[:, :], in_=ss.t[:, :], func=AF.Sqrt, bias=self.epsc.t[:, 0:1], scale=1.0 / AW), reads=[ss, self.epsc], writes=[ss])
                P.op("dve", lambda e: e.reciprocal(out=ss.t[:, :], in_=ss.t[:, :]), reads=[ss], writes=[ss])
                y = y16[i % 2]
                P.op("dve", lambda e, y=y: e.scalar_tensor_tensor(out=y.t[:, :], in0=acc.t[:, :], scalar=ss.t[:, 0:1], in1=nb.t[:, :], op0=ALU.mult, op1=ALU.mult), reads=[acc, ss, nb], writes=[y])
                self.y_to_YT(st, y, 0, i)
            self.barrier()
            P.stack = old

    def p7(self):
        P = self.P
        with contextlib.ExitStack() as es:
            old = P.stack
            P.stack = es
            s = self.alloc_ffn()
            xT, hT = s["xT"], s["hT"]
            for b in range(self.NB):
                src = self.X1T.t[:, b * T:(b + 1) * T].rearrange("(c p) t -> p c t", p=128)
                P.dma(lambda e, src=src: e.dma_start(out=xT.t[:, :, :], in_=src), xT, reads=[self.X1T], writes=[xT])
                srcy = self.YT.t[:, b * T:(b + 1) * T].rearrange("(c p) t -> p c t", p=128)
                P.dma(lambda e, srcy=srcy: e.dma_start(out=hT.t[:, :, :], in_=srcy), hT, reads=[self.YT], writes=[hT])
                for dc in range(KC):
                    w = s["wi"][s["wi_n"] % 8]
                    s["wi_n"] += 1
                    P.dma(lambda e, w=w, dc=dc: e.dma_start(out=w.t[:, :], in_=self.WOr.t[dc, :, :]), w, reads=[self.WOr], writes=[w])
                    py = s["pt"][s["pn"] % 2]
                    s["pn"] += 1
                    for kc in range(KC):
                        P.op("pe", lambda e, py=py, w=w, kc=kc: e.matmul(py.t[:, :], lhsT=w.t[:, kc * 128:(kc + 1) * 128], rhs=hT.t[:, kc, :], start=(kc == 0), stop=(kc == KC - 1)), reads=[w, hT], writes=[py])
                    Gc = self.AB.t[:, 5 * KC + dc:5 * KC + dc + 1]
                    P.op("dve", lambda e, py=py, dc=dc, Gc=Gc: e.scalar_tensor_tensor(out=xT.t[:, dc, :], in0=py.t[:, :], scalar=Gc, in1=xT.t[:, dc, :], op0=ALU.mult, op1=ALU.add),
                         reads=[py, xT, self.AB], writes=[xT])
                if "X2T" in self.dump:
                    dst = self.X2T.t[:, b * T:(b + 1) * T].rearrange("(c p) t -> p c t", p=128)
                    P.dma(lambda e, dst=dst: e.dma_start(out=dst, in_=xT.t[:, :, :]), xT, reads=[xT], writes=[self.X2T], eng="pool")
                self.rms_stats(s)
                self.norm_affine(s, 2)
                self.ffn(s, self.W2I, self.W2O, 2)
                self.rms_stats(s)
                rstd = s["rstd"]
                for c in range(KC):
                    nf = self.nrm.t[:, 3 * KC + c:3 * KC + c + 1]
                    P.op("dve", lambda e, c=c, nf=nf: e.scalar_tensor_tensor(out=xT.t[:, c, :], in0=xT.t[:, c, :], scalar=nf, in1=rstd.t[:, :], op0=ALU.mult, op1=ALU.mult), reads=[xT, rstd, self.nrm], writes=[xT])
                for j in range(T // 128):
                    xt = s["xtok"][j % 2]
                    for cg in range(KC // 4):
                        pt = s["pt"][s["pn"] % 2]
                        s["pn"] += 1
                        for ci in range(4):
                            c = cg * 4 + ci
                            P.op("pe", lambda e, pt=pt, c=c, ci=ci, j=j: e.transpose(out=pt.t[:, ci * 128:(ci + 1) * 128], in_=xT.t[:, c, j * 128:(j + 1) * 128], identity=self.ident.t[:, :]), reads=[xT, self.ident], writes=[pt])
                        if cg % 2 == 0:
                            P.op("dve", lambda e, pt=pt, xt=xt, cg=cg: e.tensor_copy(out=xt.t[:, cg * 512:(cg + 1) * 512], in_=pt.t[:, :]), reads=[pt, xt], writes=[xt])
                        else:
                            P.op("act", lambda e, pt=pt, xt=xt, cg=cg: e.activation(out=xt.t[:, cg * 512:(cg + 1) * 512], in_=pt.t[:, :], func=AF.Copy), reads=[pt, xt], writes=[xt])
                    r0 = b * T + j * 128
                    P.dma(lambda e, xt=xt, r0=r0: e.dma_start(out=self.out.t[r0:r0 + 128, :], in_=xt.t[:, :]), xt, reads=[xt], writes=[self.out], eng="pool")
            self.barrier()
            P.stack = old


import os
CH = 128
DN_STOP = int(os.environ.get('DN_STOP', '99'))


def make_dnconst():
    a = np.arange(128)
    low_i = (a[:, None] >= a[None, :]).astype(np.float32)
    up_i = (a[:, None] <= a[None, :]).astype(np.float32)
    low_s = (a[:, None] > a[None, :]).astype(np.float32)
    up_s = (a[:, None] < a[None, :]).astype(np.float32)
    sel = np.zeros((128, 8 * 128), np.float32)
    for h in range(8):
        sel[h, h * 128:(h + 1) * 128] = 1.0
    blk = ((a[:, None] // 32) == (a[None, :] // 32)).astype(np.float32)
    return np.ascontiguousarray(np.concatenate([low_i, up_i, low_s, up_s, sel, blk, 1.0 - blk], axis=1))


class MK4(MK3):
    def __init__(self, L, stages, dump=(), lite=False, as_input=()):
        super().__init__(L, stages, dump, lite, as_input)
        P = self.P
        def ein(n, s, dt=F32):
            self.ext_inputs.append(n)
            return P.dram(n, s, dt, kind="ExternalInput")
        dmp = lambda n: ("ExternalOutput" if n in self.dump else ("ExternalInput" if n in self.as_input else None))
        self.conv_wT = ein("conv_wT", [128, 120])
        self.gate_par = ein("gate_par", [16, 2])
        self.ndn_in = ein("norm_dn_out", [1, 128])
        self.dnconst = ein("dnconst", [128, 14 * 128])
        self.QT = P.dram("QT", [8, 128, L], F32, kind=dmp("QT"))
        self.KT = P.dram("KT", [8, 128, L], F32, kind=dmp("KT"))
        self.QTOK = P.dram("QTOK", [L, 8, 128], F32, kind=dmp("QTOK"))
        self.KTOK = P.dram("KTOK", [L, 8, 128], F32, kind=dmp("KTOK"))
        self.VTOK = P.dram("VTOK", [L, 8, 128], F32, kind=dmp("VTOK"))
        self.GB = P.dram("GB", [L, 32], F32, kind=dmp("GB"))
        self.ODN = P.dram("ODN", [2, L, AW], F32, kind=dmp("ODN"))

    def p6a(self):
        P = self.P
        L, NB = self.L, self.NB
        QS = DH ** -0.5
        with contextlib.ExitStack() as es:
            old = P.stack
            P.stack = es
            cw = P.sbuf("cw", [128, 120], F32)
            P.dma(lambda e: e.dma_start(out=cw.t[:, :], in_=self.conv_wT.t[:, :]), cw, writes=[cw])
            gp = P.sbuf("gp", [16, 2], F32)
            P.dma(lambda e: e.dma_start(out=gp.t[:, :], in_=self.gate_par.t[:, :]), gp, writes=[gp])
            negA = P.sbuf("negA", [16, 1], F32)
            one16 = P.sbuf("one16", [16, 1], F32)
            P.op("pool", lambda e: e.memset(one16.t[:, :], 1.0), writes=[one16])
            P.op("act", lambda e: e.activation(out=negA.t[:, :], in_=gp.t[:, 0:1], func=AF.Exp), reads=[gp], writes=[negA])
            P.op("dve", lambda e: e.tensor_scalar(out=negA.t[:, :], in0=negA.t[:, :], scalar1=-1.0, scalar2=None, op0=ALU.mult), reads=[negA], writes=[negA])
            xin = [P.sbuf("xin%d" % i, [128, T + 4], F32) for i in range(3)]
            acc = [P.sbuf("cacc%d" % i, [128, T], F32) for i in range(2)]
            sv = [P.sbuf("csv%d" % i, [128, T], F32) for i in range(2)]
            sq = [P.sbuf("csq%d" % i, [128, T], F32) for i in range(2)]
            rs = [P.sbuf("crs%d" % i, [128, T], F32) for i in range(2)]
            xn = [P.sbuf("cxn%d" % i, [128, T], F32) for i in range(2)]
            tk = [P.sbuf("ctk%d" % i, [128, 4, 128], F32) for i in range(2)]
            mk = P.sbuf("cmk", [128, T], F32)
            pss = [P.psum("cpss%d" % i, [128, T], F32) for i in range(2)]
            ptt = [P.psum("cptt%d" % i, [128, T], F32) for i in range(2)]
            braw = P.sbuf("braw", [16, T], F32)
            araw = P.sbuf("araw", [16, T], F32)
            gtk = P.sbuf("gtk", [128, 4, 32], F32)
            psg = P.psum("psg", [128, T], F32)
            it = 0
            for b in range(NB):
                P.dma(lambda e, b=b: e.dma_start(out=mk.t[:, :], in_=bcast_rows(self.mask.t, b * T, T)), mk, reads=[self.mask], writes=[mk])
                for f in range(24):
                    kind, h = f // 8, f % 8
                    xi = xin[it % 3]
                    k2 = it % 2
                    it += 1
                    lo = b * T - 2
                    hi = b * T + T + 2
                    c_lo, c_hi = 0, T + 4
                    if b == 0:
                        P.op("pool", lambda e, xi=xi: e.memset(xi.t[:, 0:2], 0.0), reads=[xi], writes=[xi])
                        lo, c_lo = 0, 2
                    if b == NB - 1:
                        P.op("pool", lambda e, xi=xi: e.memset(xi.t[:, T + 2:T + 4], 0.0), reads=[xi], writes=[xi])
                        hi, c_hi = L, T + 2
                    P.dma(lambda e, xi=xi, f=f, lo=lo, hi=hi, c_lo=c_lo, c_hi=c_hi: e.dma_start(out=xi.t[:, c_lo:c_hi], in_=self.DQKVT.t[f * 128:(f + 1) * 128, lo:hi]), xi, reads=[self.DQKVT, xi], writes=[xi])
                    ac = acc[k2]
                    P.op("dve", lambda e, ac=ac, xi=xi, f=f: e.tensor_scalar(out=ac.t[:, :], in0=xi.t[:, 0:T], scalar1=cw.t[:, f * 5:f * 5 + 1], scalar2=None, op0=ALU.mult), reads=[xi, cw], writes=[ac])
                    for j in range(1, 5):
                        P.op("dve", lambda e, ac=ac, xi=xi, f=f, j=j: e.scalar_tensor_tensor(out=ac.t[:, :], in0=xi.t[:, j:j + T], scalar=cw.t[:, f * 5 + j:f * 5 + j + 1], in1=ac.t[:, :], op0=ALU.mult, op1=ALU.add), reads=[xi, cw, ac], writes=[ac])
                    P.op("dve", lambda e, ac=ac: e.tensor_tensor(out=ac.t[:, :], in0=ac.t[:, :], in1=mk.t[:, :], op=ALU.mult), reads=[ac, mk], writes=[ac])
                    s_ = sv[k2]
                    P.op("act", lambda e, ac=ac, s_=s_: e.activation(out=s_.t[:, :], in_=ac.t[:, :], func=AF.Silu), reads=[ac], writes=[s_])
                    if kind < 2:
                        q_ = sq[k2]
                        ps = pss[k2]
                        r_ = rs[k2]
                        x_ = xn[k2]
                        P.op("act", lambda e, q_=q_, s_=s_: e.activation(out=q_.t[:, :], in_=s_.t[:, :], func=AF.Square), reads=[s_], writes=[q_])
                        P.op("pe", lambda e, ps=ps, q_=q_: e.matmul(ps.t[:, :], lhsT=self.ones.t[:, :], rhs=q_.t[:, :], start=True, stop=True), reads=[q_, self.ones], writes=[ps])
                        P.op("act", lambda e, r_=r_, ps=ps: e.activation(out=r_.t[:, :], in_=ps.t[:, :], func=AF.Sqrt, bias=self.epsc.t[:, 0:1], scale=1.0), reads=[ps, self.epsc], writes=[r_])
                        P.op("dve", lambda e, r_=r_: e.reciprocal(out=r_.t[:, :], in_=r_.t[:, :]), reads=[r_], writes=[r_])
                        scl = QS if kind == 0 else 1.0
                        P.op("dve", lambda e, x_=x_, s_=s_, r_=r_, scl=scl: e.scalar_tensor_tensor(out=x_.t[:, :], in0=s_.t[:, :], scalar=scl, in1=r_.t[:, :], op0=ALU.mult, op1=ALU.mult), reads=[s_, r_], writes=[x_])
                        dstT = (self.QT if kind == 0 else self.KT)
                        P.dma(lambda e, x_=x_, dstT=dstT, h=h, b=b: e.dma_start(out=dstT.t[h, :, b * T:(b + 1) * T], in_=x_.t[:, :]), x_, reads=[x_], writes=[dstT], eng="pool")
                        src_fm = x_
                    else:
                        src_fm = s_
                    pt = ptt[k2]
                    for j in range(4):
                        P.op("pe", lambda e, pt=pt, src_fm=src_fm, j=j: e.transpose(out=pt.t[:, j * 128:(j + 1) * 128], in_=src_fm.t[:, j * 128:(j + 1) * 128], identity=self.ident.t[:, :]), reads=[src_fm, self.ident], writes=[pt])
                    t_ = tk[k2]
                    P.op("act", lambda e, t_=t_, pt=pt: e.activation(out=t_.t[:, :, :].rearrange("p a b -> p (a b)"), in_=pt.t[:, :], func=AF.Copy), reads=[pt], writes=[t_])
                    dtok = [self.QTOK, self.KTOK, self.VTOK][kind]
                    dst = dtok.t[b * T:(b + 1) * T, h, :].rearrange("(j p) d -> p j d", p=128)
                    P.dma(lambda e, t_=t_, dst=dst: e.dma_start(out=dst, in_=t_.t[:, :, :]), t_, reads=[t_], writes=[dtok], eng="pool")
                P.dma(lambda e, b=b: e.dma_start(out=braw.t[:, :], in_=self.BAT.t[0:16, b * T:(b + 1) * T]), braw, reads=[self.BAT], writes=[braw])
                P.dma(lambda e, b=b: e.dma_start(out=araw.t[:, :], in_=self.BAT.t[16:32, b * T:(b + 1) * T]), araw, reads=[self.BAT], writes=[araw])
                P.op("act", lambda e: e.activation(out=braw.t[:, :], in_=braw.t[:, :], func=AF.Sigmoid), reads=[braw], writes=[braw])
                P.op("act", lambda e: e.activation(out=araw.t[:, :], in_=araw.t[:, :], func=AF.Exp, bias=gp.t[:, 1:2], scale=1.0), reads=[araw, gp], writes=[araw])
                P.op("act", lambda e: e.activation(out=araw.t[:, :], in_=araw.t[:, :], func=AF.Ln, bias=one16.t[:, 0:1], scale=1.0), reads=[araw, one16], writes=[araw])
                P.op("dve", lambda e: e.tensor_scalar(out=araw.t[:, :], in0=araw.t[:, :], scalar1=negA.t[:, 0:1], scalar2=None, op0=ALU.mult), reads=[araw, negA], writes=[araw])
                for j in range(4):
                    P.op("pe", lambda e, j=j: e.transpose(out=psg.t[:, j * 32:j * 32 + 16], in_=braw.t[:, j * 128:(j + 1) * 128], identity=self.ident.t[0:16, 0:16]), reads=[braw, self.ident], writes=[psg])
                    P.op("pe", lambda e, j=j: e.transpose(out=psg.t[:, j * 32 + 16:j * 32 + 32], in_=araw.t[:, j * 128:(j + 1) * 128], identity=self.ident.t[0:16, 0:16]), reads=[araw, self.ident], writes=[psg])
                P.op("dve", lambda e: e.tensor_copy(out=gtk.t[:, :, :].rearrange("p a b -> p (a b)"), in_=psg.t[:, 0:128]), reads=[psg], writes=[gtk])
                dstg = self.GB.t[b * T:(b + 1) * T, :].rearrange("(j p) c -> p j c", p=128)
                P.dma(lambda e, dstg=dstg: e.dma_start(out=dstg, in_=gtk.t[:, :, :]), gtk, reads=[gtk], writes=[self.GB], eng="pool")
            self.barrier()
            P.stack = old

    def p6b(self):
        P = self.P
        L = self.L
        NCH = L // CH
        with contextlib.ExitStack() as es:
            old = P.stack
            P.stack = es
            big = lambda n, dt=F32: P.sbuf(n, [128, 8, 128], dt)
            v3 = lambda t: t.t[:, :, :]
            f2 = lambda t: t.t[:, :, :].rearrange("p a b -> p (a b)")
            pv3 = lambda t: t.t[:, :].rearrange("p (a b) -> p a b", a=8)
            col8 = lambda ap: bc_last(ap.rearrange("p (h o) -> p h o", o=1), 128)
            dnc = P.sbuf("dnc", [128, 14 * 128], F32)
            P.dma(lambda e: e.dma_start(out=dnc.t[:, :], in_=self.dnconst.t[:, :]), dnc, writes=[dnc])
            LOWI, UPI, LOWS, UPS = [dnc.t[:, i * 128:(i + 1) * 128] for i in range(4)]
            SEL = lambda h: dnc.t[0:8, 512 + h * 128:512 + (h + 1) * 128]
            BLK = dnc.t[:, 1536:1664]
            NBLK = dnc.t[:, 1664:1792]

            def act_evac(dst, src):
                for hf in range(2):
                    P.op("act", lambda e, dst=dst, src=src, hf=hf: e.activation(out=f2(dst)[:, hf * 512:(hf + 1) * 512], in_=src.t[:, hf * 512:(hf + 1) * 512], func=AF.Copy), reads=[src], writes=[dst])

            def mm8(dst, lh, rh):
                for h in range(8):
                    P.op("pe", lambda e, h=h, dst=dst, lh=lh, rh=rh: e.matmul(dst.t[:, h * 128:(h + 1) * 128], lhsT=lh.t[:, h, :], rhs=rh.t[:, h, :], start=True, stop=True), reads=[lh, rh], writes=[dst])

            def dve_evac(dst, src):
                P.op("dve", lambda e, dst=dst, src=src: e.tensor_copy(out=f2(dst), in_=src.t[:, :]), reads=[src], writes=[dst])

            def dve_acc(dst, a_, src):
                P.op("dve", lambda e, dst=dst, a_=a_, src=src: e.tensor_tensor(out=f2(dst), in0=f2(a_), in1=src.t[:, :], op=ALU.add), reads=[a_, src], writes=[dst])

            def tt(dst_ap, in0, in1, op, reads, writes, eng="dve"):
                P.op(eng, lambda e: e.tensor_tensor(out=dst_ap, in0=in0, in1=in1, op=op), reads=reads, writes=writes)

            chains = []
            for c in range(2):
                W = dict(
                    inb=[dict(qT=big("c%d_qT%d" % (c, i)), kT=big("c%d_kT%d" % (c, i)), ktok=big("c%d_ktok%d" % (c, i)), vtok=big("c%d_vtok%d" % (c, i)),
                              gb=P.sbuf("c%d_gb%d" % (c, i), [128, 32], F32)) for i in range(2)],
                    t=[big("c%d_t%d" % (c, i)) for i in range(6)],
                    En=big("c%d_En" % c), Ens=big("c%d_Ens" % c), egrow=big("c%d_egrow" % c),
                    gsm=P.sbuf("c%d_gsm" % c, [128, 16], F32), gcrow=P.sbuf("c%d_gcrow" % c, [8, 128], F32), nbrow=P.sbuf("c%d_nbrow" % c, [8, 128], F32),
                    sm={n: P.sbuf("c%d_%s" % (c, n), [128, 8], F32) for n in ("egc", "rev", "erev", "egl", "nbeta", "bege")},
                    o_sb=[big("c%d_o%d" % (c, i)) for i in range(2)], S=big("c%d_S" % c), it=0,
                    p=[P.psum("c%d_p%d" % (c, i), [128, 1024], F32) for i in range(2)])
                chains.append(W)

            def chunk_gen(W, dirv, n):
                MR_s = LOWS if dirv == 0 else UPS
                MC = UPI if dirv == 0 else LOWI
                MC_s = UPS if dirv == 0 else LOWS
                ib = W["inb"][W["it"] % 2]
                ob = W["o_sb"][W["it"] % 2]
                W["it"] += 1
                qT, kT, ktok, vtok, gb = ib["qT"], ib["kT"], ib["ktok"], ib["vtok"], ib["gb"]
                t0, t1, t2, t3, t4, t5 = W["t"]
                En, Ens, egrow, gsm, gcrow, nbrow, S = W["En"], W["Ens"], W["egrow"], W["gsm"], W["gcrow"], W["nbrow"], W["S"]
                egc, rev, erev, egl, nbeta, bege = [W["sm"][k_] for k_ in ("egc", "rev", "erev", "egl", "nbeta", "bege")]
                p0, p1 = W["p"]
                c0, c1 = n * CH, (n + 1) * CH
                P.dma(lambda e: e.dma_start(out=v3(qT), in_=self.QT.t[:, :, c0:c1].rearrange("h d a -> d h a")), qT, reads=[self.QT], writes=[qT])
                P.dma(lambda e: e.dma_start(out=v3(kT), in_=self.KT.t[:, :, c0:c1].rearrange("h d a -> d h a")), kT, reads=[self.KT], writes=[kT])
                P.dma(lambda e: e.dma_start(out=v3(ktok), in_=self.KTOK.t[c0:c1, :, :]), ktok, reads=[self.KTOK], writes=[ktok])
                P.dma(lambda e: e.dma_start(out=v3(vtok), in_=self.VTOK.t[c0:c1, :, :]), vtok, reads=[self.VTOK], writes=[vtok])
                P.dma(lambda e: e.dma_start(out=gb.t[:, :], in_=self.GB.t[c0:c1, :]), gb, reads=[self.GB], writes=[gb])
                g8 = gb.t[:, 16 + dirv * 8:16 + dirv * 8 + 8]
                b8 = gb.t[:, dirv * 8:dirv * 8 + 8]
                gc = gsm.t[:, 0:8]
                tot = gsm.t[:, 8:16]
                P.op("pe", lambda e: e.matmul(p0.t[:, 0:8], lhsT=MC, rhs=g8, start=True, stop=True), reads=[gb, dnc], writes=[p0])
                P.op("pe", lambda e: e.matmul(p0.t[:, 8:16], lhsT=self.ones.t[:, :], rhs=g8, start=True, stop=True), reads=[gb, self.ones], writes=[p0])
                P.op("pe", lambda e: e.matmul(p1.t[0:8, 0:128], lhsT=g8, rhs=MC, start=True, stop=True), reads=[gb, dnc], writes=[p1])
                P.op("dve", lambda e: e.tensor_copy(out=gsm.t[:, :], in_=p0.t[:, 0:16]), reads=[p0], writes=[gsm])
                P.op("dve", lambda e: e.tensor_copy(out=gcrow.t[:, :], in_=p1.t[0:8, 0:128]), reads=[p1], writes=[gcrow])
                P.op("dve", lambda e: e.tensor_scalar(out=nbeta.t[:, :], in0=b8, scalar1=-1.0, scalar2=None, op0=ALU.mult), reads=[gb], writes=[nbeta])
                yield
                P.op("act", lambda e: e.activation(out=egc.t[:, :], in_=gc, func=AF.Exp), reads=[gsm], writes=[egc])
                tt(rev.t[:, :], tot, gc, ALU.subtract, [gsm], [rev])
                P.op("act", lambda e: e.activation(out=erev.t[:, :], in_=rev.t[:, :], func=AF.Exp), reads=[rev], writes=[erev])
                P.op("act", lambda e: e.activation(out=egl.t[:, :], in_=tot, func=AF.Exp), reads=[gsm], writes=[egl])
                tt(bege.t[:, :], b8, egc.t[:, :], ALU.mult, [gb, egc], [bege])
                P.op("pe", lambda e: e.matmul(p1.t[0:8, 128:256], lhsT=nbeta.t[:, :], rhs=self.ident.t[:, :], start=True, stop=True), reads=[nbeta, self.ident], writes=[p1])
                P.op("dve", lambda e: e.tensor_copy(out=nbrow.t[:, :], in_=p1.t[0:8, 128:256]), reads=[p1], writes=[nbrow])
                for h in range(8):
                    P.op("pe", lambda e, h=h: e.matmul(p0.t[:, h * 128:(h + 1) * 128], lhsT=SEL(h), rhs=gcrow.t[:, :], start=True, stop=True), reads=[dnc, gcrow], writes=[p0])
                yield
                tt(v3(t0), col8(gc), pv3(p0), ALU.subtract, [gsm, p0], [t0])
                for hf in range(2):
                    P.op("act", lambda e, hf=hf: e.activation(out=f2(egrow)[:, hf * 512:(hf + 1) * 512], in_=p0.t[:, hf * 512:(hf + 1) * 512], func=AF.Exp), reads=[p0], writes=[egrow])
                P.op("dve", lambda e: e.tensor_scalar(out=f2(t1), in0=f2(t0), scalar1=0.0, scalar2=None, op0=ALU.min), reads=[t0], writes=[t1])
                P.op("act", lambda e: e.activation(out=f2(t1), in_=f2(t1), func=AF.Exp), reads=[t1], writes=[t1])
                tt(v3(t1), v3(t1), bc_mid(MR_s, 8), ALU.mult, [t1, dnc], [t1], eng="pool")
                P.op("dve", lambda e: e.tensor_scalar(out=f2(En), in0=f2(t0), scalar1=0.0, scalar2=-1.0, op0=ALU.max, op1=ALU.mult), reads=[t0], writes=[En])
                P.op("act", lambda e: e.activation(out=f2(En), in_=f2(En), func=AF.Exp), reads=[En], writes=[En])
                tt(v3(Ens), v3(En), bc_mid(MC_s, 8), ALU.mult, [En, dnc], [Ens], eng="pool")
                tt(v3(En), v3(En), bc_mid(MC, 8), ALU.mult, [En, dnc], [En])
                yield
                for h in range(8):
                    P.op("pe", lambda e, h=h: e.matmul(p1.t[:, h * 128:(h + 1) * 128], lhsT=SEL(h), rhs=nbrow.t[:, :], start=True, stop=True), reads=[dnc, nbrow], writes=[p1])
                for h in range(8):
                    P.op("pe", lambda e, h=h: e.matmul(p0.t[:, h * 128:(h + 1) * 128], lhsT=kT.t[:, h, :], rhs=kT.t[:, h, :], start=True, stop=True), reads=[kT], writes=[p0])
                yield
                tt(f2(t0), p0.t[:, :], f2(t1), ALU.mult, [p0, t1], [t0])
                tt(v3(t0), v3(t0), col8(nbeta.t[:, :]), ALU.mult, [t0, nbeta], [t0])
                tt(f2(Ens), p0.t[:, :], f2(Ens), ALU.mult, [p0, Ens], [Ens])
                tt(f2(Ens), f2(Ens), p1.t[:, :], ALU.mult, [Ens, p1], [Ens])
                tt(v3(t0), v3(t0), bc_mid(BLK, 8), ALU.mult, [t0, dnc], [t0])
                tt(v3(t1), v3(Ens), bc_mid(BLK, 8), ALU.mult, [Ens, dnc], [t1])
                tt(v3(Ens), v3(Ens), bc_mid(NBLK, 8), ALU.mult, [Ens, dnc], [Ens])
                tt(v3(t2), v3(t1), bc_mid(self.ident.t[:, :], 8), ALU.add, [t1, self.ident], [t2])
                tt(v3(t3), v3(t0), bc_mid(self.ident.t[:, :], 8), ALU.add, [t0, self.ident], [t3])
                yield
                Xc, Xn_, Yc, Yn_, Dt, DtT, Uo = t1, t4, t0, t5, t2, t3, Ens
                for lvl in range(4):
                    mm8(p0, Xc, Yc)
                    mm8(p1, Yc, Xc)
                    yield
                    act_evac(Yn_, p0)
                    dve_evac(Xn_, p1)
                    mm8(p0, Yn_, Dt)
                    mm8(p1, Xn_, DtT)
                    yield
                    dve_acc(Dt, Dt, p0)
                    dve_acc(DtT, DtT, p1)
                    Xc, Xn_ = Xn_, Xc
                    Yc, Yn_ = Yn_, Yc
                Mt, MtT, MtT2, P1, T32 = t0, t1, t4, t5, t0
                mm8(p0, DtT, Uo)
                mm8(p1, Uo, DtT)
                yield
                act_evac(Mt, p0)
                dve_evac(MtT, p1)
                mm8(p0, Mt, MtT)
                yield
                act_evac(MtT2, p0)
                mm8(p1, MtT2, Dt)
                yield
                dve_acc(P1, Dt, p1)
                mm8(p0, MtT, P1)
                yield
                dve_acc(T32, P1, p0)
                rhs_v, rhs_w, u_sb, wT_sb, kdec, vnew = t1, t2, t3, t4, t5, Ens
                tt(v3(rhs_v), v3(vtok), col8(b8), ALU.mult, [vtok, gb], [rhs_v], eng="pool")
                tt(v3(rhs_w), v3(ktok), col8(bege.t[:, :]), ALU.mult, [ktok, bege], [rhs_w], eng="pool")
                mm8(p0, T32, rhs_v)
                mm8(p1, rhs_w, T32)
                yield
                act_evac(u_sb, p0)
                dve_evac(wT_sb, p1)
                mm8(p0, kT, qT)
                yield
                tt(f2(En), p0.t[:, :], f2(En), ALU.mult, [p0, En], [En])
                tt(f2(egrow), f2(qT), f2(egrow), ALU.mult, [qT, egrow], [egrow])
                tt(v3(kdec), v3(ktok), col8(erev.t[:, :]), ALU.mult, [ktok, erev], [kdec], eng="pool")
                yield
                mm8(p1, wT_sb, S)
                yield
                tt(f2(vnew), f2(u_sb), p1.t[:, :], ALU.subtract, [u_sb, p1], [vnew])
                for h in range(8):
                    P.op("pe", lambda e, h=h: e.matmul(p0.t[:, h * 128:(h + 1) * 128], lhsT=egrow.t[:, h, :], rhs=S.t[:, h, :], start=True, stop=False), reads=[egrow, S], writes=[p0])
                    P.op("pe", lambda e, h=h: e.matmul(p0.t[:, h * 128:(h + 1) * 128], lhsT=En.t[:, h, :], rhs=vnew.t[:, h, :], start=False, stop=True), reads=[En, vnew], writes=[p0])
                mm8(p1, kdec, vnew)
                yield
                act_evac(ob, p0)
                P.dma(lambda e: e.dma_start(out=self.ODN.t[dirv, c0:c1, :], in_=f2(ob)), ob, reads=[ob], writes=[self.ODN], eng="act")
                tt(v3(S), v3(S), col8(egl.t[:, :]), ALU.mult, [S, egl], [S])
                tt(f2(S), f2(S), p1.t[:, :], ALU.add, [S, p1], [S])
                yield

            for c in range(2):
                S_ = chains[c]["S"]
                P.op("dve", lambda e, S_=S_: e.memset(f2(S_), 0.0), reads=[S_], writes=[S_])
            for i in range(NCH):
                gens = [chunk_gen(chains[0], 0, i), chunk_gen(chains[1], 1, NCH - 1 - i)]
                alive = [True, True]
                while any(alive):
                    for c in range(2):
                        if alive[c]:
                            try:
                                next(gens[c])
                            except StopIteration:
                                alive[c] = False
            self.barrier()
            P.stack = old

    def p6c(self):
        P = self.P
        L = self.L
        with contextlib.ExitStack() as es:
            old = P.stack
            P.stack = es
            st = dict(p_t=P.psum("p6pt", [128, 8 * 128], BF16), yT=[P.sbuf("p6yT%d" % i, [128, 8, T], BF16) for i in range(2)])
            nd = P.sbuf("ndnb", [128, 128], F32)
            P.dma(lambda e: e.dma_start(out=nd.t[:, :], in_=bcast_rows(self.ndn_in.t, 0, 128)), nd, writes=[nd])
            of = [P.sbuf("of%d" % i, [128, AW], F32) for i in range(2)]
            obk = [P.sbuf("obk%d" % i, [128, AW], F32) for i in range(2)]
            zt = [P.sbuf("zt%d" % i, [128, AW], F32) for i in range(2)]
            tmp = P.sbuf("p6tmp", [128, AW], F32)
            ss = P.sbuf("p6ss", [128, 8], F32)
            y16 = [P.sbuf("p6y%d" % i, [128, AW], BF16) for i in range(2)]
            v3 = lambda t: t.t[:, :].rearrange("p (h d) -> p h d", h=8)
            for i in range(L // 128):
                k = i % 2
                a, bb, z = of[k], obk[k], zt[k]
                r0, r1 = i * 128, (i + 1) * 128
                P.dma(lambda e, a=a, r0=r0, r1=r1: e.dma_start(out=a.t[:, :], in_=self.ODN.t[0, r0:r1, :]), a, reads=[self.ODN], writes=[a])
                P.dma(lambda e, bb=bb, r0=r0, r1=r1: e.dma_start(out=bb.t[:, :], in_=self.ODN.t[1, r0:r1, :]), bb, reads=[self.ODN], writes=[bb])
                P.dma(lambda e, z=z, r0=r0, r1=r1: e.dma_start(out=z.t[:, :], in_=self.Z.t[r0:r1, :]), z, reads=[self.Z], writes=[z])
                P.op("dve", lambda e, a=a, bb=bb: e.tensor_tensor(out=a.t[:, :], in0=a.t[:, :], in1=bb.t[:, :], op=ALU.add), reads=[a, bb], writes=[a])
                P.op("dve", lambda e, a=a: e.tensor_tensor(out=tmp.t[:, :], in0=a.t[:, :], in1=a.t[:, :], op=ALU.mult), reads=[a, tmp], writes=[tmp])
                P.op("dve", lambda e: e.tensor_reduce(out=ss.t[:, :], in_=v3(tmp), axis=AX.X, op=ALU.add), reads=[tmp], writes=[ss])
                P.op("act", lambda e: e.activation(out=ss.t[:, :], in_=ss.t[:, :], func=AF.Sqrt, bias=self.epsc.t[:, 0:1], scale=1.0 / DH), reads=[ss, self.epsc], writes=[ss])
                P.op("dve", lambda e: e.reciprocal(out=ss.t[:, :], in_=ss.t[:, :]), reads=[ss], writes=[ss])
                P.op("act", lambda e, z=z: e.activation(out=z.t[:, :], in_=z.t[:, :], func=AF.Silu), reads=[z], writes=[z])
                P.op("dve", lambda e, a=a: e.tensor_tensor(out=v3(a), in0=v3(a), in1=bc_last(ss.t[:, :].rearrange("p (h o) -> p h o", o=1), 128), op=ALU.mult), reads=[a, ss], writes=[a])
                P.op("dve", lambda e, a=a: e.tensor_tensor(out=v3(a), in0=v3(a), in1=bc_mid(nd.t[:, :], 8), op=ALU.mult), reads=[a, nd], writes=[a])
                y = y16[k]
                P.op("dve", lambda e, a=a, z=z, y=y: e.tensor_tensor(out=y.t[:, :], in0=a.t[:, :], in1=z.t[:, :], op=ALU.mult), reads=[a, z], writes=[y])
                self.y_to_YT(st, y, AW, i)
            self.barrier()
            P.stack = old

    def p6(self):
        self.p6a()
        self.p6b()
        self.p6c()


def _fm(v):
    return np.ascontiguousarray(np.asarray(v, np.float32).reshape(-1, 128).T)


def _prep_core(inp, x, c, Lreal, L, shared):
    xp = np.zeros((L, D), np.float32)
    mask = np.zeros((1, L), np.float32)
    if Lreal > 0:
        xp[:Lreal] = x
        mask[0, :Lreal] = 1.0
    m = dict(shared)
    m["x"] = xp
    m["cT"] = _fm(c)
    m["mask"] = mask
    return m


_NC_CACHE = {}


def kernel(x_prompt, x_sample, c_prompt, c_sample, w_mod, b_mod, norm_ffn1, w_ffn1_in, w_ffn1_out, norm_mix, w_in, conv_w,
           a_log, dt_bias, norm_attn_out, norm_dn_out, w_out, norm_ffn2, w_ffn2_in, w_ffn2_out, rel_bias, norm_final):
    f32 = lambda a: np.ascontiguousarray(np.asarray(a, dtype=np.float32))
    x_prompt, x_sample, c_prompt, c_sample = f32(x_prompt), f32(x_sample), f32(c_prompt), f32(c_sample)
    L = x_prompt.shape[1]
    Ls = x_sample.shape[1]
    norms = [f32(norm_ffn1)[0], f32(norm_mix)[0], f32(norm_ffn2)[0], f32(norm_final)]
    shared = dict(
        w_mod=f32(w_mod)[0], b_modT=_fm(f32(b_mod)[0]),
        normsT=np.ascontiguousarray(np.concatenate([_fm(n) for n in norms], axis=1)),
        w_ffn1_in=f32(w_ffn1_in)[0], w_ffn1_out=f32(w_ffn1_out)[0], w_in=f32(w_in)[0], w_out=f32(w_out)[0],
        w_ffn2_in=f32(w_ffn2_in)[0], w_ffn2_out=f32(w_ffn2_out)[0], ident=np.eye(128, dtype=np.float32),
        rel_bias=f32(rel_bias), onehot=make_onehot(), norm_attn_out=f32(norm_attn_out)[0].reshape(1, 1024),
        conv_wT=np.ascontiguousarray(f32(conv_w)[0].T.reshape(24, 128, 5).transpose(1, 0, 2).reshape(128, 120)),
        gate_par=np.ascontiguousarray(np.stack([f32(a_log)[0].reshape(16), f32(dt_bias)[0].reshape(16)], axis=1)),
        norm_dn_out=f32(norm_dn_out)[0].reshape(1, 128), dnconst=make_dnconst(),
    )
    if L not in _NC_CACHE:
        mk = MK4(L, stages=["p0", "p1", "p2", "p3", "p4", "p5", "p6", "p7"])
        _NC_CACHE[L] = (mk.build(), list(mk.ext_inputs))
    nc, names = _NC_CACHE[L]
    zc = np.zeros((D,), np.float32)
    cores = [(x_prompt[0], c_prompt[0], L), (x_sample[0], c_sample[0], Ls), (None, zc, 0), (None, zc, 0),
             (x_prompt[1], c_prompt[1], L), (x_sample[1], c_sample[1], Ls), (None, zc, 0), (None, zc, 0)]
    in_maps = []
    for (x, c, lr) in cores:
        m = _prep_core(None, x, c, lr, L, shared)
        in_maps.append({k: m[k] for k in names})
    res = run_bass_kernel_spmd(nc, in_maps, core_ids=list(range(8)))
    outs = [np.asarray(r["out"], dtype=np.float32) for r in res.results]
    y_prompt = np.stack([outs[0], outs[4]], axis=0)
    y_sample = np.stack([outs[1][:Ls], outs[5][:Ls]], axis=0)
    return (y_prompt, y_sample)
```
